# Optimizing a Trainium2 kernel written in Bass

```python
import math
import jax, jax.numpy as jnp
from jax import lax
import numpy as np

D_MODEL = 1024
BATCH = 32
SEQ = 2048
DEPTH = 4

CHUNK = 64
N_MIXERS = 3
PLE_DIM = 256
D_FF = 4 * D_MODEL
NORM_EPS = 1e-6
SSM_GROUP = 16
SSM_GROUPS = D_MODEL // SSM_GROUP
SSM_STATE = 64
DT_MIN = 1e-3
DT_MAX = 1e-1
CONV_WIDTH = 3
FOX_HEAD_DIM = 64
FOX_HEADS = D_MODEL // FOX_HEAD_DIM
Q_BLOCK = 128

N_SSM_LAYERS = len(range(0, DEPTH, N_MIXERS))
N_CONV_LAYERS = len(range(1, DEPTH, N_MIXERS))
N_FOX_LAYERS = len(range(2, DEPTH, N_MIXERS))

kernel_name = "interleaved_s5_shortconv_fox_trunk"


def rmsnorm(x, gain):
    xf = x.astype(jnp.float32)
    out = xf * lax.rsqrt(jnp.mean(xf * xf, axis=-1, keepdims=True) + NORM_EPS) * gain.astype(jnp.float32)
    return out.astype(x.dtype)


def s5_mixer(u, lam_re, lam_im, log_dt, b_re, b_im, c_re, c_im, d_skip, w_glu):
    f32 = jnp.float32
    bsz, seqlen, _ = u.shape
    uf = u.astype(f32)
    ug = uf.reshape(bsz, seqlen, SSM_GROUPS, SSM_GROUP)
    lr = lam_re.astype(f32)
    li = lam_im.astype(f32)
    dt = jnp.exp(log_dt.astype(f32))[:, None]
    mag = jnp.exp(lr * dt)
    ab_re = mag * jnp.cos(li * dt)
    ab_im = mag * jnp.sin(li * dt)
    nr = ab_re - 1.0
    ni = ab_im
    den = lr * lr + li * li
    coef_re = (nr * lr + ni * li) / den
    coef_im = (ni * lr - nr * li) / den
    br = b_re.astype(f32)
    bi = b_im.astype(f32)
    bb_re = coef_re[..., None] * br - coef_im[..., None] * bi
    bb_im = coef_re[..., None] * bi + coef_im[..., None] * br
    bu_re = jnp.einsum('blgh,gph->blgp', ug, bb_re)
    bu_im = jnp.einsum('blgh,gph->blgp', ug, bb_im)
    a_re = jnp.broadcast_to(ab_re, (1, seqlen, SSM_GROUPS, SSM_STATE))
    a_im = jnp.broadcast_to(ab_im, (1, seqlen, SSM_GROUPS, SSM_STATE))

    def combine(left, right):
        a1r, a1i, b1r, b1i = left
        a2r, a2i, b2r, b2i = right
        return (a2r * a1r - a2i * a1i,
                a2r * a1i + a2i * a1r,
                a2r * b1r - a2i * b1i + b2r,
                a2r * b1i + a2i * b1r + b2i)

    _, _, xs_re, xs_im = lax.associative_scan(combine, (a_re, a_im, bu_re, bu_im), axis=1)
    y = (jnp.einsum('blgp,ghp->blgh', xs_re, c_re.astype(f32))
         - jnp.einsum('blgp,ghp->blgh', xs_im, c_im.astype(f32)))
    y = y.reshape(bsz, seqlen, D_MODEL) + d_skip.astype(f32) * uf
    y = jax.nn.gelu(y).astype(u.dtype)
    a, g = jnp.split(y @ w_glu, 2, axis=-1)
    return a * jax.nn.sigmoid(g)


def short_conv_mixer(h, w_in, conv_w, w_out):
    b_gate, c_gate, v = jnp.split(h @ w_in, 3, axis=-1)
    z = c_gate * v
    rhs = conv_w[:, None, :].astype(z.dtype)
    conv = lax.conv_general_dilated(
        z, rhs, window_strides=(1,), padding=[(CONV_WIDTH - 1, 0)],
        dimension_numbers=('NWC', 'WIO', 'NWC'), feature_group_count=D_MODEL)
    return (b_gate * conv) @ w_out


def fox_mixer(h, w_in, b_f, w_out):
    f32 = jnp.float32
    bsz, seqlen, _ = h.shape
    proj = h @ w_in
    q, k, v, f_logit = jnp.split(proj, [D_MODEL, 2 * D_MODEL, 3 * D_MODEL], axis=-1)
    q = q.reshape(bsz, seqlen, FOX_HEADS, FOX_HEAD_DIM)
    k = k.reshape(bsz, seqlen, FOX_HEADS, FOX_HEAD_DIM)
    v = v.reshape(bsz, seqlen, FOX_HEADS, FOX_HEAD_DIM)
    log_f = jax.nn.log_sigmoid(f_logit.astype(f32) + b_f.astype(f32))
    cum = jnp.cumsum(log_f, axis=1).transpose(0, 2, 1)
    scale = FOX_HEAD_DIM ** -0.5
    outs = []
    for blk in range(seqlen // Q_BLOCK):
        q0 = blk * Q_BLOCK
        kv_len = q0 + Q_BLOCK
        s = jnp.einsum('bqhd,bkhd->bhqk', q[:, q0:kv_len], k[:, :kv_len]).astype(f32) * scale
        decay = cum[:, :, q0:kv_len, None] - cum[:, :, None, :kv_len]
        causal = (q0 + jnp.arange(Q_BLOCK))[:, None] >= jnp.arange(kv_len)[None, :]
        s = jnp.where(causal, s + decay, -jnp.inf)
        pr = jax.nn.softmax(s, axis=-1).astype(v.dtype)
        outs.append(jnp.einsum('bhqk,bkhd->bqhd', pr, v[:, :kv_len]))
    o = jnp.concatenate(outs, axis=1).reshape(bsz, seqlen, D_MODEL)
    return o @ w_out


def sqrelu_mlp(h, w1, w2):
    a = jax.nn.relu(h @ w1)
    return (a * a) @ w2


def setup_inputs(seed: int = 0) -> dict:
    key = jax.random.key(seed)
    ks = jax.random.split(key, 26)
    f32 = jnp.float32
    nrm = lambda k, shape, s: jax.random.normal(k, shape, f32) * s
    gain = lambda k, shape: 1.0 + 0.05 * jax.random.normal(k, shape, f32)
    ns, nc, nf = N_SSM_LAYERS, N_CONV_LAYERS, N_FOX_LAYERS
    G, P, H = SSM_GROUPS, SSM_STATE, SSM_GROUP
    lam_im_init = jnp.broadcast_to(jnp.arange(P, dtype=f32) * math.pi, (ns, G, P))
    return {
        "x": nrm(ks[0], (BATCH, SEQ, D_MODEL), 1.0),
        "p": nrm(ks[1], (DEPTH, BATCH, SEQ, PLE_DIM), 1.0),
        "norm_mix": gain(ks[2], (DEPTH, D_MODEL)),
        "norm_ffn": gain(ks[3], (DEPTH, D_MODEL)),
        "norm_ple": gain(ks[4], (DEPTH, D_MODEL)),
        "norm_final": gain(ks[5], (D_MODEL,)),
        "ssm_lam_re": -0.5 + 0.01 * jax.random.normal(ks[6], (ns, G, P), f32),
        "ssm_lam_im": lam_im_init + 0.01 * jax.random.normal(ks[7], (ns, G, P), f32),
        "ssm_log_dt": jax.random.uniform(ks[8], (ns, G), f32, math.log(DT_MIN), math.log(DT_MAX)),
        "ssm_b_re": nrm(ks[9], (ns, G, P, H), (2 * H) ** -0.5),
        "ssm_b_im": nrm(ks[10], (ns, G, P, H), (2 * H) ** -0.5),
        "ssm_c_re": nrm(ks[11], (ns, G, H, P), (2 * P) ** -0.5),
        "ssm_c_im": nrm(ks[12], (ns, G, H, P), (2 * P) ** -0.5),
        "ssm_d": nrm(ks[13], (ns, D_MODEL), 1.0),
        "ssm_w_glu": nrm(ks[14], (ns, D_MODEL, 2 * D_MODEL), D_MODEL ** -0.5),
        "conv_w_in": nrm(ks[15], (nc, D_MODEL, 3 * D_MODEL), D_MODEL ** -0.5),
        "conv_w": nrm(ks[16], (nc, CONV_WIDTH, D_MODEL), CONV_WIDTH ** -0.5),
        "conv_w_out": nrm(ks[17], (nc, D_MODEL, D_MODEL), D_MODEL ** -0.5),
        "fox_w_in": nrm(ks[18], (nf, D_MODEL, 3 * D_MODEL + FOX_HEADS), D_MODEL ** -0.5),
        "fox_b_f": 3.0 + 0.5 * jax.random.normal(ks[19], (nf, FOX_HEADS), f32),
        "fox_w_out": nrm(ks[20], (nf, D_MODEL, D_MODEL), D_MODEL ** -0.5),
        "mlp_w1": nrm(ks[21], (DEPTH, D_MODEL, D_FF), D_MODEL ** -0.5),
        "mlp_w2": nrm(ks[22], (DEPTH, D_FF, D_MODEL), D_FF ** -0.5),
        "ple_w": nrm(ks[23], (DEPTH, PLE_DIM, D_MODEL), PLE_DIM ** -0.5),
        "ple_gate_w": nrm(ks[24], (DEPTH, D_MODEL, D_MODEL), D_MODEL ** -0.5),
    }


def reference(x, p, norm_mix, norm_ffn, norm_ple, norm_final,
              ssm_lam_re, ssm_lam_im, ssm_log_dt, ssm_b_re, ssm_b_im, ssm_c_re, ssm_c_im,
              ssm_d, ssm_w_glu, conv_w_in, conv_w, conv_w_out, fox_w_in, fox_b_f, fox_w_out,
              mlp_w1, mlp_w2, ple_w, ple_gate_w):
    for i in range(DEPTH):
        kind, slot = i % N_MIXERS, i // N_MIXERS
        h = rmsnorm(x, norm_mix[i])
        if kind == 0:
            mix = s5_mixer(h, ssm_lam_re[slot], ssm_lam_im[slot], ssm_log_dt[slot],
                           ssm_b_re[slot], ssm_b_im[slot], ssm_c_re[slot], ssm_c_im[slot],
                           ssm_d[slot], ssm_w_glu[slot])
        elif kind == 1:
            mix = short_conv_mixer(h, conv_w_in[slot], conv_w[slot], conv_w_out[slot])
        else:
            mix = fox_mixer(h, fox_w_in[slot], fox_b_f[slot], fox_w_out[slot])
        x = x + mix
        x = x + sqrelu_mlp(rmsnorm(x, norm_ffn[i]), mlp_w1[i], mlp_w2[i])
        gate = jax.nn.sigmoid(rmsnorm(x, norm_ple[i]) @ ple_gate_w[i])
        x = x + (p[i] @ ple_w[i]) * gate
    return rmsnorm(x, norm_final)
```

```python
import math
import numpy as np
import concourse.bass as bass
import concourse.mybir as mybir
from concourse.bass_utils import run_bass_kernel_spmd

F32 = mybir.dt.float32
BF16 = mybir.dt.bfloat16
AF = mybir.ActivationFunctionType
ALU = mybir.AluOpType

D = 1024
L = 2048
FC = 8
NBLK = 4
TB = 512
DFF = 4096
PLE = 256
DEPTH = 4
NCORES = 8
SEQ_PER_CORE = 4
EPS = 1e-6
NDS = 16
KS_STEPS = 11


class Buf:
    __slots__ = ("name", "w", "r")

    def __init__(self, name=""):
        self.name = name
        self.w = None
        self.r = {}


class Trk:
    def __init__(self, nc):
        self.nc = nc
        self.eng = {"pe": nc.tensor, "act": nc.scalar, "dve": nc.vector, "pool": nc.gpsimd, "sp": nc.sync}
        self.sem = {e: nc.alloc_semaphore("sem_" + e) for e in self.eng}
        self.cnt = {e: 0 for e in self.eng}
        self.seen = {e: {} for e in self.eng}
        self.dsem = [nc.alloc_semaphore("dsem%d" % i) for i in range(NDS)]
        self.dval = [0] * NDS
        self.dnext = 0
        self.fdeps = {}
        self.nbank = 0

    def fence(self):
        d = {}
        for e, c in self.cnt.items():
            if c > 0:
                d[e] = c
        for i, v in enumerate(self.dval):
            if v > 0:
                d[i] = v
        self.fdeps = d

    def _wait(self, e, deps):
        for key, val in deps.items():
            if self.seen[e].get(key, 0) >= val:
                continue
            semh = self.sem[key] if isinstance(key, str) else self.dsem[key]
            self.eng[e].wait_ge(semh, val)
            self.seen[e][key] = val

    def _deps(self, e, reads, writes, same_ok, nofence):
        deps = {}

        def add(k, v):
            if k == e and same_ok:
                return
            if deps.get(k, 0) < v:
                deps[k] = v

        for b in reads:
            if b.w:
                add(*b.w)
        for b in writes:
            if b.w:
                add(*b.w)
            for k, v in b.r.items():
                add(k, v)
        if not nofence:
            for k, v in self.fdeps.items():
                add(k, v)
        return deps

    def op(self, e, fn, reads=(), writes=(), inc=True):
        self._wait(e, self._deps(e, reads, writes, e == "pe", False))
        inst = fn(self.eng[e])
        if inc:
            self.cnt[e] += 1
            inst.then_inc(self.sem[e], 1)
            n = self.cnt[e]
        else:
            n = self.cnt[e] + 1
        for b in reads:
            b.r[e] = n
        for b in writes:
            b.w = (e, n)
            b.r = {}

    def dma(self, e, out, in_, reads=(), writes=(), nofence=False, slow=False):
        i = self.dnext
        self.dnext = (self.dnext + 1) % NDS
        deps = self._deps(e, reads, writes, False, nofence)
        if self.dval[i] > 0 and deps.get(i, 0) < self.dval[i]:
            deps[i] = self.dval[i]
        self._wait(e, deps)
        self.dval[i] += 16
        v = self.dval[i]
        if slow:
            self.eng[e].dma_start(out=out, in_=in_, allow_slow_non_contiguous=True).then_inc(self.dsem[i], 16)
        else:
            self.eng[e].dma_start(out=out, in_=in_).then_inc(self.dsem[i], 16)
        for b in reads:
            b.r[i] = v
        for b in writes:
            b.w = (i, v)
            b.r = {}


def sap(t, rs, p0, pn, off, dims):
    return bass.AP(t, p0 * rs + off, [[rs, pn]] + [[a, b] for a, b in dims])


def dap(t, off, dims):
    return bass.AP(t, off, [[a, b] for a, b in dims])


class Prog:
    def __init__(self, nseq=SEQ_PER_CORE, layers=(0, 1, 2, 3), do_final=True, mid_after=None, s5dbg=False):
        self.s5dbg = s5dbg
        self.mid_after = mid_after
        self.nseq = nseq
        self.layers = tuple(layers)
        self.do_final = do_final
        nc = bass.Bass("TRN2", target_bir_lowering=False)
        self.nc = nc
        self.K = Trk(nc)
        if getattr(self, "s5dbg", False):
            self.s5debug()
            return
        self.decl()
        self.alloc()
        self.consts()
        self.prologue_weights()
        for sl in range(2):
            if (3 * sl) in self.layers:
                self.s5_prologue(sl)
        self.K.fence()
        self.K._wait("sp", dict(self.K.fdeps))
        for s in range(nseq):
            self.load_x(s)
            for li in self.layers:
                self.layer(s, li)
                if li == self.mid_after:
                    self.store_out(s, self.out_mid, False)
            self.store_out(s, self.out, self.do_final)
        self.finish()

    def s5debug(self):
        nc = self.nc
        K = self.K
        inp = lambda name, shape: nc.dram_tensor(name, list(shape), F32, kind="ExternalInput")
        self.ssm_lam_re = inp("ssm_lam_re", [2, 64, 64])
        self.ssm_lam_im = inp("ssm_lam_im", [2, 64, 64])
        self.ssm_log_dt = inp("ssm_log_dt", [2, 64])
        self.ssm_b_re = inp("ssm_b_re", [2, 64, 64, 16])
        self.ssm_b_im = inp("ssm_b_im", [2, 64, 64, 16])
        self.ssm_c_re = inp("ssm_c_re", [2, 64, 16, 64])
        self.ssm_c_im = inp("ssm_c_im", [2, 64, 16, 64])
        self.alloc()
        cB = self.cB
        K.op("pool", lambda e: e.memset(self.onesf[:, :], 1.0), writes=[cB])
        K.op("pool", lambda e: e.affine_select(out=self.ident[:, :], in_=self.onesf[:, :], pattern=[[1, 128]],
                                                compare_op=ALU.is_equal, fill=0.0, base=0, channel_multiplier=-1),
             reads=[cB], writes=[cB])
        self.s5_prologue(0)
        K.fence()
        d_ap = nc.dram_tensor("d_apow", [128, 2 * KS_STEPS * 3 * 32], F32, kind="ExternalOutput")
        d_bt = nc.dram_tensor("d_bt", [128, 2 * 2 * FC * 128], BF16, kind="ExternalOutput")
        d_cd = nc.dram_tensor("d_cd", [128, 2 * 2 * 1024], BF16, kind="ExternalOutput")
        d_x = nc.dram_tensor("d_x", [128, 8192], F32, kind="ExternalOutput")
        d_id = nc.dram_tensor("d_id", [128, 128], F32, kind="ExternalOutput")
        ob = Buf()
        K.dma("sp", d_ap.ap(), self.Apow[:, :], writes=[ob])
        K.dma("sp", d_bt.ap(), self.BT[:, :], writes=[ob])
        K.dma("sp", d_cd.ap(), self.Cd[:, :], writes=[ob])
        K.dma("sp", d_x.ap(), sap(self.xT, FC * L, 0, 128, 0, [(1, 8192)]), writes=[ob])
        K.dma("sp", d_id.ap(), self.ident[:, :], writes=[ob])
        self.finish()

    def decl(self):
        nc = self.nc
        S = self.nseq

        def inp(name, shape):
            return nc.dram_tensor(name, list(shape), F32, kind="ExternalInput")

        self.x = inp("x", [S, L, D])
        self.p = inp("p", [DEPTH, S, L, PLE])
        self.norm_mix = inp("norm_mix", [DEPTH, D])
        self.norm_ffn = inp("norm_ffn", [DEPTH, D])
        self.norm_ple = inp("norm_ple", [DEPTH, D])
        self.norm_final = inp("norm_final", [D])
        self.ssm_lam_re = inp("ssm_lam_re", [2, 64, 64])
        self.ssm_lam_im = inp("ssm_lam_im", [2, 64, 64])
        self.ssm_log_dt = inp("ssm_log_dt", [2, 64])
        self.ssm_b_re = inp("ssm_b_re", [2, 64, 64, 16])
        self.ssm_b_im = inp("ssm_b_im", [2, 64, 64, 16])
        self.ssm_c_re = inp("ssm_c_re", [2, 64, 16, 64])
        self.ssm_c_im = inp("ssm_c_im", [2, 64, 16, 64])
        self.ssm_d = inp("ssm_d", [2, D])
        self.ssm_w_glu = inp("ssm_w_glu", [2, D, 2 * D])
        self.conv_w_in = inp("conv_w_in", [1, D, 3 * D])
        self.conv_w = inp("conv_w", [1, 3, D])
        self.conv_w_out = inp("conv_w_out", [1, D, D])
        self.fox_w_in = inp("fox_w_in", [1, D, 3 * D + 16])
        self.fox_b_f = inp("fox_b_f", [1, 16])
        self.fox_w_out = inp("fox_w_out", [1, D, D])
        self.mlp_w1 = inp("mlp_w1", [DEPTH, D, DFF])
        self.mlp_w2 = inp("mlp_w2", [DEPTH, DFF, D])
        self.ple_w = inp("ple_w", [DEPTH, PLE, D])
        self.ple_gate_w = inp("ple_gate_w", [DEPTH, D, D])
        self.out = nc.dram_tensor("out", [S, L, D], F32, kind="ExternalOutput")
        if self.mid_after is not None:
            self.out_mid = nc.dram_tensor("out_mid", [S, L, D], F32, kind="ExternalOutput")
        self.wscr = {}

    def add_w(self, key, src, src_off, ncols_total, nk, cw, col0, ncols):
        npieces = ncols // cw
        pe = nk * cw
        t = self.nc.dram_tensor("ws_" + key, [npieces * 128, pe], BF16, kind="Internal")
        self.wscr[key] = dict(t=t, npieces=npieces, pe=pe, nk=nk, cw=cw, src=src, src_off=src_off,
                              rs=ncols_total, col0=col0, ksplit=False)

    def add_w_ksplit(self, key, src, src_off, ncols_total, nkp, npieces):
        pe = nkp * ncols_total
        t = self.nc.dram_tensor("ws_" + key, [npieces * 128, pe], BF16, kind="Internal")
        self.wscr[key] = dict(t=t, npieces=npieces, pe=pe, nk=nkp, cw=ncols_total, src=src, src_off=src_off,
                              rs=ncols_total, col0=0, ksplit=True)

    def alloc(self):
        nc = self.nc
        self.xT = nc.alloc_sbuf_tensor("xT", [128, FC * L], F32)
        self.hB = nc.alloc_sbuf_tensor("hB", [128, FC * L], BF16)
        self.AR = 24576
        self.arena = nc.alloc_sbuf_tensor("arena", [128, self.AR], BF16)
        self.arena32 = self.arena.bitcast(F32)
        self.ring = nc.alloc_sbuf_tensor("ring", [128, 4 * 4096], BF16)
        self.ring32 = self.ring.bitcast(F32)
        self.ringB = [Buf("ring%d" % i) for i in range(4)]
        self.ringn = 0
        self.xBuf = [[Buf("x%d_%d" % (f, b)) for b in range(NBLK)] for f in range(FC)]
        self.hBuf = [[Buf("h%d_%d" % (f, b)) for b in range(NBLK)] for f in range(FC)]
        self.ps = [nc.alloc_psum_tensor("ps%d" % i, [128, 512], F32) for i in range(8)]
        self.psB = [Buf("ps%d" % i) for i in range(8)]
        self.gT = nc.alloc_sbuf_tensor("gT", [128, 3 * DEPTH * FC], F32)
        self.gfin = nc.alloc_sbuf_tensor("gfin", [128, D], F32)
        self.cwT = nc.alloc_sbuf_tensor("cwT", [128, 3 * FC], F32)
        self.dT = nc.alloc_sbuf_tensor("dT", [128, 2 * FC], F32)
        self.nbf = nc.alloc_sbuf_tensor("nbf", [128, 1], F32)
        self.ident = nc.alloc_sbuf_tensor("ident", [128, 128], F32)
        self.identb = nc.alloc_sbuf_tensor("identb", [128, 128], BF16)
        self.onesb = nc.alloc_sbuf_tensor("onesb", [128, 128], BF16)
        self.onesf = nc.alloc_sbuf_tensor("onesf", [128, 128], F32)
        self.trib = nc.alloc_sbuf_tensor("trib", [128, 128], BF16)
        self.wfT = nc.alloc_sbuf_tensor("wfT", [128, 8 * 16], BF16)
        self.halo = nc.alloc_sbuf_tensor("halo", [128, FC * 2], F32)
        self.BT = nc.alloc_sbuf_tensor("BT", [128, 2 * 2 * FC * 128], BF16)
        self.Cd = nc.alloc_sbuf_tensor("Cd", [128, 2 * 2 * 32 * 32], BF16)
        self.Apow = nc.alloc_sbuf_tensor("Apow", [128, 2 * KS_STEPS * 3 * 32], F32)
        self.cB = Buf("consts")

    def bank(self):
        bs = getattr(self, "bank_set", None) or list(range(8))
        self.K.nbank = (self.K.nbank + 1) % len(bs)
        i = bs[self.K.nbank]
        return self.ps[i], self.psB[i]

    def fbank(self, i):
        return self.ps[i], self.psB[i]

    def consts(self):
        K = self.K
        cB = self.cB
        for k, t in enumerate((self.norm_mix, self.norm_ffn, self.norm_ple)):
            for li in range(DEPTH):
                K.dma("sp", sap(self.gT, 3 * DEPTH * FC, 0, 128, (k * DEPTH + li) * FC, [(1, FC)]),
                      dap(t, li * D, [(1, 128), (128, FC)]), writes=[cB], slow=True)
        K.dma("sp", sap(self.gfin, D, 0, 128, 0, [(1, D)]), dap(self.norm_final, 0, [(0, 128), (1, D)]), writes=[cB])
        for tap in range(3):
            K.dma("sp", sap(self.cwT, 3 * FC, 0, 128, tap * FC, [(1, FC)]),
                  dap(self.conv_w, tap * D, [(1, 128), (128, FC)]), writes=[cB], slow=True)
        for sl in range(2):
            K.dma("sp", sap(self.dT, 2 * FC, 0, 128, sl * FC, [(1, FC)]),
                  dap(self.ssm_d, sl * D, [(1, 128), (128, FC)]), writes=[cB], slow=True)
        K.dma("sp", sap(self.nbf, 1, 0, 16, 0, [(1, 1)]), dap(self.fox_b_f, 0, [(1, 16), (1, 1)]), writes=[cB], slow=True)
        K.op("dve", lambda e: e.tensor_scalar(out=sap(self.nbf, 1, 0, 16, 0, [(1, 1)]), in0=sap(self.nbf, 1, 0, 16, 0, [(1, 1)]),
                                               scalar1=-1.0, scalar2=None, op0=ALU.mult), reads=[cB], writes=[cB])
        K.op("pool", lambda e: e.memset(self.onesf[:, :], 1.0), writes=[cB])
        K.op("pool", lambda e: e.memset(self.onesb[:, :], 1.0), writes=[cB])
        K.op("pool", lambda e: e.affine_select(out=self.ident[:, :], in_=self.onesf[:, :], pattern=[[1, 128]],
                                                compare_op=ALU.is_equal, fill=0.0, base=0, channel_multiplier=-1),
             reads=[cB], writes=[cB])
        K.op("pool", lambda e: e.affine_select(out=self.identb[:, :], in_=self.onesb[:, :], pattern=[[1, 128]],
                                                compare_op=ALU.is_equal, fill=0.0, base=0, channel_multiplier=-1),
             reads=[cB], writes=[cB])
        K.op("pool", lambda e: e.affine_select(out=self.trib[:, :], in_=self.onesb[:, :], pattern=[[1, 128]],
                                                compare_op=ALU.is_ge, fill=0.0, base=0, channel_multiplier=-1),
             reads=[cB], writes=[cB])
        K.op("pool", lambda e: e.memset(self.halo[:, :], 0.0), writes=[cB])

    def prologue_weights(self):
        K = self.K
        for li in self.layers:
            self.add_w("w1_%d" % li, self.mlp_w1, li * D * DFF, DFF, 8, 512, 0, DFF)
            self.add_w("w2_%d" % li, self.mlp_w2, li * DFF * D, D, 32, 128, 0, D)
            self.add_w("gate_%d" % li, self.ple_gate_w, li * D * D, D, 8, 512, 0, D)
            self.add_w("ple_%d" % li, self.ple_w, li * PLE * D, D, 2, 1024, 0, D)
            kind, sl = li % 3, li // 3
            if kind == 0:
                self.add_w("glu_%d" % sl, self.ssm_w_glu, sl * D * 2 * D, 2 * D, 8, 512, 0, 2 * D)
            elif kind == 1:
                self.add_w("cin", self.conv_w_in, 0, 3 * D, 8, 512, 0, 3 * D)
                self.add_w("cout", self.conv_w_out, 0, D, 8, 512, 0, D)
            else:
                self.add_w("fin", self.fox_w_in, 0, 3 * D + 16, 8, 256, 0, 3 * D)
                self.add_w_ksplit("fout", self.fox_w_out, 0, D, 2, 4)
        stB = [Buf("stg0"), Buf("stg1")]
        cvB = [Buf("cv0"), Buf("cv1")]
        n = 0
        for key, w in self.wscr.items():
            for pc in range(w["npieces"]):
                j = n % 2
                nk, cw, rs = w["nk"], w["cw"], w["rs"]
                if w["ksplit"]:
                    src = dap(w["src"], w["src_off"] + pc * nk * 128 * rs, [(rs, 128), (128 * rs, nk), (1, cw)])
                else:
                    src = dap(w["src"], w["src_off"] + w["col0"] + pc * cw, [(rs, 128), (128 * rs, nk), (1, cw)])
                stg = sap(self.xT, FC * L, 0, 128, j * 4096, [(cw, nk), (1, cw)])
                K.dma("sp", stg, src, writes=[stB[j]])
                cv = sap(self.hB, FC * L, 0, 128, j * 4096, [(1, nk * cw)])
                stg_flat = sap(self.xT, FC * L, 0, 128, j * 4096, [(1, nk * cw)])
                eng = ("act", "dve", "pool")[n % 3]
                if eng == "act":
                    K.op("act", lambda e, o=cv, i=stg_flat: e.activation(out=o, in_=i, func=AF.Copy), reads=[stB[j]], writes=[cvB[j]])
                else:
                    K.op(eng, lambda e, o=cv, i=stg_flat: e.tensor_copy(out=o, in_=i), reads=[stB[j]], writes=[cvB[j]])
                dst = dap(w["t"], pc * 128 * w["pe"], [(w["pe"], 128), (1, w["pe"])])
                K.dma("sp", dst, cv, reads=[cvB[j]], writes=[self.wB(key)])
                n += 1
        if 2 in self.layers:
            stg = sap(self.xT, FC * L, 0, 128, 2 * 4096, [(16, 8), (1, 16)])
            K.dma("sp", stg, dap(self.fox_w_in, 3 * D, [(3 * D + 16, 128), (128 * (3 * D + 16), 8), (1, 16)]), writes=[stB[0]], slow=True)
            K.op("dve", lambda e: e.tensor_copy(out=self.wfT[:, :], in_=sap(self.xT, FC * L, 0, 128, 2 * 4096, [(1, 128)])),
                 reads=[stB[0]], writes=[self.cB])

    def wB(self, key):
        w = self.wscr[key]
        if "B" not in w:
            w["B"] = Buf("w_" + key)
        return w["B"]

    def ring_load(self, key, pc):
        K = self.K
        w = self.wscr[key]
        i = self.ringn
        self.ringn = (i + 1) % 4
        pe = w["pe"]
        K.dma("sp", sap(self.ring, 4 * 4096, 0, 128, i * 4096, [(1, pe)]),
              dap(w["t"], pc * 128 * pe, [(pe, 128), (1, pe)]), reads=[self.wB(key)], writes=[self.ringB[i]], nofence=True)
        return i

    def stream(self, plist, consume, depth=2):
        n = len(plist)
        slots = {}
        for i in range(min(depth, n)):
            slots[i] = self.ring_load(*plist[i])
        for i in range(n):
            consume(i, slots.pop(i))
            if i + depth < n:
                slots[i + depth] = self.ring_load(*plist[i + depth])

    def rslot(self, i, off, dims, p0=0, pn=128):
        return sap(self.ring, 4 * 4096, p0, pn, i * 4096 + off, dims)

    def xap(self, fc, blk, p0=0, pn=128, n=TB, off=0):
        return sap(self.xT, FC * L, p0, pn, fc * L + blk * TB + off, [(1, n)])

    def hap(self, fc, tok0, n, p0=0, pn=128):
        return sap(self.hB, FC * L, p0, pn, fc * L + tok0, [(1, n)])

    def load_x(self, s):
        K = self.K
        K.fence()
        stB = [Buf(), Buf()]
        for tt in range(16):
            j = tt % 2
            stg = sap(self.arena32, self.AR // 2, 0, 128, j * 1024, [(1, 1024)])
            K.dma("sp", stg, dap(self.x, (s * L + tt * 128) * D, [(D, 128), (1, D)]), writes=[stB[j]])
            for half in range(2):
                pt, pb = self.bank()
                for q in range(4):
                    fc = half * 4 + q
                    K.op("pe", lambda e, pt=pt, q=q, fc=fc, j=j: e.transpose(
                        out=sap(pt, 512, 0, 128, q * 128, [(1, 128)]),
                        in_=sap(self.arena32, self.AR // 2, 0, 128, j * 1024 + fc * 128, [(1, 128)]),
                        identity=self.ident[:, :]), reads=[stB[j], self.cB], writes=[pb], inc=(q == 3))
                dst = sap(self.xT, FC * L, 0, 128, half * 4 * L + tt * 128, [(L, 4), (1, 128)])
                src = sap(pt, 512, 0, 128, 0, [(128, 4), (1, 128)])
                wr = [self.xBuf[half * 4 + q][tt // 4] for q in range(4)]
                if (tt + half) % 2 == 0:
                    K.op("act", lambda e, dst=dst, src=src: e.activation(out=dst, in_=src, func=AF.Copy), reads=[pb], writes=wr)
                else:
                    K.op("dve", lambda e, dst=dst, src=src: e.tensor_copy(out=dst, in_=src), reads=[pb], writes=wr)

    def store_out(self, s, target, final):
        K = self.K
        K.fence()
        oB = [Buf(), Buf()]
        ssB = [Buf(), Buf()]
        for tt in range(16):
            j = tt % 2
            blk = tt // 4
            ot = sap(self.arena32, self.AR // 2, 0, 128, j * 1024, [(1, 1024)])
            for half in range(2):
                pt, pb = self.bank()
                for q in range(4):
                    fc = half * 4 + q
                    K.op("pe", lambda e, pt=pt, q=q, fc=fc: e.transpose(
                        out=sap(pt, 512, 0, 128, q * 128, [(1, 128)]),
                        in_=sap(self.xT, FC * L, 0, 128, fc * L + tt * 128, [(1, 128)]),
                        identity=self.ident[:, :]), reads=[self.xBuf[fc][blk], self.cB], writes=[pb], inc=(q == 3))
                dst = sap(self.arena32, self.AR // 2, 0, 128, j * 1024 + half * 512, [(1, 512)])
                if final:
                    K.op("dve", lambda e, dst=dst, pt=pt: e.tensor_copy(out=dst, in_=pt[:, :]), reads=[pb], writes=[oB[j]])
                else:
                    K.op("act", lambda e, dst=dst, pt=pt: e.activation(out=dst, in_=pt[:, :], func=AF.Copy), reads=[pb], writes=[oB[j]])
            if final:
                sq = sap(self.arena32, self.AR // 2, 0, 128, 2048 + j * 1024, [(1, 1024)])
                ss = sap(self.arena32, self.AR // 2, 0, 128, 4096 + j, [(1, 1)])
                K.op("act", lambda e, sq=sq, ot=ot, ss=ss: e.activation(out=sq, in_=ot, func=AF.Square, accum_out=ss),
                     reads=[oB[j]], writes=[ssB[j]])
                K.op("act", lambda e, ss=ss: e.activation(out=ss, in_=ss, func=AF.Sqrt, scale=1.0 / D, bias=EPS), reads=[ssB[j]], writes=[ssB[j]])
                K.op("dve", lambda e, ss=ss: e.reciprocal(out=ss, in_=ss), reads=[ssB[j]], writes=[ssB[j]])
                K.op("dve", lambda e, ot=ot, ss=ss: e.scalar_tensor_tensor(out=ot, in0=ot, scalar=ss, in1=self.gfin[:, :],
                                                                            op0=ALU.mult, op1=ALU.mult),
                     reads=[oB[j], ssB[j], self.cB], writes=[oB[j]])
            K.dma("sp", dap(target, (s * L + tt * 128) * D, [(D, 128), (1, D)]), ot, reads=[oB[j]], writes=[self.outB()])

    def outB(self):
        if not hasattr(self, "_outB"):
            self._outB = Buf("out")
        return self._outB

    def finish(self):
        K = self.K
        deps = {i: v for i, v in enumerate(K.dval) if v > 0}
        K._wait("sp", deps)

    def norm(self, kind, li):
        K = self.K
        K.fence()
        self.bank_set = None
        sqB = Buf()
        rB = [Buf(), Buf()]
        goff = (kind * DEPTH + li) * FC
        for blk in range(NBLK):
            sq = sap(self.arena, self.AR, 0, 128, 0, [(TB, FC), (1, TB)])
            xin = sap(self.xT, FC * L, 0, 128, blk * TB, [(L, FC), (1, TB)])
            K.op("act", lambda e, sq=sq, xin=xin: e.activation(out=sq, in_=xin, func=AF.Square),
                 reads=[self.xBuf[f][blk] for f in range(FC)], writes=[sqB])
            pt, pb = self.bank()
            for fc in range(FC):
                K.op("pe", lambda e, pt=pt, fc=fc: e.matmul(pt[:, :], lhsT=self.onesb[:, :],
                                                            rhs=sap(self.arena, self.AR, 0, 128, fc * TB, [(1, TB)]),
                                                            start=(fc == 0), stop=(fc == FC - 1)),
                     reads=[sqB, self.cB], writes=[pb], inc=(fc == FC - 1))
            j = blk % 2
            rs = sap(self.arena32, self.AR // 2, 0, 128, 2048 + j * TB, [(1, TB)])
            K.op("act", lambda e, rs=rs, pt=pt: e.activation(out=rs, in_=pt[:, :], func=AF.Sqrt, scale=1.0 / D, bias=EPS),
                 reads=[pb], writes=[rB[j]])
            K.op("dve", lambda e, rs=rs: e.reciprocal(out=rs, in_=rs), reads=[rB[j]], writes=[rB[j]])
            for fc in range(FC):
                K.op("dve", lambda e, fc=fc, rs=rs, blk=blk: e.scalar_tensor_tensor(
                    out=self.hap(fc, blk * TB, TB), in0=self.xap(fc, blk),
                    scalar=sap(self.gT, 3 * DEPTH * FC, 0, 128, goff + fc, [(1, 1)]), in1=rs, op0=ALU.mult, op1=ALU.mult),
                     reads=[self.xBuf[fc][blk], rB[j], self.cB], writes=[self.hBuf[fc][blk]])

    def mlp(self, li):
        K = self.K
        K.fence()
        aB = [Buf() for _ in range(32)]
        for blk in range(NBLK):
            def c1(i, slot, blk=blk):
                for mm in range(4):
                    m = i * 4 + mm
                    pt, pb = self.bank()
                    for kc in range(FC):
                        K.op("pe", lambda e, pt=pt, kc=kc, mm=mm: e.matmul(
                            pt[:, :], lhsT=self.rslot(slot, kc * 512 + mm * 128, [(1, 128)]),
                            rhs=self.hap(kc, blk * TB, TB), start=(kc == 0), stop=(kc == FC - 1)),
                            reads=[self.ringB[slot], self.hBuf[kc][blk]], writes=[pb], inc=(kc == FC - 1))
                    a = sap(self.arena, self.AR, 0, 128, m * TB, [(1, TB)])
                    K.op("act", lambda e, a=a, pt=pt: e.activation(out=a, in_=pt[:, :], func=AF.Relu), reads=[pb], writes=[aB[m]])
                    eng = "pool" if m % 2 == 0 else "dve"
                    K.op(eng, lambda e, a=a: e.tensor_tensor(out=a, in0=a, in1=a, op=ALU.mult), reads=[aB[m]], writes=[aB[m]])
            self.stream([("w1_%d" % li, pc) for pc in range(8)], c1)

            def c2(i, slot, blk=blk):
                fo = i
                pt, pb = self.bank()
                for mc in range(32):
                    K.op("pe", lambda e, pt=pt, mc=mc: e.matmul(
                        pt[:, :], lhsT=self.rslot(slot, mc * 128, [(1, 128)]),
                        rhs=sap(self.arena, self.AR, 0, 128, mc * TB, [(1, TB)]), start=(mc == 0), stop=(mc == 31)),
                        reads=[self.ringB[slot], aB[mc]], writes=[pb], inc=(mc == 31))
                K.op("dve", lambda e, pt=pt, fo=fo: e.tensor_tensor(out=self.xap(fo, blk), in0=pt[:, :], in1=self.xap(fo, blk), op=ALU.add),
                     reads=[pb, self.xBuf[fo][blk]], writes=[self.xBuf[fo][blk]])
            self.stream([("w2_%d" % li, pc) for pc in range(8)], c2)

    def ple(self, s, li):
        K = self.K
        K.fence()
        A32 = self.AR // 2
        pstB = [Buf(), Buf()]
        pTB = Buf()
        gB = [Buf(), Buf()]
        tB = [Buf(), Buf()]
        for blk in range(NBLK):
            pts = [self.bank(), self.bank()]
            for t4 in range(4):
                j = t4 % 2
                stg = sap(self.arena32, A32, 0, 128, 1024 + j * 256, [(1, 256)])
                K.dma("sp", stg, dap(self.p, ((li * self.nseq + s) * L + blk * TB + t4 * 128) * PLE, [(PLE, 128), (1, PLE)]),
                      writes=[pstB[j]])
                for kc in range(2):
                    pt, pb = pts[kc]
                    K.op("pe", lambda e, pt=pt, kc=kc, j=j, t4=t4: e.transpose(
                        out=sap(pt, 512, 0, 128, t4 * 128, [(1, 128)]),
                        in_=sap(self.arena32, A32, 0, 128, 1024 + j * 256 + kc * 128, [(1, 128)]),
                        identity=self.ident[:, :]), reads=[pstB[j], self.cB], writes=[pb])
            for kc in range(2):
                pt, pb = pts[kc]
                K.op("act", lambda e, pt=pt, kc=kc: e.activation(out=sap(self.arena, self.AR, 0, 128, kc * TB, [(1, TB)]),
                                                                in_=pt[:, :], func=AF.Copy), reads=[pb], writes=[pTB])

            def cons(i, slot, blk=blk):
                if i < 2:
                    for q in range(4):
                        fo = i * 4 + q
                        pt, pb = self.bank()
                        for kc in range(FC):
                            K.op("pe", lambda e, pt=pt, kc=kc, q=q: e.matmul(
                                pt[:, :], lhsT=self.rslot(slot, kc * 512 + q * 128, [(1, 128)]),
                                rhs=self.hap(kc, blk * TB, TB), start=(kc == 0), stop=(kc == FC - 1)),
                                reads=[self.ringB[slot], self.hBuf[kc][blk]], writes=[pb], inc=(kc == FC - 1))
                        g = sap(self.arena32, A32, 0, 128, 2048 + fo * TB, [(1, TB)])
                        K.op("act", lambda e, g=g, pt=pt: e.activation(out=g, in_=pt[:, :], func=AF.Sigmoid), reads=[pb], writes=[self._gB[fo]])
                else:
                    for fo in range(FC):
                        pt, pb = self.bank()
                        for kc in range(2):
                            K.op("pe", lambda e, pt=pt, kc=kc, fo=fo: e.matmul(
                                pt[:, :], lhsT=self.rslot(slot, kc * 1024 + fo * 128, [(1, 128)]),
                                rhs=sap(self.arena, self.AR, 0, 128, kc * TB, [(1, TB)]), start=(kc == 0), stop=(kc == 1)),
                                reads=[self.ringB[slot], pTB], writes=[pb], inc=(kc == 1))
                        g = sap(self.arena32, A32, 0, 128, 2048 + fo * TB, [(1, TB)])
                        j = fo % 2
                        t = sap(self.arena32, A32, 0, 128, 6144 + j * TB, [(1, TB)])
                        K.op("dve", lambda e, t=t, pt=pt, g=g: e.tensor_tensor(out=t, in0=pt[:, :], in1=g, op=ALU.mult),
                             reads=[pb, self._gB[fo]], writes=[tB[j]])
                        K.op("pool", lambda e, t=t, fo=fo: e.tensor_tensor(out=self.xap(fo, blk), in0=self.xap(fo, blk), in1=t, op=ALU.add),
                             reads=[tB[j], self.xBuf[fo][blk]], writes=[self.xBuf[fo][blk]])
            self._gB = [Buf() for _ in range(FC)]
            self.stream([("gate_%d" % li, 0), ("gate_%d" % li, 1), ("ple_%d" % li, 0)], cons, depth=2)

    def conv(self):
        K = self.K
        K.fence()
        A32 = self.AR // 2
        ZS, CS, CV = 0, 4112, 6160
        GB = 14368
        zB = [Buf() for _ in range(FC)]
        cSB = [Buf() for _ in range(4)]
        cvB = [Buf(), Buf()]
        gB = [Buf() for _ in range(FC)]
        hlB = Buf()
        for blk in range(NBLK):
            if blk == 0:
                K.op("pool", lambda e: e.memset(sap(self.arena32, A32, 0, 128, ZS, [(514, FC), (1, 2)]), 0.0), writes=zB)
            else:
                K.op("pool", lambda e: e.tensor_copy(out=sap(self.arena32, A32, 0, 128, ZS, [(514, FC), (1, 2)]),
                                                      in_=sap(self.halo, FC * 2, 0, 128, 0, [(2, FC), (1, 2)])), reads=[hlB], writes=zB)

            def cons(i, slot, blk=blk):
                half = i // 3
                kind = i % 3
                for q in range(4):
                    fc = half * 4 + q
                    pt, pb = self.bank()
                    for kc in range(FC):
                        K.op("pe", lambda e, pt=pt, kc=kc, q=q: e.matmul(
                            pt[:, :], lhsT=self.rslot(slot, kc * 512 + q * 128, [(1, 128)]),
                            rhs=self.hap(kc, blk * TB, TB), start=(kc == 0), stop=(kc == FC - 1)),
                            reads=[self.ringB[slot], self.hBuf[kc][blk]], writes=[pb], inc=(kc == FC - 1))
                    z = sap(self.arena32, A32, 0, 128, ZS + fc * 514 + 2, [(1, TB)])
                    if kind == 0:
                        c = sap(self.arena32, A32, 0, 128, CS + q * TB, [(1, TB)])
                        K.op("act", lambda e, c=c, pt=pt: e.activation(out=c, in_=pt[:, :], func=AF.Copy), reads=[pb], writes=[cSB[q]])
                    elif kind == 1:
                        c = sap(self.arena32, A32, 0, 128, CS + q * TB, [(1, TB)])
                        K.op("dve", lambda e, z=z, pt=pt, c=c: e.tensor_tensor(out=z, in0=pt[:, :], in1=c, op=ALU.mult),
                             reads=[pb, cSB[q]], writes=[zB[fc]])
                    else:
                        j = q % 2
                        cv = sap(self.arena32, A32, 0, 128, CV + j * TB, [(1, TB)])
                        w = lambda tap, fc=fc: sap(self.cwT, 3 * FC, 0, 128, tap * FC + fc, [(1, 1)])
                        K.op("act", lambda e, cv=cv, z=z, w=w: e.activation(out=cv, in_=z, func=AF.Copy, scale=w(2)),
                             reads=[zB[fc], self.cB], writes=[cvB[j]])
                        z1 = sap(self.arena32, A32, 0, 128, ZS + fc * 514 + 1, [(1, TB)])
                        z0 = sap(self.arena32, A32, 0, 128, ZS + fc * 514 + 0, [(1, TB)])
                        K.op("dve", lambda e, cv=cv, z1=z1, w=w: e.scalar_tensor_tensor(out=cv, in0=z1, scalar=w(1), in1=cv, op0=ALU.mult, op1=ALU.add),
                             reads=[zB[fc], cvB[j], self.cB], writes=[cvB[j]])
                        K.op("dve", lambda e, cv=cv, z0=z0, w=w: e.scalar_tensor_tensor(out=cv, in0=z0, scalar=w(0), in1=cv, op0=ALU.mult, op1=ALU.add),
                             reads=[zB[fc], cvB[j], self.cB], writes=[cvB[j]])
                        g = sap(self.arena, self.AR, 0, 128, GB + fc * TB, [(1, TB)])
                        K.op("dve", lambda e, g=g, pt=pt, cv=cv: e.tensor_tensor(out=g, in0=pt[:, :], in1=cv, op=ALU.mult),
                             reads=[pb, cvB[j]], writes=[gB[fc]])
            self.stream([("cin", 2), ("cin", 4), ("cin", 0), ("cin", 3), ("cin", 5), ("cin", 1)], cons)
            K.op("pool", lambda e: e.tensor_copy(out=sap(self.halo, FC * 2, 0, 128, 0, [(2, FC), (1, 2)]),
                                                  in_=sap(self.arena32, A32, 0, 128, ZS + 512, [(514, FC), (1, 2)])), reads=zB, writes=[hlB])

            def cons2(i, slot, blk=blk):
                for q in range(4):
                    fo = i * 4 + q
                    pt, pb = self.bank()
                    for kc in range(FC):
                        K.op("pe", lambda e, pt=pt, kc=kc, q=q: e.matmul(
                            pt[:, :], lhsT=self.rslot(slot, kc * 512 + q * 128, [(1, 128)]),
                            rhs=sap(self.arena, self.AR, 0, 128, GB + kc * TB, [(1, TB)]), start=(kc == 0), stop=(kc == FC - 1)),
                            reads=[self.ringB[slot], gB[kc]], writes=[pb], inc=(kc == FC - 1))
                    K.op("dve", lambda e, pt=pt, fo=fo: e.tensor_tensor(out=self.xap(fo, blk), in0=pt[:, :], in1=self.xap(fo, blk), op=ALU.add),
                         reads=[pb, self.xBuf[fo][blk]], writes=[self.xBuf[fo][blk]])
            self.stream([("cout", 0), ("cout", 1)], cons2)

    def fox(self):
        K = self.K
        K.fence()
        A32 = self.AR // 2
        AR = self.AR
        QT, KT, VV, OG, PT = 0, 4096, 8192, 12352, 14400
        CUM, CTM, BIA, OSB, RRW = 8192, 10240, 10496, 11520, 11776
        cumB, ctmB, biaB = Buf(), Buf(), Buf()
        allh = lambda blk: [self.hBuf[f][blk] for f in range(FC)]
        cum = lambda c0, n: sap(self.arena32, A32, 0, 16, CUM + c0, [(1, n)])
        for blk in range(NBLK):
            pt, pb = self.bank()
            for kc in range(FC):
                K.op("pe", lambda e, pt=pt, kc=kc, blk=blk: e.matmul(
                    sap(pt, 512, 0, 16, 0, [(1, TB)]), lhsT=sap(self.wfT, 128, 0, 128, kc * 16, [(1, 16)]),
                    rhs=self.hap(kc, blk * TB, TB), start=(kc == 0), stop=(kc == FC - 1)),
                    reads=[self.cB, self.hBuf[kc][blk]], writes=[pb], inc=(kc == FC - 1))
            K.op("act", lambda e, pt=pt, blk=blk: e.activation(out=cum(blk * TB, TB), in_=sap(pt, 512, 0, 16, 0, [(1, TB)]),
                                                              func=AF.Exp, scale=-1.0, bias=sap(self.nbf, 1, 0, 16, 0, [(1, 1)])),
                 reads=[pb, self.cB], writes=[cumB])
        K.op("act", lambda e: e.activation(out=cum(0, L), in_=cum(0, L), func=AF.Ln, scale=1.0, bias=1.0), reads=[cumB], writes=[cumB])
        K.op("dve", lambda e: e.tensor_scalar(out=cum(0, L), in0=cum(0, L), scalar1=-1.0, scalar2=None, op0=ALU.mult), reads=[cumB], writes=[cumB])
        K.op("dve", lambda e: e.tensor_tensor_scan(out=cum(0, L), data0=sap(self.onesf, 128, 0, 16, 0, [(0, L)]), data1=cum(0, L),
                                                   initial=0.0, op0=ALU.mult, op1=ALU.add), reads=[cumB, self.cB], writes=[cumB])
        pt, pb = self.bank()
        for tt in range(16):
            K.op("pe", lambda e, pt=pt, tt=tt: e.transpose(out=sap(pt, 512, 0, 128, tt * 16, [(1, 16)]), in_=cum(tt * 128, 128),
                                                          identity=sap(self.ident, 128, 0, 16, 0, [(1, 16)])),
                 reads=[cumB, self.cB], writes=[pb], inc=(tt == 15))
        K.op("dve", lambda e, pt=pt: e.tensor_copy(out=sap(self.arena32, A32, 0, 128, CTM, [(1, 256)]), in_=sap(pt, 512, 0, 128, 0, [(1, 256)])),
             reads=[pb], writes=[ctmB])
        rhsD = sap(self.arena32, A32, 0, 16, OSB, [(1, 256)])
        K.op("dve", lambda e: e.tensor_tensor(out=sap(self.arena32, A32, 0, 16, OSB, [(16, 16), (1, 16)]),
                                              in0=sap(self.ident, 128, 0, 16, 0, [(0, 16), (1, 16)]),
                                              in1=sap(self.arena32, A32, 0, 16, CUM, [(128, 16), (0, 16)]), op=ALU.mult),
             reads=[cumB, self.cB], writes=[biaB])
        ptc, pbc = self.bank()
        K.op("pe", lambda e: e.matmul(sap(ptc, 512, 0, 128, 0, [(1, 256)]), lhsT=sap(self.onesf, 128, 0, 16, 0, [(1, 128)]), rhs=rhsD,
                                      start=True, stop=True), reads=[biaB, self.cB], writes=[pbc])
        BIH = CUM
        for hf in range(8):
            K.op("dve", lambda e, hf=hf: e.tensor_tensor(out=sap(self.arena32, A32, 0, 128, BIH + hf * 256, [(16, 16), (1, 16)]),
                                                         in0=sap(ptc, 512, 0, 128, (2 * hf + 1) * 16, [(0, 16), (1, 16)]),
                                                         in1=sap(self.arena32, A32, 0, 128, CTM, [(16, 16), (1, 16)]), op=ALU.subtract),
                 reads=[pbc, ctmB, cumB], writes=[biaB, cumB])
        qB, kB, vB = Buf(), Buf(), Buf()
        pTB = [Buf(), Buf(), Buf()]
        oGB = [Buf(), Buf()]
        osB, rrB = Buf(), Buf()
        npt = 0
        for hg in range(4):
            K.op("pool", lambda e: e.memset(sap(self.arena, AR, 0, 128, VV + 64, [(65, 64), (1, 1)]), 1.0), reads=[vB], writes=[vB])

            def consqkv(i, slot, hg=hg):
                if i < 2:
                    base, bb = (QT, qB) if i == 0 else (KT, kB)
                    for blk in range(NBLK):
                        for c2 in range(2):
                            pt, pb = self.bank()
                            for kc in range(FC):
                                K.op("pe", lambda e, pt=pt, kc=kc, c2=c2, blk=blk: e.matmul(
                                    pt[:, :], lhsT=self.rslot(slot, kc * 256 + c2 * 128, [(1, 128)]),
                                    rhs=self.hap(kc, blk * TB, TB), start=(kc == 0), stop=(kc == FC - 1)),
                                    reads=[self.ringB[slot], self.hBuf[kc][blk]], writes=[pb], inc=(kc == FC - 1))
                            dst = sap(self.arena, AR, 0, 128, base + c2 * L + blk * TB, [(1, TB)])
                            if (blk + c2) % 2 == 0:
                                K.op("act", lambda e, dst=dst, pt=pt: e.activation(out=dst, in_=pt[:, :], func=AF.Copy), reads=[pb], writes=[bb])
                            else:
                                K.op("dve", lambda e, dst=dst, pt=pt: e.tensor_copy(out=dst, in_=pt[:, :]), reads=[pb], writes=[bb])
                else:
                    for tt in range(16):
                        pt, pb = self.bank()
                        for kc in range(FC):
                            K.op("pe", lambda e, pt=pt, kc=kc, tt=tt: e.matmul(
                                sap(pt, 512, 0, 128, 0, [(1, 256)]), lhsT=self.hap(kc, tt * 128, 128),
                                rhs=self.rslot(slot, kc * 256, [(1, 256)]), start=(kc == 0), stop=(kc == FC - 1)),
                                reads=[self.ringB[slot], self.hBuf[kc][tt // 4]], writes=[pb], inc=(kc == FC - 1))
                        dst = sap(self.arena, AR, 0, 128, VV + tt * 260, [(65, 4), (1, 64)])
                        src = sap(pt, 512, 0, 128, 0, [(64, 4), (1, 64)])
                        if tt % 2 == 0:
                            K.op("act", lambda e, dst=dst, src=src: e.activation(out=dst, in_=src, func=AF.Copy), reads=[pb], writes=[vB])
                        else:
                            K.op("dve", lambda e, dst=dst, src=src: e.tensor_copy(out=dst, in_=src), reads=[pb], writes=[vB])
            self.stream([("fin", hg), ("fin", 4 + hg), ("fin", 8 + hg)], consqkv)
            wslot = self.ring_load("fout", hg)
            self.bank_set = [0, 1, 2, 3, 4]
            nhead = 0
            for qb in range(4):
                og = OG + (qb % 2) * 1024
                for h4 in range(4):
                    c2, ph = h4 // 2, 64 * (h4 % 2)
                    h = hg * 4 + h4
                    po, pob = self.fbank(6 + nhead % 2)
                    nhead += 1
                    nj = 4 * qb + 4
                    for j in range(nj):
                        r = j - 4 * qb
                        c0 = 128 * r if r > 0 else 0
                        n = TB - c0
                        pst, psb = self.bank()
                        K.op("pe", lambda e, pst=pst, j=j, c0=c0, n=n, c2=c2, ph=ph, qb=qb: e.matmul(
                            sap(pst, 512, 0, 128, c0, [(1, n)]),
                            lhsT=sap(self.arena, AR, ph, 64, KT + c2 * L + j * 128, [(1, 128)]),
                            rhs=sap(self.arena, AR, ph, 64, QT + c2 * L + qb * TB + c0, [(1, n)]), start=True, stop=True),
                            reads=[qB, kB], writes=[psb])
                        pj = npt % 3
                        npt += 1
                        pT = sap(self.arena, AR, 0, 128, PT + pj * TB + c0, [(1, n)])
                        for hq in range(2):
                            lo = max(c0, 256 * hq)
                            hi = 256 * hq + 256
                            if lo >= hi:
                                continue
                            sub = sap(self.arena, AR, 0, 128, PT + pj * TB + lo, [(1, hi - lo)])
                            K.op("act", lambda e, sub=sub, pst=pst, lo=lo, hi=hi, hq=hq, qb=qb, j=j, h=h: e.activation(
                                out=sub, in_=sap(pst, 512, 0, 128, lo, [(1, hi - lo)]), func=AF.Exp, scale=0.125,
                                bias=sap(self.arena32, A32, 0, 128, BIH + (2 * qb + hq) * 256 + j * 16 + h, [(1, 1)])),
                                reads=[psb, biaB], writes=[pTB[pj]])
                        if r >= 0:
                            pd = sap(self.arena, AR, 0, 128, PT + pj * TB + 128 * r, [(1, 128)])
                            K.op("pool", lambda e, pd=pd: e.tensor_tensor(out=pd, in0=pd, in1=self.trib[:, :], op=ALU.mult),
                                 reads=[pTB[pj], self.cB], writes=[pTB[pj]])
                        K.op("pe", lambda e, po=po, pT=pT, j=j, c0=c0, n=n, h4=h4, nj=nj: e.matmul(
                            sap(po, 512, 0, 65, c0, [(1, n)]),
                            lhsT=sap(self.arena, AR, 0, 128, VV + j * 260 + h4 * 65, [(1, 65)]), rhs=pT,
                            start=(j == 0), stop=(j == nj - 1)), reads=[vB, pTB[pj]], writes=[pob], inc=(j == nj - 1))
                    rr = sap(self.arena32, A32, 64, 1, RRW, [(1, TB)])
                    K.op("dve", lambda e, rr=rr, po=po: e.reciprocal(out=rr, in_=sap(po, 512, 64, 1, 0, [(1, TB)])), reads=[pob], writes=[rrB])
                    osb = sap(self.arena32, A32, 0, 64, OSB, [(1, TB)])
                    K.op("act", lambda e, osb=osb, po=po: e.activation(out=osb, in_=sap(po, 512, 0, 64, 0, [(1, TB)]), func=AF.Copy),
                         reads=[pob, biaB], writes=[osB])
                    pr, prb = self.fbank(5)
                    K.op("pe", lambda e, pr=pr, rr=rr: e.matmul(sap(pr, 512, 0, 64, 0, [(1, TB)]),
                                                               lhsT=sap(self.onesf, 128, 64, 1, 0, [(1, 64)]), rhs=rr, start=True, stop=True),
                         reads=[rrB, self.cB], writes=[prb])
                    K.op("dve", lambda e, osb=osb, pr=pr, og=og, c2=c2, ph=ph: e.tensor_tensor(
                        out=sap(self.arena, AR, ph, 64, og + c2 * TB, [(1, TB)]), in0=osb, in1=sap(pr, 512, 0, 64, 0, [(1, TB)]), op=ALU.mult),
                        reads=[osB, prb], writes=[oGB[qb % 2]])
                for fo in range(FC):
                    pt, pb = self.bank()
                    for kc in range(2):
                        K.op("pe", lambda e, pt=pt, kc=kc, fo=fo, og=og: e.matmul(
                            pt[:, :], lhsT=self.rslot(wslot, kc * 1024 + fo * 128, [(1, 128)]),
                            rhs=sap(self.arena, AR, 0, 128, og + kc * TB, [(1, TB)]), start=(kc == 0), stop=(kc == 1)),
                            reads=[self.ringB[wslot], oGB[qb % 2]], writes=[pb], inc=(kc == 1))
                    K.op("dve", lambda e, pt=pt, fo=fo, qb=qb: e.tensor_tensor(out=self.xap(fo, qb), in0=pt[:, :], in1=self.xap(fo, qb), op=ALU.add),
                         reads=[pb, self.xBuf[fo][qb]], writes=[self.xBuf[fo][qb]])

    def s5_prologue(self, sl):
        K = self.K
        K.fence()
        X = self.xT
        RS = FC * L
        B = Buf("s5pro")
        o = [0]

        def T(n):
            a = o[0]
            o[0] += n
            return a
        t = lambda off, n=32: sap(X, RS, 0, 128, off, [(1, n)])
        lr, li_, ldt, dt, th, mg, s16, c16 = (T(32) for _ in range(8))
        er, ei, r2, m2, nm, den, cr, ci, nr, t1, t2 = (T(32) for _ in range(11))
        for (src, dst) in ((self.ssm_lam_re, lr), (self.ssm_lam_im, li_)):
            for g2 in range(2):
                K.dma("sp", sap(X, RS, 64 * g2, 64, dst, [(1, 32)]), dap(src, sl * 4096 + g2 * 64, [(1, 64), (128, 32)]), writes=[B], slow=True)
        for g2 in range(2):
            K.dma("sp", sap(X, RS, 64 * g2, 64, ldt, [(1, 32)]), dap(self.ssm_log_dt, sl * 64 + g2, [(0, 64), (2, 32)]), writes=[B], slow=True)
        A = lambda fn: K.op("act", fn, reads=[B], writes=[B])
        V = lambda fn: K.op("dve", fn, reads=[B], writes=[B])
        tt_ = lambda o_, a, b, op, n=32: V(lambda e: e.tensor_tensor(out=t(o_, n), in0=t(a, n), in1=t(b, n), op=op))
        A(lambda e: e.activation(out=t(dt), in_=t(ldt), func=AF.Exp))
        tt_(th, li_, dt, ALU.mult)
        tt_(mg, lr, dt, ALU.mult)
        V(lambda e: e.tensor_scalar(out=t(th), in0=t(th), scalar1=1.0 / 16, scalar2=None, op0=ALU.mult))
        V(lambda e: e.tensor_scalar(out=t(mg), in0=t(mg), scalar1=1.0 / 16, scalar2=None, op0=ALU.mult))
        uu = T(32)
        acc = T(32)
        tt_(uu, th, th, ALU.mult)

        def horner(dst, coefs, var):
            V(lambda e: e.tensor_scalar(out=t(acc), in0=t(var), scalar1=float(coefs[-1]), scalar2=None, op0=ALU.mult))
            for c in coefs[-2:0:-1]:
                V(lambda e, c=c: e.scalar_tensor_tensor(out=t(acc), in0=t(acc), scalar=float(c), in1=t(var), op0=ALU.add, op1=ALU.mult))
            V(lambda e: e.tensor_scalar(out=t(dst), in0=t(acc), scalar1=float(coefs[0]), scalar2=None, op0=ALU.add))
        f = math.factorial
        horner(s16, [(-1.0) ** k / f(2 * k + 1) for k in range(7)], uu)
        tt_(s16, s16, th, ALU.mult)
        horner(c16, [(-1.0) ** k / f(2 * k) for k in range(7)], uu)
        y_ = T(32)
        V(lambda e: e.tensor_copy(out=t(y_), in_=t(mg)))
        horner(mg, [1.0 / f(k) for k in range(5)], y_)
        tt_(er, mg, c16, ALU.mult)
        tt_(ei, mg, s16, ALU.mult)

        def square():
            tt_(r2, er, er, ALU.mult)
            tt_(m2, ei, ei, ALU.mult)
            tt_(nm, er, ei, ALU.mult)
            tt_(er, r2, m2, ALU.subtract)
            V(lambda e: e.tensor_scalar(out=t(ei), in0=t(nm), scalar1=2.0, scalar2=None, op0=ALU.mult))
        for _ in range(4):
            square()
        V(lambda e: e.tensor_scalar(out=t(nr), in0=t(er), scalar1=-1.0, scalar2=None, op0=ALU.add))
        tt_(t1, lr, lr, ALU.mult)
        tt_(t2, li_, li_, ALU.mult)
        tt_(den, t1, t2, ALU.add)
        V(lambda e: e.reciprocal(out=t(den), in_=t(den)))
        tt_(t1, nr, lr, ALU.mult)
        tt_(t2, ei, li_, ALU.mult)
        tt_(cr, t1, t2, ALU.add)
        tt_(cr, cr, den, ALU.mult)
        tt_(t1, ei, lr, ALU.mult)
        tt_(t2, nr, li_, ALU.mult)
        tt_(ci, t1, t2, ALU.subtract)
        tt_(ci, ci, den, ALU.mult)
        for k in range(KS_STEPS):
            ap_ = lambda c, k=k: sap(self.Apow, 2 * KS_STEPS * 3 * 32, 0, 128, ((sl * KS_STEPS + k) * 3 + c) * 32, [(1, 32)])
            V(lambda e, ap_=ap_: e.tensor_copy(out=ap_(0), in_=t(er)))
            V(lambda e, ap_=ap_: e.tensor_copy(out=ap_(1), in_=t(ei)))
            V(lambda e, ap_=ap_: e.tensor_scalar(out=ap_(2), in0=t(ei), scalar1=-1.0, scalar2=None, op0=ALU.mult))
            if k < KS_STEPS - 1:
                square()
        Bn = [T(512), T(512)]
        for part, src in enumerate((self.ssm_b_re, self.ssm_b_im)):
            K.dma("sp", sap(X, RS, 0, 128, Bn[part], [(16, 32), (1, 16)]), dap(src, sl * 65536, [(16, 128), (2048, 32), (1, 16)]), writes=[B])
        Bb = [T(512), T(512)]
        u1, u2 = T(512), T(512)
        b3 = lambda off: sap(X, RS, 0, 128, off, [(16, 32), (1, 16)])
        cb = lambda off: sap(X, RS, 0, 128, off, [(1, 32), (0, 16)])
        V(lambda e: e.tensor_tensor(out=b3(u1), in0=b3(Bn[0]), in1=cb(cr), op=ALU.mult))
        V(lambda e: e.tensor_tensor(out=b3(u2), in0=b3(Bn[1]), in1=cb(ci), op=ALU.mult))
        V(lambda e: e.tensor_tensor(out=b3(Bb[0]), in0=b3(u1), in1=b3(u2), op=ALU.subtract))
        V(lambda e: e.tensor_tensor(out=b3(u1), in0=b3(Bn[1]), in1=cb(cr), op=ALU.mult))
        V(lambda e: e.tensor_tensor(out=b3(u2), in0=b3(Bn[0]), in1=cb(ci), op=ALU.mult))
        V(lambda e: e.tensor_tensor(out=b3(Bb[1]), in0=b3(u1), in1=b3(u2), op=ALU.add))
        Bd = [T(1024), T(1024)]
        for part in range(2):
            V(lambda e, part=part: e.memset(t(Bd[part], 1024), 0.0))
            for g2 in range(2):
                V(lambda e, part=part, g2=g2: e.tensor_copy(out=sap(X, RS, 64 * g2, 64, Bd[part] + 16 * g2, [(32, 32), (1, 16)]),
                                                            in_=sap(X, RS, 64 * g2, 64, Bb[part], [(16, 32), (1, 16)])))
            for fc in range(FC):
                pt, pb = self.bank()
                K.op("pe", lambda e, pt=pt, part=part, fc=fc: e.transpose(out=sap(pt, 512, 0, 128, 0, [(1, 128)]),
                                                                         in_=t(Bd[part] + fc * 128, 128), identity=self.ident[:, :]),
                     reads=[B, self.cB], writes=[pb])
                K.op("act", lambda e, pt=pt, part=part, fc=fc: e.activation(
                    out=sap(self.BT, 2 * 2 * FC * 128, 0, 128, ((sl * 2 + part) * FC + fc) * 128, [(1, 128)]),
                    in_=sap(pt, 512, 0, 128, 0, [(1, 128)]), func=AF.Copy), reads=[pb], writes=[self.cB])
        CT = [T(1024), T(1024)]
        for part, src in enumerate((self.ssm_c_re, self.ssm_c_im)):
            V(lambda e, part=part: e.memset(t(CT[part], 1024), 0.0))
            for gpl in range(4):
                for g2 in range(2):
                    p0 = 32 * gpl + 16 * g2
                    K.dma("sp", sap(X, RS, p0, 16, CT[part] + 64 * g2, [(128, FC), (1, 64)]),
                          dap(src, sl * 65536 + (2 * gpl + g2) * 1024, [(64, 16), (8192, FC), (1, 64)]), reads=[B], writes=[B])
            for fc in range(FC):
                pt, pb = self.bank()
                K.op("pe", lambda e, pt=pt, part=part, fc=fc: e.transpose(out=sap(pt, 512, 0, 128, 0, [(1, 128)]),
                                                                         in_=t(CT[part] + fc * 128, 128), identity=self.ident[:, :]),
                     reads=[B, self.cB], writes=[pb])
                K.op("act", lambda e, pt=pt, part=part, fc=fc: e.activation(
                    out=sap(self.Cd, 2 * 2 * 1024, 0, 128, (sl * 2 + part) * 1024 + fc * 128, [(1, 128)]),
                    in_=sap(pt, 512, 0, 128, 0, [(1, 128)]), func=AF.Copy, scale=(1.0 if part == 0 else -1.0)), reads=[pb], writes=[self.cB])

    def s5(self, sl):
        K = self.K
        K.fence()
        A32 = self.AR // 2
        AR = self.AR
        RE, IM, TA, TBf = 0, 2048, 4096, 6144
        XB = 16384
        reB, imB, taB, tbB = Buf(), Buf(), Buf(), Buf()
        xbB = [Buf(), Buf()]
        st = lambda off, c0, n: sap(self.arena32, A32, 0, 128, off + c0, [(1, n)])
        ybanks = [(self.ps[4 + b], self.psB[4 + b]) for b in range(4)]
        sbanks = [(self.ps[b], self.psB[b]) for b in range(4)]
        for fc in range(FC):
            for gpl in range(4):
                gp = fc * 4 + gpl
                for part, (off, bb) in enumerate(((RE, reB), (IM, imB))):
                    for blk in range(NBLK):
                        pt, pb = sbanks[blk]
                        K.op("pe", lambda e, pt=pt, part=part, blk=blk, gpl=gpl, fc=fc: e.matmul(
                            pt[:, :], lhsT=sap(self.BT, 2 * 2 * FC * 128, 32 * gpl, 32, ((sl * 2 + part) * FC + fc) * 128, [(1, 128)]),
                            rhs=self.hap(fc, blk * TB, TB, p0=32 * gpl, pn=32), start=True, stop=True,
                            tile_position=(32 * gpl, 0)), reads=[self.cB, self.hBuf[fc][blk]], writes=[pb])
                        K.op("act", lambda e, pt=pt, off=off, blk=blk: e.activation(out=st(off, blk * TB, TB), in_=pt[:, :], func=AF.Copy),
                             reads=[pb], writes=[bb])
                for k in range(KS_STEPS):
                    d = 1 << k
                    n = L - d
                    cf = lambda c, k=k, gp=gp: sap(self.Apow, 2 * KS_STEPS * 3 * 32, 0, 128, ((sl * KS_STEPS + k) * 3 + c) * 32 + gp, [(1, 1)])
                    K.op("dve", lambda e, d=d, n=n, cf=cf: e.scalar_tensor_tensor(out=st(TA, d, n), in0=st(RE, 0, n), scalar=cf(0), in1=st(RE, d, n),
                                                                                  op0=ALU.mult, op1=ALU.add), reads=[reB, self.cB], writes=[taB])
                    K.op("dve", lambda e, d=d, n=n, cf=cf: e.scalar_tensor_tensor(out=st(TBf, d, n), in0=st(RE, 0, n), scalar=cf(1), in1=st(IM, d, n),
                                                                                  op0=ALU.mult, op1=ALU.add), reads=[reB, imB, self.cB], writes=[tbB])
                    K.op("dve", lambda e, d=d, n=n, cf=cf: e.scalar_tensor_tensor(out=st(RE, d, n), in0=st(IM, 0, n), scalar=cf(2), in1=st(TA, d, n),
                                                                                  op0=ALU.mult, op1=ALU.add), reads=[imB, taB, self.cB], writes=[reB])
                    K.op("dve", lambda e, d=d, n=n, cf=cf: e.scalar_tensor_tensor(out=st(TA, d, n), in0=st(IM, 0, n), scalar=cf(0), in1=st(TBf, d, n),
                                                                                  op0=ALU.mult, op1=ALU.add), reads=[imB, tbB, self.cB], writes=[taB])
                    K.op("act", lambda e, d=d, n=n: e.activation(out=st(IM, d, n), in_=st(TA, d, n), func=AF.Copy), reads=[taB], writes=[imB])
                for part, (off, bb) in enumerate(((RE, reB), (IM, imB))):
                    K.op("pool", lambda e, part=part, off=off: e.tensor_copy(out=sap(self.arena, AR, 0, 128, XB + part * L, [(1, L)]), in_=st(off, 0, L)),
                         reads=[bb], writes=[xbB[part]])
                for blk in range(NBLK):
                    yt, yb = ybanks[blk]
                    for part in range(2):
                        K.op("pe", lambda e, yt=yt, part=part, blk=blk, gpl=gpl, gp=gp: e.matmul(
                            sap(yt, 512, 32 * gpl, 32, 0, [(1, TB)]),
                            lhsT=sap(self.Cd, 2 * 2 * 1024, 0, 128, (sl * 2 + part) * 1024 + gp * 32, [(1, 32)]),
                            rhs=sap(self.arena, AR, 0, 128, XB + part * L + blk * TB, [(1, TB)]), start=(part == 0), stop=(part == 1),
                            tile_position=(0, 32 * gpl)), reads=[self.cB, xbB[part]], writes=[yb], inc=(part == 1))
            for blk in range(NBLK):
                yt, yb = ybanks[blk]
                K.op("dve", lambda e, yt=yt, blk=blk, fc=fc: e.scalar_tensor_tensor(
                    out=st(TA, blk * TB, TB), in0=self.hap(fc, blk * TB, TB), scalar=sap(self.dT, 2 * FC, 0, 128, sl * FC + fc, [(1, 1)]),
                    in1=yt[:, :], op0=ALU.mult, op1=ALU.add), reads=[yb, self.hBuf[fc][blk], self.cB], writes=[taB])
                K.op("act", lambda e, blk=blk, fc=fc: e.activation(out=self.hap(fc, blk * TB, TB), in_=st(TA, blk * TB, TB), func=AF.Gelu_apprx_tanh),
                     reads=[taB], writes=[self.hBuf[fc][blk]])
        K.fence()
        sgB = [Buf() for _ in range(FC)]
        tB_ = [Buf(), Buf()]
        for blk in range(NBLK):
            def cons(i, slot, blk=blk):
                for q in range(4):
                    fo = (i % 2) * 4 + q
                    pt, pb = self.bank()
                    for kc in range(FC):
                        K.op("pe", lambda e, pt=pt, kc=kc, q=q: e.matmul(
                            pt[:, :], lhsT=self.rslot(slot, kc * 512 + q * 128, [(1, 128)]),
                            rhs=self.hap(kc, blk * TB, TB), start=(kc == 0), stop=(kc == FC - 1)),
                            reads=[self.ringB[slot], self.hBuf[kc][blk]], writes=[pb], inc=(kc == FC - 1))
                    sg = sap(self.arena32, A32, 0, 128, fo * TB, [(1, TB)])
                    if i < 2:
                        K.op("act", lambda e, sg=sg, pt=pt: e.activation(out=sg, in_=pt[:, :], func=AF.Sigmoid), reads=[pb], writes=[sgB[fo]])
                    else:
                        j = fo % 2
                        tt = sap(self.arena32, A32, 0, 128, 4096 + j * TB, [(1, TB)])
                        K.op("dve", lambda e, tt=tt, pt=pt, sg=sg: e.tensor_tensor(out=tt, in0=pt[:, :], in1=sg, op=ALU.mult),
                             reads=[pb, sgB[fo]], writes=[tB_[j]])
                        K.op("pool", lambda e, tt=tt, fo=fo: e.tensor_tensor(out=self.xap(fo, blk), in0=self.xap(fo, blk), in1=tt, op=ALU.add),
                             reads=[tB_[j], self.xBuf[fo][blk]], writes=[self.xBuf[fo][blk]])
            self.stream([("glu_%d" % sl, 2), ("glu_%d" % sl, 3), ("glu_%d" % sl, 0), ("glu_%d" % sl, 1)], cons)

    def layer(self, s, li):
        kind, sl = li % 3, li // 3
        self.norm(0, li)
        if kind == 0:
            self.s5(sl)
        elif kind == 1:
            self.conv()
        else:
            self.fox()
        self.norm(1, li)
        self.mlp(li)
        self.norm(2, li)
        self.ple(s, li)


_CACHE = {}


def _get_prog():
    if "p" not in _CACHE:
        _CACHE["p"] = Prog()
    return _CACHE["p"]


def kernel(**inputs):
    prog = _get_prog()
    names = ["norm_mix", "norm_ffn", "norm_ple", "norm_final", "ssm_lam_re", "ssm_lam_im", "ssm_log_dt", "ssm_b_re",
             "ssm_b_im", "ssm_c_re", "ssm_c_im", "ssm_d", "ssm_w_glu", "conv_w_in", "conv_w", "conv_w_out", "fox_w_in",
             "fox_b_f", "fox_w_out", "mlp_w1", "mlp_w2", "ple_w", "ple_gate_w"]
    shared = {n: np.ascontiguousarray(np.asarray(inputs[n], dtype=np.float32)) for n in names}
    x = np.asarray(inputs["x"], dtype=np.float32)
    p = np.asarray(inputs["p"], dtype=np.float32)
    in_maps = []
    for c in range(NCORES):
        m = dict(shared)
        m["x"] = np.ascontiguousarray(x[c * SEQ_PER_CORE:(c + 1) * SEQ_PER_CORE])
        m["p"] = np.ascontiguousarray(p[:, c * SEQ_PER_CORE:(c + 1) * SEQ_PER_CORE])
        in_maps.append(m)
    res = run_bass_kernel_spmd(prog.nc, in_maps, core_ids=list(range(NCORES)))
    return np.concatenate([np.asarray(r["out"]) for r in res.results], axis=0).astype(np.float32)
```

```python
import math
import numpy as np
import concourse.bass as bass
import concourse.mybir as mybir
from concourse.bass_utils import run_bass_kernel_spmd

F32 = mybir.dt.float32
BF16 = mybir.dt.bfloat16
AF = mybir.ActivationFunctionType
ALU = mybir.AluOpType

D = 1024
L = 2048
FC = 8
NBLK = 4
TB = 512
DFF = 4096
PLE = 256
DEPTH = 4
NCORES = 8
SEQ_PER_CORE = 4
EPS = 1e-6
NDS = 16
KS_STEPS = 11


class Buf:
    __slots__ = ("name", "w", "r")

    def __init__(self, name=""):
        self.name = name
        self.w = None
        self.r = {}


class Trk:
    def __init__(self, nc):
        self.nc = nc
        self.eng = {"pe": nc.tensor, "act": nc.scalar, "dve": nc.vector, "pool": nc.gpsimd, "sp": nc.sync}
        self.sem = {e: nc.alloc_semaphore("sem_" + e) for e in self.eng}
        self.cnt = {e: 0 for e in self.eng}
        self.seen = {e: {} for e in self.eng}
        self.dsem = [nc.alloc_semaphore("dsem%d" % i) for i in range(NDS)]
        self.dval = [0] * NDS
        self.dnext = 0
        self.fdeps = {}
        self.nbank = 0

    def fence(self):
        d = {}
        for e, c in self.cnt.items():
            if c > 0:
                d[e] = c
        for i, v in enumerate(self.dval):
            if v > 0:
                d[i] = v
        self.fdeps = d

    def _wait(self, e, deps):
        for key, val in deps.items():
            if self.seen[e].get(key, 0) >= val:
                continue
            semh = self.sem[key] if isinstance(key, str) else self.dsem[key]
            self.eng[e].wait_ge(semh, val)
            self.seen[e][key] = val

    def _deps(self, e, reads, writes, same_ok, nofence):
        deps = {}

        def add(k, v):
            if k == e and same_ok:
                return
            if deps.get(k, 0) < v:
                deps[k] = v

        for b in reads:
            if b.w:
                add(*b.w)
        for b in writes:
            if b.w:
                add(*b.w)
            for k, v in b.r.items():
                add(k, v)
        if not nofence:
            for k, v in self.fdeps.items():
                add(k, v)
        return deps

    def op(self, e, fn, reads=(), writes=(), inc=True):
        self._wait(e, self._deps(e, reads, writes, e == "pe", False))
        inst = fn(self.eng[e])
        if inc:
            self.cnt[e] += 1
            inst.then_inc(self.sem[e], 1)
            n = self.cnt[e]
        else:
            n = self.cnt[e] + 1
        for b in reads:
            b.r[e] = n
        for b in writes:
            b.w = (e, n)
            b.r = {}

    def dma(self, e, out, in_, reads=(), writes=(), nofence=False, slow=False):
        i = self.dnext
        self.dnext = (self.dnext + 1) % NDS
        deps = self._deps(e, reads, writes, False, nofence)
        if self.dval[i] > 0 and deps.get(i, 0) < self.dval[i]:
            deps[i] = self.dval[i]
        self._wait(e, deps)
        self.dval[i] += 16
        v = self.dval[i]
        if slow:
            self.eng[e].dma_start(out=out, in_=in_, allow_slow_non_contiguous=True).then_inc(self.dsem[i], 16)
        else:
            self.eng[e].dma_start(out=out, in_=in_).then_inc(self.dsem[i], 16)
        for b in reads:
            b.r[i] = v
        for b in writes:
            b.w = (i, v)
            b.r = {}


def sap(t, rs, p0, pn, off, dims):
    return bass.AP(t, p0 * rs + off, [[rs, pn]] + [[a, b] for a, b in dims])


def dap(t, off, dims):
    return bass.AP(t, off, [[a, b] for a, b in dims])


class Prog:
    def __init__(self, nseq=SEQ_PER_CORE, layers=(0, 1, 2, 3), do_final=True, mid_after=None, s5dbg=False, s5main=False):
        self.s5dbg = s5dbg
        self.s5main = s5main
        self.mid_after = mid_after
        self.nseq = nseq
        self.layers = tuple(layers)
        self.do_final = do_final
        nc = bass.Bass("TRN2", target_bir_lowering=False)
        self.nc = nc
        self.K = Trk(nc)
        if getattr(self, "s5dbg", False):
            self.s5debug()
            return
        self.decl()
        self.alloc()
        self.consts()
        self.prologue_weights()
        for sl in range(2):
            if (3 * sl) in self.layers:
                self.s5_prologue(sl)
                self.s5_prologue2(sl)
        self.K.fence()
        self.K._wait("sp", dict(self.K.fdeps))
        for s in range(nseq):
            self.load_x(s)
            for li in self.layers:
                self.layer(s, li)
                if li == self.mid_after:
                    self.store_out(s, self.out_mid, False)
            self.store_out(s, self.out, self.do_final)
        self.finish()

    def s5debug(self):
        nc = self.nc
        K = self.K
        inp = lambda name, shape: nc.dram_tensor(name, list(shape), F32, kind="ExternalInput")
        self.ssm_lam_re = inp("ssm_lam_re", [2, 64, 64])
        self.ssm_lam_im = inp("ssm_lam_im", [2, 64, 64])
        self.ssm_log_dt = inp("ssm_log_dt", [2, 64])
        self.ssm_b_re = inp("ssm_b_re", [2, 64, 64, 16])
        self.ssm_b_im = inp("ssm_b_im", [2, 64, 64, 16])
        self.ssm_c_re = inp("ssm_c_re", [2, 64, 16, 64])
        self.ssm_c_im = inp("ssm_c_im", [2, 64, 16, 64])
        self.wscr = {}
        self.alloc()
        cB = self.cB
        K.op("pool", lambda e: e.memset(self.onesf[:, :], 1.0), writes=[cB])
        K.op("pool", lambda e: e.affine_select(out=self.ident[:, :], in_=self.onesf[:, :], pattern=[[1, 128]],
                                                compare_op=ALU.is_equal, fill=0.0, base=0, channel_multiplier=-1),
             reads=[cB], writes=[cB])
        self.s5_prologue(0)
        self.s5_prologue2(0)
        K.fence()
        K._wait("sp", dict(K.fdeps))
        if getattr(self, "s5main", False):
            u_in = inp("u", [128, FC * L])
            self.ssm_d = inp("ssm_d", [2, D])
            for sl_ in range(2):
                K.dma("sp", sap(self.dT, 2 * FC, 0, 128, sl_ * FC, [(1, FC)]), dap(self.ssm_d, sl_ * D, [(1, 128), (128, FC)]), writes=[cB], slow=True)
            ub = Buf()
            for fc in range(FC):
                K.dma("sp", sap(self.xT, FC * L, 0, 128, fc * L, [(1, L)]), dap(u_in, fc * L, [(FC * L, 128), (1, L)]), writes=[ub])
                K.op("act", lambda e, fc=fc: e.activation(out=self.hap(fc, 0, L), in_=sap(self.xT, FC * L, 0, 128, fc * L, [(1, L)]), func=AF.Copy),
                     reads=[ub], writes=[self.hBuf[fc][b] for b in range(NBLK)])
            self.s5(0, glu=False)
            K.fence()
            d_y = nc.dram_tensor("d_y", [128, FC * L], BF16, kind="ExternalOutput")
            K.dma("sp", d_y.ap(), self.hB[:, :], reads=[self.hBuf[f][b] for f in range(FC) for b in range(NBLK)], writes=[Buf()])
        for nm in ("s5A_0", "s5B_0", "s5C_0"):
            w = self.wscr[nm]
            dd = nc.dram_tensor("d_" + nm, [w["npieces"] * 128, 4096], BF16, kind="ExternalOutput")
            K.dma("sp", dd.ap(), w["t"].ap(), reads=[self.wB(nm)], writes=[Buf()])
        d_ap = nc.dram_tensor("d_apow", [128, 2 * KS_STEPS * 3 * 32], F32, kind="ExternalOutput")
        d_bt = nc.dram_tensor("d_bt", [128, 2 * 2 * FC * 128], BF16, kind="ExternalOutput")
        d_cd = nc.dram_tensor("d_cd", [128, 2 * 2 * 1024], BF16, kind="ExternalOutput")
        d_x = nc.dram_tensor("d_x", [128, 8192], F32, kind="ExternalOutput")
        d_id = nc.dram_tensor("d_id", [128, 128], F32, kind="ExternalOutput")
        ob = Buf()
        K.dma("sp", d_ap.ap(), self.Apow[:, :], writes=[ob])
        K.dma("sp", d_bt.ap(), self.BT[:, :], writes=[ob])
        K.dma("sp", d_cd.ap(), self.Cd[:, :], writes=[ob])
        K.dma("sp", d_x.ap(), sap(self.xT, FC * L, 0, 128, 0, [(1, 8192)]), writes=[ob])
        K.dma("sp", d_id.ap(), self.ident[:, :], writes=[ob])
        self.finish()

    def decl(self):
        nc = self.nc
        S = self.nseq

        def inp(name, shape):
            return nc.dram_tensor(name, list(shape), F32, kind="ExternalInput")

        self.x = inp("x", [S, L, D])
        self.p = inp("p", [DEPTH, S, L, PLE])
        self.norm_mix = inp("norm_mix", [DEPTH, D])
        self.norm_ffn = inp("norm_ffn", [DEPTH, D])
        self.norm_ple = inp("norm_ple", [DEPTH, D])
        self.norm_final = inp("norm_final", [D])
        self.ssm_lam_re = inp("ssm_lam_re", [2, 64, 64])
        self.ssm_lam_im = inp("ssm_lam_im", [2, 64, 64])
        self.ssm_log_dt = inp("ssm_log_dt", [2, 64])
        self.ssm_b_re = inp("ssm_b_re", [2, 64, 64, 16])
        self.ssm_b_im = inp("ssm_b_im", [2, 64, 64, 16])
        self.ssm_c_re = inp("ssm_c_re", [2, 64, 16, 64])
        self.ssm_c_im = inp("ssm_c_im", [2, 64, 16, 64])
        self.ssm_d = inp("ssm_d", [2, D])
        self.ssm_w_glu = inp("ssm_w_glu", [2, D, 2 * D])
        self.conv_w_in = inp("conv_w_in", [1, D, 3 * D])
        self.conv_w = inp("conv_w", [1, 3, D])
        self.conv_w_out = inp("conv_w_out", [1, D, D])
        self.fox_w_in = inp("fox_w_in", [1, D, 3 * D + 16])
        self.fox_b_f = inp("fox_b_f", [1, 16])
        self.fox_w_out = inp("fox_w_out", [1, D, D])
        self.mlp_w1 = inp("mlp_w1", [DEPTH, D, DFF])
        self.mlp_w2 = inp("mlp_w2", [DEPTH, DFF, D])
        self.ple_w = inp("ple_w", [DEPTH, PLE, D])
        self.ple_gate_w = inp("ple_gate_w", [DEPTH, D, D])
        self.out = nc.dram_tensor("out", [S, L, D], F32, kind="ExternalOutput")
        if self.mid_after is not None:
            self.out_mid = nc.dram_tensor("out_mid", [S, L, D], F32, kind="ExternalOutput")
        self.wscr = {}

    def add_w(self, key, src, src_off, ncols_total, nk, cw, col0, ncols):
        npieces = ncols // cw
        pe = nk * cw
        t = self.nc.dram_tensor("ws_" + key, [npieces * 128, pe], BF16, kind="Internal")
        self.wscr[key] = dict(t=t, npieces=npieces, pe=pe, nk=nk, cw=cw, src=src, src_off=src_off,
                              rs=ncols_total, col0=col0, ksplit=False)

    def add_w_ksplit(self, key, src, src_off, ncols_total, nkp, npieces):
        pe = nkp * ncols_total
        t = self.nc.dram_tensor("ws_" + key, [npieces * 128, pe], BF16, kind="Internal")
        self.wscr[key] = dict(t=t, npieces=npieces, pe=pe, nk=nkp, cw=ncols_total, src=src, src_off=src_off,
                              rs=ncols_total, col0=0, ksplit=True)

    def alloc(self):
        nc = self.nc
        self.xT = nc.alloc_sbuf_tensor("xT", [128, FC * L], F32)
        self.hB = nc.alloc_sbuf_tensor("hB", [128, FC * L], BF16)
        self.AR = 24576
        self.arena = nc.alloc_sbuf_tensor("arena", [128, self.AR], BF16)
        self.arena32 = self.arena.bitcast(F32)
        self.ring = nc.alloc_sbuf_tensor("ring", [128, 4 * 4096], BF16)
        self.ring32 = self.ring.bitcast(F32)
        self.hB32 = self.hB.bitcast(F32)
        self.ringB = [Buf("ring%d" % i) for i in range(4)]
        self.ringn = 0
        self.xBuf = [[Buf("x%d_%d" % (f, b)) for b in range(NBLK)] for f in range(FC)]
        self.hBuf = [[Buf("h%d_%d" % (f, b)) for b in range(NBLK)] for f in range(FC)]
        self.ps = [nc.alloc_psum_tensor("ps%d" % i, [128, 512], F32) for i in range(8)]
        self.psB = [Buf("ps%d" % i) for i in range(8)]
        self.gT = nc.alloc_sbuf_tensor("gT", [128, 3 * DEPTH * FC], F32)
        self.gfin = nc.alloc_sbuf_tensor("gfin", [128, D], F32)
        self.cwT = nc.alloc_sbuf_tensor("cwT", [128, 3 * FC], F32)
        self.dT = nc.alloc_sbuf_tensor("dT", [128, 2 * FC], F32)
        self.nbf = nc.alloc_sbuf_tensor("nbf", [128, 1], F32)
        self.ident = nc.alloc_sbuf_tensor("ident", [128, 128], F32)
        self.identb = nc.alloc_sbuf_tensor("identb", [128, 128], BF16)
        self.onesb = nc.alloc_sbuf_tensor("onesb", [128, 128], BF16)
        self.onesf = nc.alloc_sbuf_tensor("onesf", [128, 128], F32)
        self.trib = nc.alloc_sbuf_tensor("trib", [128, 128], BF16)
        self.wfT = nc.alloc_sbuf_tensor("wfT", [128, 8 * 16], BF16)
        self.halo = nc.alloc_sbuf_tensor("halo", [128, FC * 2], F32)
        self.BT = nc.alloc_sbuf_tensor("BT", [128, 2 * 2 * FC * 128], BF16)
        self.Cd = nc.alloc_sbuf_tensor("Cd", [128, 2 * 2 * 32 * 32], BF16)
        self.Apow = nc.alloc_sbuf_tensor("Apow", [128, 2 * KS_STEPS * 3 * 32], F32)
        self.cB = Buf("consts")

    def bank(self):
        bs = getattr(self, "bank_set", None) or list(range(8))
        self.K.nbank = (self.K.nbank + 1) % len(bs)
        i = bs[self.K.nbank]
        return self.ps[i], self.psB[i]

    def fbank(self, i):
        return self.ps[i], self.psB[i]

    def consts(self):
        K = self.K
        cB = self.cB
        for k, t in enumerate((self.norm_mix, self.norm_ffn, self.norm_ple)):
            for li in range(DEPTH):
                K.dma("sp", sap(self.gT, 3 * DEPTH * FC, 0, 128, (k * DEPTH + li) * FC, [(1, FC)]),
                      dap(t, li * D, [(1, 128), (128, FC)]), writes=[cB], slow=True)
        K.dma("sp", sap(self.gfin, D, 0, 128, 0, [(1, D)]), dap(self.norm_final, 0, [(0, 128), (1, D)]), writes=[cB])
        for tap in range(3):
            K.dma("sp", sap(self.cwT, 3 * FC, 0, 128, tap * FC, [(1, FC)]),
                  dap(self.conv_w, tap * D, [(1, 128), (128, FC)]), writes=[cB], slow=True)
        for sl in range(2):
            K.dma("sp", sap(self.dT, 2 * FC, 0, 128, sl * FC, [(1, FC)]),
                  dap(self.ssm_d, sl * D, [(1, 128), (128, FC)]), writes=[cB], slow=True)
        K.dma("sp", sap(self.nbf, 1, 0, 16, 0, [(1, 1)]), dap(self.fox_b_f, 0, [(1, 16), (1, 1)]), writes=[cB], slow=True)
        K.op("dve", lambda e: e.tensor_scalar(out=sap(self.nbf, 1, 0, 16, 0, [(1, 1)]), in0=sap(self.nbf, 1, 0, 16, 0, [(1, 1)]),
                                               scalar1=-1.0, scalar2=None, op0=ALU.mult), reads=[cB], writes=[cB])
        K.op("pool", lambda e: e.memset(self.onesf[:, :], 1.0), writes=[cB])
        K.op("pool", lambda e: e.memset(self.onesb[:, :], 1.0), writes=[cB])
        K.op("pool", lambda e: e.affine_select(out=self.ident[:, :], in_=self.onesf[:, :], pattern=[[1, 128]],
                                                compare_op=ALU.is_equal, fill=0.0, base=0, channel_multiplier=-1),
             reads=[cB], writes=[cB])
        K.op("pool", lambda e: e.affine_select(out=self.identb[:, :], in_=self.onesb[:, :], pattern=[[1, 128]],
                                                compare_op=ALU.is_equal, fill=0.0, base=0, channel_multiplier=-1),
             reads=[cB], writes=[cB])
        K.op("pool", lambda e: e.affine_select(out=self.trib[:, :], in_=self.onesb[:, :], pattern=[[1, 128]],
                                                compare_op=ALU.is_ge, fill=0.0, base=0, channel_multiplier=-1),
             reads=[cB], writes=[cB])
        K.op("pool", lambda e: e.memset(self.halo[:, :], 0.0), writes=[cB])

    def prologue_weights(self):
        K = self.K
        for li in self.layers:
            self.add_w("w1_%d" % li, self.mlp_w1, li * D * DFF, DFF, 8, 512, 0, DFF)
            self.add_w("w2_%d" % li, self.mlp_w2, li * DFF * D, D, 32, 128, 0, D)
            self.add_w("gate_%d" % li, self.ple_gate_w, li * D * D, D, 8, 512, 0, D)
            self.add_w("ple_%d" % li, self.ple_w, li * PLE * D, D, 2, 1024, 0, D)
            kind, sl = li % 3, li // 3
            if kind == 0:
                self.add_w("glu_%d" % sl, self.ssm_w_glu, sl * D * 2 * D, 2 * D, 8, 512, 0, 2 * D)
            elif kind == 1:
                self.add_w("cin", self.conv_w_in, 0, 3 * D, 8, 512, 0, 3 * D)
                self.add_w("cout", self.conv_w_out, 0, D, 8, 512, 0, D)
            else:
                self.add_w("fin", self.fox_w_in, 0, 3 * D + 16, 8, 256, 0, 3 * D)
                self.add_w_ksplit("fout", self.fox_w_out, 0, D, 2, 4)
        stB = [Buf("stg0"), Buf("stg1")]
        cvB = [Buf("cv0"), Buf("cv1")]
        n = 0
        for key, w in self.wscr.items():
            for pc in range(w["npieces"]):
                j = n % 2
                nk, cw, rs = w["nk"], w["cw"], w["rs"]
                if w["ksplit"]:
                    src = dap(w["src"], w["src_off"] + pc * nk * 128 * rs, [(rs, 128), (128 * rs, nk), (1, cw)])
                else:
                    src = dap(w["src"], w["src_off"] + w["col0"] + pc * cw, [(rs, 128), (128 * rs, nk), (1, cw)])
                stg = sap(self.xT, FC * L, 0, 128, j * 4096, [(cw, nk), (1, cw)])
                K.dma("sp", stg, src, writes=[stB[j]])
                cv = sap(self.hB, FC * L, 0, 128, j * 4096, [(1, nk * cw)])
                stg_flat = sap(self.xT, FC * L, 0, 128, j * 4096, [(1, nk * cw)])
                eng = ("act", "dve", "pool")[n % 3]
                if eng == "act":
                    K.op("act", lambda e, o=cv, i=stg_flat: e.activation(out=o, in_=i, func=AF.Copy), reads=[stB[j]], writes=[cvB[j]])
                else:
                    K.op(eng, lambda e, o=cv, i=stg_flat: e.tensor_copy(out=o, in_=i), reads=[stB[j]], writes=[cvB[j]])
                dst = dap(w["t"], pc * 128 * w["pe"], [(w["pe"], 128), (1, w["pe"])])
                K.dma("sp", dst, cv, reads=[cvB[j]], writes=[self.wB(key)])
                n += 1
        if 2 in self.layers:
            stg = sap(self.xT, FC * L, 0, 128, 2 * 4096, [(16, 8), (1, 16)])
            K.dma("sp", stg, dap(self.fox_w_in, 3 * D, [(3 * D + 16, 128), (128 * (3 * D + 16), 8), (1, 16)]), writes=[stB[0]], slow=True)
            K.op("dve", lambda e: e.tensor_copy(out=self.wfT[:, :], in_=sap(self.xT, FC * L, 0, 128, 2 * 4096, [(1, 128)])),
                 reads=[stB[0]], writes=[self.cB])

    def wB(self, key):
        w = self.wscr[key]
        if "B" not in w:
            w["B"] = Buf("w_" + key)
        return w["B"]

    def ring_load(self, key, pc):
        K = self.K
        w = self.wscr[key]
        i = self.ringn
        self.ringn = (i + 1) % 4
        pe = w["pe"]
        K.dma("sp", sap(self.ring, 4 * 4096, 0, 128, i * 4096, [(1, pe)]),
              dap(w["t"], pc * 128 * pe, [(pe, 128), (1, pe)]), reads=[self.wB(key)], writes=[self.ringB[i]], nofence=True)
        return i

    def stream(self, plist, consume, depth=2):
        n = len(plist)
        slots = {}
        for i in range(min(depth, n)):
            slots[i] = self.ring_load(*plist[i])
        for i in range(n):
            consume(i, slots.pop(i))
            if i + depth < n:
                slots[i + depth] = self.ring_load(*plist[i + depth])

    def rslot(self, i, off, dims, p0=0, pn=128):
        return sap(self.ring, 4 * 4096, p0, pn, i * 4096 + off, dims)

    def xap(self, fc, blk, p0=0, pn=128, n=TB, off=0):
        return sap(self.xT, FC * L, p0, pn, fc * L + blk * TB + off, [(1, n)])

    def hap(self, fc, tok0, n, p0=0, pn=128):
        return sap(self.hB, FC * L, p0, pn, fc * L + tok0, [(1, n)])

    def load_x(self, s):
        K = self.K
        K.fence()
        stB = [Buf(), Buf()]
        for tt in range(16):
            j = tt % 2
            stg = sap(self.arena32, self.AR // 2, 0, 128, j * 1024, [(1, 1024)])
            K.dma("sp", stg, dap(self.x, (s * L + tt * 128) * D, [(D, 128), (1, D)]), writes=[stB[j]])
            for half in range(2):
                pt, pb = self.bank()
                for q in range(4):
                    fc = half * 4 + q
                    K.op("pe", lambda e, pt=pt, q=q, fc=fc, j=j: e.transpose(
                        out=sap(pt, 512, 0, 128, q * 128, [(1, 128)]),
                        in_=sap(self.arena32, self.AR // 2, 0, 128, j * 1024 + fc * 128, [(1, 128)]),
                        identity=self.ident[:, :]), reads=[stB[j], self.cB], writes=[pb], inc=(q == 3))
                dst = sap(self.xT, FC * L, 0, 128, half * 4 * L + tt * 128, [(L, 4), (1, 128)])
                src = sap(pt, 512, 0, 128, 0, [(128, 4), (1, 128)])
                wr = [self.xBuf[half * 4 + q][tt // 4] for q in range(4)]
                if (tt + half) % 2 == 0:
                    K.op("act", lambda e, dst=dst, src=src: e.activation(out=dst, in_=src, func=AF.Copy), reads=[pb], writes=wr)
                else:
                    K.op("dve", lambda e, dst=dst, src=src: e.tensor_copy(out=dst, in_=src), reads=[pb], writes=wr)

    def store_out(self, s, target, final):
        K = self.K
        K.fence()
        oB = [Buf(), Buf()]
        ssB = [Buf(), Buf()]
        for tt in range(16):
            j = tt % 2
            blk = tt // 4
            ot = sap(self.arena32, self.AR // 2, 0, 128, j * 1024, [(1, 1024)])
            for half in range(2):
                pt, pb = self.bank()
                for q in range(4):
                    fc = half * 4 + q
                    K.op("pe", lambda e, pt=pt, q=q, fc=fc: e.transpose(
                        out=sap(pt, 512, 0, 128, q * 128, [(1, 128)]),
                        in_=sap(self.xT, FC * L, 0, 128, fc * L + tt * 128, [(1, 128)]),
                        identity=self.ident[:, :]), reads=[self.xBuf[fc][blk], self.cB], writes=[pb], inc=(q == 3))
                dst = sap(self.arena32, self.AR // 2, 0, 128, j * 1024 + half * 512, [(1, 512)])
                if final:
                    K.op("dve", lambda e, dst=dst, pt=pt: e.tensor_copy(out=dst, in_=pt[:, :]), reads=[pb], writes=[oB[j]])
                else:
                    K.op("act", lambda e, dst=dst, pt=pt: e.activation(out=dst, in_=pt[:, :], func=AF.Copy), reads=[pb], writes=[oB[j]])
            if final:
                sq = sap(self.arena32, self.AR // 2, 0, 128, 2048 + j * 1024, [(1, 1024)])
                ss = sap(self.arena32, self.AR // 2, 0, 128, 4096 + j, [(1, 1)])
                K.op("act", lambda e, sq=sq, ot=ot, ss=ss: e.activation(out=sq, in_=ot, func=AF.Square, accum_out=ss),
                     reads=[oB[j]], writes=[ssB[j]])
                K.op("act", lambda e, ss=ss: e.activation(out=ss, in_=ss, func=AF.Sqrt, scale=1.0 / D, bias=EPS), reads=[ssB[j]], writes=[ssB[j]])
                K.op("dve", lambda e, ss=ss: e.reciprocal(out=ss, in_=ss), reads=[ssB[j]], writes=[ssB[j]])
                K.op("dve", lambda e, ot=ot, ss=ss: e.scalar_tensor_tensor(out=ot, in0=ot, scalar=ss, in1=self.gfin[:, :],
                                                                            op0=ALU.mult, op1=ALU.mult),
                     reads=[oB[j], ssB[j], self.cB], writes=[oB[j]])
            K.dma("sp", dap(target, (s * L + tt * 128) * D, [(D, 128), (1, D)]), ot, reads=[oB[j]], writes=[self.outB()])

    def outB(self):
        if not hasattr(self, "_outB"):
            self._outB = Buf("out")
        return self._outB

    def finish(self):
        K = self.K
        deps = {i: v for i, v in enumerate(K.dval) if v > 0}
        K._wait("sp", deps)

    def norm(self, kind, li):
        K = self.K
        K.fence()
        self.bank_set = None
        sqB = Buf()
        rB = [Buf(), Buf()]
        goff = (kind * DEPTH + li) * FC
        for blk in range(NBLK):
            sq = sap(self.arena, self.AR, 0, 128, 0, [(TB, FC), (1, TB)])
            xin = sap(self.xT, FC * L, 0, 128, blk * TB, [(L, FC), (1, TB)])
            K.op("act", lambda e, sq=sq, xin=xin: e.activation(out=sq, in_=xin, func=AF.Square),
                 reads=[self.xBuf[f][blk] for f in range(FC)], writes=[sqB])
            pt, pb = self.bank()
            for fc in range(FC):
                K.op("pe", lambda e, pt=pt, fc=fc: e.matmul(pt[:, :], lhsT=self.onesb[:, :],
                                                            rhs=sap(self.arena, self.AR, 0, 128, fc * TB, [(1, TB)]),
                                                            start=(fc == 0), stop=(fc == FC - 1)),
                     reads=[sqB, self.cB], writes=[pb], inc=(fc == FC - 1))
            j = blk % 2
            rs = sap(self.arena32, self.AR // 2, 0, 128, 2048 + j * TB, [(1, TB)])
            K.op("act", lambda e, rs=rs, pt=pt: e.activation(out=rs, in_=pt[:, :], func=AF.Sqrt, scale=1.0 / D, bias=EPS),
                 reads=[pb], writes=[rB[j]])
            K.op("dve", lambda e, rs=rs: e.reciprocal(out=rs, in_=rs), reads=[rB[j]], writes=[rB[j]])
            for fc in range(FC):
                K.op("dve", lambda e, fc=fc, rs=rs, blk=blk: e.scalar_tensor_tensor(
                    out=self.hap(fc, blk * TB, TB), in0=self.xap(fc, blk),
                    scalar=sap(self.gT, 3 * DEPTH * FC, 0, 128, goff + fc, [(1, 1)]), in1=rs, op0=ALU.mult, op1=ALU.mult),
                     reads=[self.xBuf[fc][blk], rB[j], self.cB], writes=[self.hBuf[fc][blk]])

    def mlp(self, li):
        K = self.K
        K.fence()
        aB = [Buf() for _ in range(32)]
        for blk in range(NBLK):
            def c1(i, slot, blk=blk):
                for mm in range(4):
                    m = i * 4 + mm
                    pt, pb = self.bank()
                    for kc in range(FC):
                        K.op("pe", lambda e, pt=pt, kc=kc, mm=mm: e.matmul(
                            pt[:, :], lhsT=self.rslot(slot, kc * 512 + mm * 128, [(1, 128)]),
                            rhs=self.hap(kc, blk * TB, TB), start=(kc == 0), stop=(kc == FC - 1)),
                            reads=[self.ringB[slot], self.hBuf[kc][blk]], writes=[pb], inc=(kc == FC - 1))
                    a = sap(self.arena, self.AR, 0, 128, m * TB, [(1, TB)])
                    K.op("act", lambda e, a=a, pt=pt: e.activation(out=a, in_=pt[:, :], func=AF.Relu), reads=[pb], writes=[aB[m]])
                    eng = "pool" if m % 2 == 0 else "dve"
                    K.op(eng, lambda e, a=a: e.tensor_tensor(out=a, in0=a, in1=a, op=ALU.mult), reads=[aB[m]], writes=[aB[m]])
            self.stream([("w1_%d" % li, pc) for pc in range(8)], c1)

            def c2(i, slot, blk=blk):
                fo = i
                pt, pb = self.bank()
                for mc in range(32):
                    K.op("pe", lambda e, pt=pt, mc=mc: e.matmul(
                        pt[:, :], lhsT=self.rslot(slot, mc * 128, [(1, 128)]),
                        rhs=sap(self.arena, self.AR, 0, 128, mc * TB, [(1, TB)]), start=(mc == 0), stop=(mc == 31)),
                        reads=[self.ringB[slot], aB[mc]], writes=[pb], inc=(mc == 31))
                K.op("dve", lambda e, pt=pt, fo=fo: e.tensor_tensor(out=self.xap(fo, blk), in0=pt[:, :], in1=self.xap(fo, blk), op=ALU.add),
                     reads=[pb, self.xBuf[fo][blk]], writes=[self.xBuf[fo][blk]])
            self.stream([("w2_%d" % li, pc) for pc in range(8)], c2)

    def ple(self, s, li):
        K = self.K
        K.fence()
        A32 = self.AR // 2
        pstB = [Buf(), Buf()]
        pTB = Buf()
        gB = [Buf(), Buf()]
        tB = [Buf(), Buf()]
        for blk in range(NBLK):
            pts = [self.bank(), self.bank()]
            for t4 in range(4):
                j = t4 % 2
                stg = sap(self.arena32, A32, 0, 128, 1024 + j * 256, [(1, 256)])
                K.dma("sp", stg, dap(self.p, ((li * self.nseq + s) * L + blk * TB + t4 * 128) * PLE, [(PLE, 128), (1, PLE)]),
                      writes=[pstB[j]])
                for kc in range(2):
                    pt, pb = pts[kc]
                    K.op("pe", lambda e, pt=pt, kc=kc, j=j, t4=t4: e.transpose(
                        out=sap(pt, 512, 0, 128, t4 * 128, [(1, 128)]),
                        in_=sap(self.arena32, A32, 0, 128, 1024 + j * 256 + kc * 128, [(1, 128)]),
                        identity=self.ident[:, :]), reads=[pstB[j], self.cB], writes=[pb])
            for kc in range(2):
                pt, pb = pts[kc]
                K.op("act", lambda e, pt=pt, kc=kc: e.activation(out=sap(self.arena, self.AR, 0, 128, kc * TB, [(1, TB)]),
                                                                in_=pt[:, :], func=AF.Copy), reads=[pb], writes=[pTB])

            def cons(i, slot, blk=blk):
                if i < 2:
                    for q in range(4):
                        fo = i * 4 + q
                        pt, pb = self.bank()
                        for kc in range(FC):
                            K.op("pe", lambda e, pt=pt, kc=kc, q=q: e.matmul(
                                pt[:, :], lhsT=self.rslot(slot, kc * 512 + q * 128, [(1, 128)]),
                                rhs=self.hap(kc, blk * TB, TB), start=(kc == 0), stop=(kc == FC - 1)),
                                reads=[self.ringB[slot], self.hBuf[kc][blk]], writes=[pb], inc=(kc == FC - 1))
                        g = sap(self.arena32, A32, 0, 128, 2048 + fo * TB, [(1, TB)])
                        K.op("act", lambda e, g=g, pt=pt: e.activation(out=g, in_=pt[:, :], func=AF.Sigmoid), reads=[pb], writes=[self._gB[fo]])
                else:
                    for fo in range(FC):
                        pt, pb = self.bank()
                        for kc in range(2):
                            K.op("pe", lambda e, pt=pt, kc=kc, fo=fo: e.matmul(
                                pt[:, :], lhsT=self.rslot(slot, kc * 1024 + fo * 128, [(1, 128)]),
                                rhs=sap(self.arena, self.AR, 0, 128, kc * TB, [(1, TB)]), start=(kc == 0), stop=(kc == 1)),
                                reads=[self.ringB[slot], pTB], writes=[pb], inc=(kc == 1))
                        g = sap(self.arena32, A32, 0, 128, 2048 + fo * TB, [(1, TB)])
                        j = fo % 2
                        t = sap(self.arena32, A32, 0, 128, 6144 + j * TB, [(1, TB)])
                        K.op("dve", lambda e, t=t, pt=pt, g=g: e.tensor_tensor(out=t, in0=pt[:, :], in1=g, op=ALU.mult),
                             reads=[pb, self._gB[fo]], writes=[tB[j]])
                        K.op("pool", lambda e, t=t, fo=fo: e.tensor_tensor(out=self.xap(fo, blk), in0=self.xap(fo, blk), in1=t, op=ALU.add),
                             reads=[tB[j], self.xBuf[fo][blk]], writes=[self.xBuf[fo][blk]])
            self._gB = [Buf() for _ in range(FC)]
            self.stream([("gate_%d" % li, 0), ("gate_%d" % li, 1), ("ple_%d" % li, 0)], cons, depth=2)

    def conv(self):
        K = self.K
        K.fence()
        A32 = self.AR // 2
        ZS, CS, CV = 0, 4112, 6160
        GB = 14368
        zB = [Buf() for _ in range(FC)]
        cSB = [Buf() for _ in range(4)]
        cvB = [Buf(), Buf()]
        gB = [Buf() for _ in range(FC)]
        hlB = Buf()
        for blk in range(NBLK):
            if blk == 0:
                K.op("pool", lambda e: e.memset(sap(self.arena32, A32, 0, 128, ZS, [(514, FC), (1, 2)]), 0.0), writes=zB)
            else:
                K.op("pool", lambda e: e.tensor_copy(out=sap(self.arena32, A32, 0, 128, ZS, [(514, FC), (1, 2)]),
                                                      in_=sap(self.halo, FC * 2, 0, 128, 0, [(2, FC), (1, 2)])), reads=[hlB], writes=zB)

            def cons(i, slot, blk=blk):
                half = i // 3
                kind = i % 3
                for q in range(4):
                    fc = half * 4 + q
                    pt, pb = self.bank()
                    for kc in range(FC):
                        K.op("pe", lambda e, pt=pt, kc=kc, q=q: e.matmul(
                            pt[:, :], lhsT=self.rslot(slot, kc * 512 + q * 128, [(1, 128)]),
                            rhs=self.hap(kc, blk * TB, TB), start=(kc == 0), stop=(kc == FC - 1)),
                            reads=[self.ringB[slot], self.hBuf[kc][blk]], writes=[pb], inc=(kc == FC - 1))
                    z = sap(self.arena32, A32, 0, 128, ZS + fc * 514 + 2, [(1, TB)])
                    if kind == 0:
                        c = sap(self.arena32, A32, 0, 128, CS + q * TB, [(1, TB)])
                        K.op("act", lambda e, c=c, pt=pt: e.activation(out=c, in_=pt[:, :], func=AF.Copy), reads=[pb], writes=[cSB[q]])
                    elif kind == 1:
                        c = sap(self.arena32, A32, 0, 128, CS + q * TB, [(1, TB)])
                        K.op("dve", lambda e, z=z, pt=pt, c=c: e.tensor_tensor(out=z, in0=pt[:, :], in1=c, op=ALU.mult),
                             reads=[pb, cSB[q]], writes=[zB[fc]])
                    else:
                        j = q % 2
                        cv = sap(self.arena32, A32, 0, 128, CV + j * TB, [(1, TB)])
                        w = lambda tap, fc=fc: sap(self.cwT, 3 * FC, 0, 128, tap * FC + fc, [(1, 1)])
                        K.op("act", lambda e, cv=cv, z=z, w=w: e.activation(out=cv, in_=z, func=AF.Copy, scale=w(2)),
                             reads=[zB[fc], self.cB], writes=[cvB[j]])
                        z1 = sap(self.arena32, A32, 0, 128, ZS + fc * 514 + 1, [(1, TB)])
                        z0 = sap(self.arena32, A32, 0, 128, ZS + fc * 514 + 0, [(1, TB)])
                        K.op("dve", lambda e, cv=cv, z1=z1, w=w: e.scalar_tensor_tensor(out=cv, in0=z1, scalar=w(1), in1=cv, op0=ALU.mult, op1=ALU.add),
                             reads=[zB[fc], cvB[j], self.cB], writes=[cvB[j]])
                        K.op("dve", lambda e, cv=cv, z0=z0, w=w: e.scalar_tensor_tensor(out=cv, in0=z0, scalar=w(0), in1=cv, op0=ALU.mult, op1=ALU.add),
                             reads=[zB[fc], cvB[j], self.cB], writes=[cvB[j]])
                        g = sap(self.arena, self.AR, 0, 128, GB + fc * TB, [(1, TB)])
                        K.op("dve", lambda e, g=g, pt=pt, cv=cv: e.tensor_tensor(out=g, in0=pt[:, :], in1=cv, op=ALU.mult),
                             reads=[pb, cvB[j]], writes=[gB[fc]])
            self.stream([("cin", 2), ("cin", 4), ("cin", 0), ("cin", 3), ("cin", 5), ("cin", 1)], cons)
            K.op("pool", lambda e: e.tensor_copy(out=sap(self.halo, FC * 2, 0, 128, 0, [(2, FC), (1, 2)]),
                                                  in_=sap(self.arena32, A32, 0, 128, ZS + 512, [(514, FC), (1, 2)])), reads=zB, writes=[hlB])

            def cons2(i, slot, blk=blk):
                for q in range(4):
                    fo = i * 4 + q
                    pt, pb = self.bank()
                    for kc in range(FC):
                        K.op("pe", lambda e, pt=pt, kc=kc, q=q: e.matmul(
                            pt[:, :], lhsT=self.rslot(slot, kc * 512 + q * 128, [(1, 128)]),
                            rhs=sap(self.arena, self.AR, 0, 128, GB + kc * TB, [(1, TB)]), start=(kc == 0), stop=(kc == FC - 1)),
                            reads=[self.ringB[slot], gB[kc]], writes=[pb], inc=(kc == FC - 1))
                    K.op("dve", lambda e, pt=pt, fo=fo: e.tensor_tensor(out=self.xap(fo, blk), in0=pt[:, :], in1=self.xap(fo, blk), op=ALU.add),
                         reads=[pb, self.xBuf[fo][blk]], writes=[self.xBuf[fo][blk]])
            self.stream([("cout", 0), ("cout", 1)], cons2)

    def fox(self):
        K = self.K
        K.fence()
        A32 = self.AR // 2
        AR = self.AR
        QT, KT, VV, OG, PT = 0, 4096, 8192, 12352, 14400
        CUM, CTM, BIA, OSB, RRW = 8192, 10240, 10496, 11520, 11776
        cumB, ctmB, biaB = Buf(), Buf(), Buf()
        allh = lambda blk: [self.hBuf[f][blk] for f in range(FC)]
        cum = lambda c0, n: sap(self.arena32, A32, 0, 16, CUM + c0, [(1, n)])
        for blk in range(NBLK):
            pt, pb = self.bank()
            for kc in range(FC):
                K.op("pe", lambda e, pt=pt, kc=kc, blk=blk: e.matmul(
                    sap(pt, 512, 0, 16, 0, [(1, TB)]), lhsT=sap(self.wfT, 128, 0, 128, kc * 16, [(1, 16)]),
                    rhs=self.hap(kc, blk * TB, TB), start=(kc == 0), stop=(kc == FC - 1)),
                    reads=[self.cB, self.hBuf[kc][blk]], writes=[pb], inc=(kc == FC - 1))
            K.op("act", lambda e, pt=pt, blk=blk: e.activation(out=cum(blk * TB, TB), in_=sap(pt, 512, 0, 16, 0, [(1, TB)]),
                                                              func=AF.Exp, scale=-1.0, bias=sap(self.nbf, 1, 0, 16, 0, [(1, 1)])),
                 reads=[pb, self.cB], writes=[cumB])
        K.op("act", lambda e: e.activation(out=cum(0, L), in_=cum(0, L), func=AF.Ln, scale=1.0, bias=1.0), reads=[cumB], writes=[cumB])
        K.op("dve", lambda e: e.tensor_scalar(out=cum(0, L), in0=cum(0, L), scalar1=-1.0, scalar2=None, op0=ALU.mult), reads=[cumB], writes=[cumB])
        K.op("dve", lambda e: e.tensor_tensor_scan(out=cum(0, L), data0=sap(self.onesf, 128, 0, 16, 0, [(0, L)]), data1=cum(0, L),
                                                   initial=0.0, op0=ALU.mult, op1=ALU.add), reads=[cumB, self.cB], writes=[cumB])
        pt, pb = self.bank()
        for tt in range(16):
            K.op("pe", lambda e, pt=pt, tt=tt: e.transpose(out=sap(pt, 512, 0, 128, tt * 16, [(1, 16)]), in_=cum(tt * 128, 128),
                                                          identity=sap(self.ident, 128, 0, 16, 0, [(1, 16)])),
                 reads=[cumB, self.cB], writes=[pb], inc=(tt == 15))
        K.op("dve", lambda e, pt=pt: e.tensor_copy(out=sap(self.arena32, A32, 0, 128, CTM, [(1, 256)]), in_=sap(pt, 512, 0, 128, 0, [(1, 256)])),
             reads=[pb], writes=[ctmB])
        rhsD = sap(self.arena32, A32, 0, 16, OSB, [(1, 256)])
        K.op("dve", lambda e: e.tensor_tensor(out=sap(self.arena32, A32, 0, 16, OSB, [(16, 16), (1, 16)]),
                                              in0=sap(self.ident, 128, 0, 16, 0, [(0, 16), (1, 16)]),
                                              in1=sap(self.arena32, A32, 0, 16, CUM, [(128, 16), (0, 16)]), op=ALU.mult),
             reads=[cumB, self.cB], writes=[biaB])
        ptc, pbc = self.bank()
        K.op("pe", lambda e: e.matmul(sap(ptc, 512, 0, 128, 0, [(1, 256)]), lhsT=sap(self.onesf, 128, 0, 16, 0, [(1, 128)]), rhs=rhsD,
                                      start=True, stop=True), reads=[biaB, self.cB], writes=[pbc])
        BIH = CUM
        for hf in range(8):
            K.op("dve", lambda e, hf=hf: e.tensor_tensor(out=sap(self.arena32, A32, 0, 128, BIH + hf * 256, [(16, 16), (1, 16)]),
                                                         in0=sap(ptc, 512, 0, 128, (2 * hf + 1) * 16, [(0, 16), (1, 16)]),
                                                         in1=sap(self.arena32, A32, 0, 128, CTM, [(16, 16), (1, 16)]), op=ALU.subtract),
                 reads=[pbc, ctmB, cumB], writes=[biaB, cumB])
        qB, kB, vB = Buf(), Buf(), Buf()
        pTB = [Buf(), Buf(), Buf()]
        oGB = [Buf(), Buf()]
        osB, rrB = Buf(), Buf()
        npt = 0
        for hg in range(4):
            K.op("pool", lambda e: e.memset(sap(self.arena, AR, 0, 128, VV + 64, [(65, 64), (1, 1)]), 1.0), reads=[vB], writes=[vB])

            def consqkv(i, slot, hg=hg):
                if i < 2:
                    base, bb = (QT, qB) if i == 0 else (KT, kB)
                    for blk in range(NBLK):
                        for c2 in range(2):
                            pt, pb = self.bank()
                            for kc in range(FC):
                                K.op("pe", lambda e, pt=pt, kc=kc, c2=c2, blk=blk: e.matmul(
                                    pt[:, :], lhsT=self.rslot(slot, kc * 256 + c2 * 128, [(1, 128)]),
                                    rhs=self.hap(kc, blk * TB, TB), start=(kc == 0), stop=(kc == FC - 1)),
                                    reads=[self.ringB[slot], self.hBuf[kc][blk]], writes=[pb], inc=(kc == FC - 1))
                            dst = sap(self.arena, AR, 0, 128, base + c2 * L + blk * TB, [(1, TB)])
                            if (blk + c2) % 2 == 0:
                                K.op("act", lambda e, dst=dst, pt=pt: e.activation(out=dst, in_=pt[:, :], func=AF.Copy), reads=[pb], writes=[bb])
                            else:
                                K.op("dve", lambda e, dst=dst, pt=pt: e.tensor_copy(out=dst, in_=pt[:, :]), reads=[pb], writes=[bb])
                else:
                    for tt in range(16):
                        pt, pb = self.bank()
                        for kc in range(FC):
                            K.op("pe", lambda e, pt=pt, kc=kc, tt=tt: e.matmul(
                                sap(pt, 512, 0, 128, 0, [(1, 256)]), lhsT=self.hap(kc, tt * 128, 128),
                                rhs=self.rslot(slot, kc * 256, [(1, 256)]), start=(kc == 0), stop=(kc == FC - 1)),
                                reads=[self.ringB[slot], self.hBuf[kc][tt // 4]], writes=[pb], inc=(kc == FC - 1))
                        dst = sap(self.arena, AR, 0, 128, VV + tt * 260, [(65, 4), (1, 64)])
                        src = sap(pt, 512, 0, 128, 0, [(64, 4), (1, 64)])
                        if tt % 2 == 0:
                            K.op("act", lambda e, dst=dst, src=src: e.activation(out=dst, in_=src, func=AF.Copy), reads=[pb], writes=[vB])
                        else:
                            K.op("dve", lambda e, dst=dst, src=src: e.tensor_copy(out=dst, in_=src), reads=[pb], writes=[vB])
            self.stream([("fin", hg), ("fin", 4 + hg), ("fin", 8 + hg)], consqkv)
            wslot = self.ring_load("fout", hg)
            self.bank_set = [0, 1, 2, 3, 4]
            nhead = 0
            for qb in range(4):
                og = OG + (qb % 2) * 1024
                for h4 in range(4):
                    c2, ph = h4 // 2, 64 * (h4 % 2)
                    h = hg * 4 + h4
                    po, pob = self.fbank(6 + nhead % 2)
                    nhead += 1
                    nj = 4 * qb + 4
                    for j in range(nj):
                        r = j - 4 * qb
                        c0 = 128 * r if r > 0 else 0
                        n = TB - c0
                        pst, psb = self.bank()
                        K.op("pe", lambda e, pst=pst, j=j, c0=c0, n=n, c2=c2, ph=ph, qb=qb: e.matmul(
                            sap(pst, 512, 0, 128, c0, [(1, n)]),
                            lhsT=sap(self.arena, AR, ph, 64, KT + c2 * L + j * 128, [(1, 128)]),
                            rhs=sap(self.arena, AR, ph, 64, QT + c2 * L + qb * TB + c0, [(1, n)]), start=True, stop=True),
                            reads=[qB, kB], writes=[psb])
                        pj = npt % 3
                        npt += 1
                        pT = sap(self.arena, AR, 0, 128, PT + pj * TB + c0, [(1, n)])
                        for hq in range(2):
                            lo = max(c0, 256 * hq)
                            hi = 256 * hq + 256
                            if lo >= hi:
                                continue
                            sub = sap(self.arena, AR, 0, 128, PT + pj * TB + lo, [(1, hi - lo)])
                            K.op("act", lambda e, sub=sub, pst=pst, lo=lo, hi=hi, hq=hq, qb=qb, j=j, h=h: e.activation(
                                out=sub, in_=sap(pst, 512, 0, 128, lo, [(1, hi - lo)]), func=AF.Exp, scale=0.125,
                                bias=sap(self.arena32, A32, 0, 128, BIH + (2 * qb + hq) * 256 + j * 16 + h, [(1, 1)])),
                                reads=[psb, biaB], writes=[pTB[pj]])
                        if r >= 0:
                            pd = sap(self.arena, AR, 0, 128, PT + pj * TB + 128 * r, [(1, 128)])
                            K.op("pool", lambda e, pd=pd: e.tensor_tensor(out=pd, in0=pd, in1=self.trib[:, :], op=ALU.mult),
                                 reads=[pTB[pj], self.cB], writes=[pTB[pj]])
                        K.op("pe", lambda e, po=po, pT=pT, j=j, c0=c0, n=n, h4=h4, nj=nj: e.matmul(
                            sap(po, 512, 0, 65, c0, [(1, n)]),
                            lhsT=sap(self.arena, AR, 0, 128, VV + j * 260 + h4 * 65, [(1, 65)]), rhs=pT,
                            start=(j == 0), stop=(j == nj - 1)), reads=[vB, pTB[pj]], writes=[pob], inc=(j == nj - 1))
                    rr = sap(self.arena32, A32, 64, 1, RRW, [(1, TB)])
                    K.op("dve", lambda e, rr=rr, po=po: e.reciprocal(out=rr, in_=sap(po, 512, 64, 1, 0, [(1, TB)])), reads=[pob], writes=[rrB])
                    osb = sap(self.arena32, A32, 0, 64, OSB, [(1, TB)])
                    K.op("act", lambda e, osb=osb, po=po: e.activation(out=osb, in_=sap(po, 512, 0, 64, 0, [(1, TB)]), func=AF.Copy),
                         reads=[pob, biaB], writes=[osB])
                    pr, prb = self.fbank(5)
                    K.op("pe", lambda e, pr=pr, rr=rr: e.matmul(sap(pr, 512, 0, 64, 0, [(1, TB)]),
                                                               lhsT=sap(self.onesf, 128, 64, 1, 0, [(1, 64)]), rhs=rr, start=True, stop=True),
                         reads=[rrB, self.cB], writes=[prb])
                    K.op("dve", lambda e, osb=osb, pr=pr, og=og, c2=c2, ph=ph: e.tensor_tensor(
                        out=sap(self.arena, AR, ph, 64, og + c2 * TB, [(1, TB)]), in0=osb, in1=sap(pr, 512, 0, 64, 0, [(1, TB)]), op=ALU.mult),
                        reads=[osB, prb], writes=[oGB[qb % 2]])
                for fo in range(FC):
                    pt, pb = self.bank()
                    for kc in range(2):
                        K.op("pe", lambda e, pt=pt, kc=kc, fo=fo, og=og: e.matmul(
                            pt[:, :], lhsT=self.rslot(wslot, kc * 1024 + fo * 128, [(1, 128)]),
                            rhs=sap(self.arena, AR, 0, 128, og + kc * TB, [(1, TB)]), start=(kc == 0), stop=(kc == 1)),
                            reads=[self.ringB[wslot], oGB[qb % 2]], writes=[pb], inc=(kc == 1))
                    K.op("dve", lambda e, pt=pt, fo=fo, qb=qb: e.tensor_tensor(out=self.xap(fo, qb), in0=pt[:, :], in1=self.xap(fo, qb), op=ALU.add),
                         reads=[pb, self.xBuf[fo][qb]], writes=[self.xBuf[fo][qb]])

    def s5_prologue(self, sl):
        K = self.K
        K.fence()
        X = self.xT
        RS = FC * L
        B = Buf("s5pro")
        o = [0]

        def T(n):
            a = o[0]
            o[0] += n
            return a
        t = lambda off, n=32: sap(X, RS, 0, 128, off, [(1, n)])
        lr, li_, ldt, dt, th, mg, s16, c16 = (T(32) for _ in range(8))
        er, ei, r2, m2, nm, den, cr, ci, nr, t1, t2 = (T(32) for _ in range(11))
        for (src, dst) in ((self.ssm_lam_re, lr), (self.ssm_lam_im, li_)):
            for g2 in range(2):
                K.dma("sp", sap(X, RS, 64 * g2, 64, dst, [(1, 32)]), dap(src, sl * 4096 + g2 * 64, [(1, 64), (128, 32)]), writes=[B], slow=True)
        for g2 in range(2):
            K.dma("sp", sap(X, RS, 64 * g2, 64, ldt, [(1, 32)]), dap(self.ssm_log_dt, sl * 64 + g2, [(0, 64), (2, 32)]), writes=[B], slow=True)
        A = lambda fn: K.op("act", fn, reads=[B], writes=[B])
        V = lambda fn: K.op("dve", fn, reads=[B], writes=[B])
        tt_ = lambda o_, a, b, op, n=32: V(lambda e: e.tensor_tensor(out=t(o_, n), in0=t(a, n), in1=t(b, n), op=op))
        A(lambda e: e.activation(out=t(dt), in_=t(ldt), func=AF.Exp))
        tt_(th, li_, dt, ALU.mult)
        tt_(mg, lr, dt, ALU.mult)
        V(lambda e: e.tensor_scalar(out=t(th), in0=t(th), scalar1=1.0 / 16, scalar2=None, op0=ALU.mult))
        V(lambda e: e.tensor_scalar(out=t(mg), in0=t(mg), scalar1=1.0 / 16, scalar2=None, op0=ALU.mult))
        uu = T(32)
        acc = T(32)
        tt_(uu, th, th, ALU.mult)

        def horner(dst, coefs, var):
            V(lambda e: e.tensor_scalar(out=t(acc), in0=t(var), scalar1=float(coefs[-1]), scalar2=None, op0=ALU.mult))
            for c in coefs[-2:0:-1]:
                V(lambda e, c=c: e.scalar_tensor_tensor(out=t(acc), in0=t(acc), scalar=float(c), in1=t(var), op0=ALU.add, op1=ALU.mult))
            V(lambda e: e.tensor_scalar(out=t(dst), in0=t(acc), scalar1=float(coefs[0]), scalar2=None, op0=ALU.add))
        f = math.factorial
        horner(s16, [(-1.0) ** k / f(2 * k + 1) for k in range(7)], uu)
        tt_(s16, s16, th, ALU.mult)
        horner(c16, [(-1.0) ** k / f(2 * k) for k in range(7)], uu)
        y_ = T(32)
        V(lambda e: e.tensor_copy(out=t(y_), in_=t(mg)))
        horner(mg, [1.0 / f(k) for k in range(5)], y_)
        tt_(er, mg, c16, ALU.mult)
        tt_(ei, mg, s16, ALU.mult)

        def square():
            tt_(r2, er, er, ALU.mult)
            tt_(m2, ei, ei, ALU.mult)
            tt_(nm, er, ei, ALU.mult)
            tt_(er, r2, m2, ALU.subtract)
            V(lambda e: e.tensor_scalar(out=t(ei), in0=t(nm), scalar1=2.0, scalar2=None, op0=ALU.mult))
        for _ in range(4):
            square()
        V(lambda e: e.tensor_scalar(out=t(nr), in0=t(er), scalar1=-1.0, scalar2=None, op0=ALU.add))
        tt_(t1, lr, lr, ALU.mult)
        tt_(t2, li_, li_, ALU.mult)
        tt_(den, t1, t2, ALU.add)
        V(lambda e: e.reciprocal(out=t(den), in_=t(den)))
        tt_(t1, nr, lr, ALU.mult)
        tt_(t2, ei, li_, ALU.mult)
        tt_(cr, t1, t2, ALU.add)
        tt_(cr, cr, den, ALU.mult)
        tt_(t1, ei, lr, ALU.mult)
        tt_(t2, nr, li_, ALU.mult)
        tt_(ci, t1, t2, ALU.subtract)
        tt_(ci, ci, den, ALU.mult)
        for k in range(KS_STEPS):
            ap_ = lambda c, k=k: sap(self.Apow, 2 * KS_STEPS * 3 * 32, 0, 128, ((sl * KS_STEPS + k) * 3 + c) * 32, [(1, 32)])
            V(lambda e, ap_=ap_: e.tensor_copy(out=ap_(0), in_=t(er)))
            V(lambda e, ap_=ap_: e.tensor_copy(out=ap_(1), in_=t(ei)))
            V(lambda e, ap_=ap_: e.tensor_scalar(out=ap_(2), in0=t(ei), scalar1=-1.0, scalar2=None, op0=ALU.mult))
            if k < KS_STEPS - 1:
                square()
        Bn = [T(512), T(512)]
        for part, src in enumerate((self.ssm_b_re, self.ssm_b_im)):
            K.dma("sp", sap(X, RS, 0, 128, Bn[part], [(16, 32), (1, 16)]), dap(src, sl * 65536, [(16, 128), (2048, 32), (1, 16)]), writes=[B])
        Bb = [T(512), T(512)]
        u1, u2 = T(512), T(512)
        b3 = lambda off: sap(X, RS, 0, 128, off, [(16, 32), (1, 16)])
        cb = lambda off: sap(X, RS, 0, 128, off, [(1, 32), (0, 16)])
        V(lambda e: e.tensor_tensor(out=b3(u1), in0=b3(Bn[0]), in1=cb(cr), op=ALU.mult))
        V(lambda e: e.tensor_tensor(out=b3(u2), in0=b3(Bn[1]), in1=cb(ci), op=ALU.mult))
        V(lambda e: e.tensor_tensor(out=b3(Bb[0]), in0=b3(u1), in1=b3(u2), op=ALU.subtract))
        V(lambda e: e.tensor_tensor(out=b3(u1), in0=b3(Bn[1]), in1=cb(cr), op=ALU.mult))
        V(lambda e: e.tensor_tensor(out=b3(u2), in0=b3(Bn[0]), in1=cb(ci), op=ALU.mult))
        V(lambda e: e.tensor_tensor(out=b3(Bb[1]), in0=b3(u1), in1=b3(u2), op=ALU.add))
        Bd = [T(1024), T(1024)]
        self._s5off = dict(Bd=Bd, T=T, B=B)
        for part in range(2):
            V(lambda e, part=part: e.memset(t(Bd[part], 1024), 0.0))
            for g2 in range(2):
                V(lambda e, part=part, g2=g2: e.tensor_copy(out=sap(X, RS, 64 * g2, 64, Bd[part] + 16 * g2, [(32, 32), (1, 16)]),
                                                            in_=sap(X, RS, 64 * g2, 64, Bb[part], [(16, 32), (1, 16)])))
            for fc in range(FC):
                pt, pb = self.bank()
                K.op("pe", lambda e, pt=pt, part=part, fc=fc: e.transpose(out=sap(pt, 512, 0, 128, 0, [(1, 128)]),
                                                                         in_=t(Bd[part] + fc * 128, 128), identity=self.ident[:, :]),
                     reads=[B, self.cB], writes=[pb])
                K.op("act", lambda e, pt=pt, part=part, fc=fc: e.activation(
                    out=sap(self.BT, 2 * 2 * FC * 128, 0, 128, ((sl * 2 + part) * FC + fc) * 128, [(1, 128)]),
                    in_=sap(pt, 512, 0, 128, 0, [(1, 128)]), func=AF.Copy), reads=[pb], writes=[self.cB])
        CT = [T(1024), T(1024)]
        self._s5off["CT"] = CT
        for part, src in enumerate((self.ssm_c_re, self.ssm_c_im)):
            V(lambda e, part=part: e.memset(t(CT[part], 1024), 0.0))
            for gpl in range(4):
                for g2 in range(2):
                    p0 = 32 * gpl + 16 * g2
                    K.dma("sp", sap(X, RS, p0, 16, CT[part] + 64 * g2, [(128, FC), (1, 64)]),
                          dap(src, sl * 65536 + (2 * gpl + g2) * 1024, [(64, 16), (8192, FC), (1, 64)]), reads=[B], writes=[B])
            for fc in range(FC):
                pt, pb = self.bank()
                K.op("pe", lambda e, pt=pt, part=part, fc=fc: e.transpose(out=sap(pt, 512, 0, 128, 0, [(1, 128)]),
                                                                         in_=t(CT[part] + fc * 128, 128), identity=self.ident[:, :]),
                     reads=[B, self.cB], writes=[pb])
                K.op("act", lambda e, pt=pt, part=part, fc=fc: e.activation(
                    out=sap(self.Cd, 2 * 2 * 1024, 0, 128, (sl * 2 + part) * 1024 + fc * 128, [(1, 128)]),
                    in_=sap(pt, 512, 0, 128, 0, [(1, 128)]), func=AF.Copy, scale=(1.0 if part == 0 else -1.0)), reads=[pb], writes=[self.cB])

    def s5_prologue2(self, sl):
        K = self.K
        nc = self.nc
        X = self.xT
        RS = FC * L
        T = self._s5off["T"]
        Bd = self._s5off["Bd"]
        B = self._s5off["B"]
        A32 = self.AR // 2
        for nm, npc in (("s5A_%d" % sl, 16), ("s5B_%d" % sl, 16), ("s5C_%d" % sl, 8)):
            t_ = nc.dram_tensor("ws_" + nm, [npc * 128, 4096], BF16, kind="Internal")
            self.wscr[nm] = dict(t=t_, npieces=npc, pe=4096)
        V = lambda fn, r=(), w=(): K.op("dve", fn, reads=[B] + list(r), writes=[B] + list(w))
        G = lambda fn, r=(), w=(): K.op("pool", fn, reads=[B] + list(r), writes=[B] + list(w))
        APW = 2 * KS_STEPS * 3 * 32
        apw = lambda k, c, dims: sap(self.Apow, APW, 0, 128, ((sl * KS_STEPS + k) * 3 + c) * 32, dims)
        PW = T(33 * 64)
        pw = lambda t0, c, dims: sap(X, RS, 0, 128, PW + t0 * 64 + c * 32, dims)
        tq1, tq2 = T(16 * 32), T(16 * 32)
        tq = lambda off, d: sap(X, RS, 0, 128, off, [(32, d), (1, 32)])
        V(lambda e: e.memset(pw(0, 0, [(1, 32)]), 1.0))
        V(lambda e: e.memset(pw(0, 1, [(1, 32)]), 0.0))
        V(lambda e: e.tensor_copy(out=pw(1, 0, [(1, 32)]), in_=apw(0, 0, [(1, 32)])))
        V(lambda e: e.tensor_copy(out=pw(1, 1, [(1, 32)]), in_=apw(0, 1, [(1, 32)])))
        for k in range(1, 5):
            d = 1 << k
            pr = pw(0, 0, [(64, d), (1, 32)])
            pi = pw(0, 1, [(64, d), (1, 32)])
            ar = apw(k, 0, [(0, d), (1, 32)])
            ai = apw(k, 1, [(0, d), (1, 32)])
            V(lambda e, pr=pr, ar=ar, d=d: e.tensor_tensor(out=tq(tq1, d), in0=pr, in1=ar, op=ALU.mult))
            V(lambda e, pi=pi, ai=ai, d=d: e.tensor_tensor(out=tq(tq2, d), in0=pi, in1=ai, op=ALU.mult))
            V(lambda e, d=d: e.tensor_tensor(out=pw(d, 0, [(64, d), (1, 32)]), in0=tq(tq1, d), in1=tq(tq2, d), op=ALU.subtract))
            V(lambda e, pr=pr, ai=ai, d=d: e.tensor_tensor(out=tq(tq1, d), in0=pr, in1=ai, op=ALU.mult))
            V(lambda e, pi=pi, ar=ar, d=d: e.tensor_tensor(out=tq(tq2, d), in0=pi, in1=ar, op=ALU.mult))
            V(lambda e, d=d: e.tensor_tensor(out=pw(d, 1, [(64, d), (1, 32)]), in0=tq(tq1, d), in1=tq(tq2, d), op=ALU.add))
        V(lambda e: e.tensor_copy(out=pw(32, 0, [(1, 32)]), in_=apw(5, 0, [(1, 32)])))
        V(lambda e: e.tensor_copy(out=pw(32, 1, [(1, 32)]), in_=apw(5, 1, [(1, 32)])))
        BdB = T(1024)
        Xb = X.bitcast(BF16)
        bdb = lambda part, gp: sap(Xb, 2 * RS, 0, 128, 2 * BdB + part * 1024 + gp * 32, [(1, 32)])
        for part in range(2):
            V(lambda e, part=part: e.tensor_copy(out=sap(Xb, 2 * RS, 0, 128, 2 * BdB + part * 1024, [(1, 1024)]),
                                                 in_=sap(X, RS, 0, 128, Bd[part], [(1, 1024)])))
        YN = T(4224)
        xr = lambda dims, off=0: sap(self.hB32, FC * L // 2, 0, 128, off, dims)
        t1 = lambda dims: sap(self.arena32, A32, 0, 128, 0, dims)
        t2 = lambda dims: sap(self.arena32, A32, 0, 128, 4224, dims)
        yr = lambda dims, off=0: sap(self.ring32, 2 * 4096, 0, 128, off, dims)
        yn = lambda dims, off=0: sap(X, RS, 0, 128, YN + off, dims)
        WAst = lambda dims, off=0: sap(self.ring, 4 * 4096, 0, 128, 8448 + off, dims)
        WBst = lambda dims, off=0: sap(Xb, 2 * RS, 0, 128, 2 * self._s5off["CT"][0] + off, dims)
        WCO = 16896
        WCst = lambda dims, off=0: sap(self.arena, self.AR, 0, 128, WCO + off, dims)
        xB_, t1B, t2B, yB_, wAB, wBB, wCB, ycB = (Buf() for _ in range(8))
        cdap = lambda part, fc: sap(self.Cd, 2 * 2 * 1024, 0, 128, (sl * 2 + part) * 1024 + fc * 128, [(32, 4), (0, 33), (1, 32)])
        G(lambda e: e.memset(WCst([(1, 4096)]), 0.0), w=[wCB])
        full = [(1, 4096)]
        f33 = [(1, 4224)]
        for fc in range(FC):
            bdd = lambda part: sap(X, RS, 0, 128, Bd[part] + fc * 128, [(32, 4), (0, 32), (1, 32)])
            pwd = lambda c, nt: sap(X, RS, 0, 128, PW + c * 32 + 4 * fc, [(1, 4), (64, nt), (0, 32)])
            bdx = lambda part: sap(X, RS, 0, 128, Bd[part] + fc * 128, [(0, 32), (32, 4), (1, 32)])
            pwx = lambda c: sap(X, RS, 0, 128, PW + c * 32 + 4 * fc, [(64, 32), (1, 4), (0, 32)])
            V(lambda e: e.tensor_tensor(out=t1(full), in0=bdx(0), in1=pwx(0), op=ALU.mult), w=[t1B])
            G(lambda e: e.tensor_tensor(out=t2(full), in0=bdx(1), in1=pwx(1), op=ALU.mult), w=[t2B])
            V(lambda e: e.tensor_tensor(out=xr(full), in0=t1(full), in1=t2(full), op=ALU.subtract), r=[t1B, t2B], w=[xB_, ycB])
            V(lambda e: e.tensor_tensor(out=t1(full), in0=bdx(1), in1=pwx(0), op=ALU.mult), w=[t1B])
            G(lambda e: e.tensor_tensor(out=t2(full), in0=bdx(0), in1=pwx(1), op=ALU.mult), w=[t2B])
            V(lambda e: e.tensor_tensor(out=xr(full, 4096), in0=t1(full), in1=t2(full), op=ALU.add), r=[t1B, t2B], w=[xB_, ycB])
            for part in range(2):
                for tb in range(8):
                    pt, pb = self.bank()
                    for q in range(4):
                        t = tb * 4 + q
                        K.op("pe", lambda e, pt=pt, q=q, t=t, part=part: e.transpose(
                            out=sap(pt, 512, 0, 128, q * 128, [(1, 128)]),
                            in_=xr([(1, 128)], part * 4096 + t * 128), identity=self.ident[:, :]),
                            reads=[xB_, self.cB], writes=[pb], inc=(q == 3))
                    if tb % 2 == 0:
                        K.op("act", lambda e, pt=pt, tb=tb: e.activation(out=WAst([(1, 512)], tb * 512), in_=pt[:, :], func=AF.Copy),
                             reads=[pb], writes=[wAB])
                    else:
                        K.op("dve", lambda e, pt=pt, tb=tb: e.tensor_copy(out=WAst([(1, 512)], tb * 512), in_=pt[:, :]), reads=[pb], writes=[wAB])
                wa = self.wscr["s5A_%d" % sl]
                K.dma("sp", dap(wa["t"], (fc * 2 + part) * 128 * 4096, [(4096, 128), (1, 4096)]), WAst(full), reads=[wAB], writes=[self.wB("s5A_%d" % sl)])
            V(lambda e: e.tensor_tensor(out=t1(f33), in0=cdap(0, fc), in1=pwd(0, 33), op=ALU.mult), w=[t1B])
            G(lambda e: e.tensor_tensor(out=t2(f33), in0=cdap(1, fc), in1=pwd(1, 33), op=ALU.mult), w=[t2B])
            V(lambda e: e.tensor_tensor(out=yr(f33), in0=t1(f33), in1=t2(f33), op=ALU.add), r=[t1B, t2B], w=[yB_])
            V(lambda e: e.tensor_tensor(out=t1(f33), in0=cdap(1, fc), in1=pwd(0, 33), op=ALU.mult), w=[t1B])
            G(lambda e: e.tensor_tensor(out=t2(f33), in0=cdap(0, fc), in1=pwd(1, 33), op=ALU.mult), w=[t2B])
            V(lambda e: e.tensor_tensor(out=yn(f33), in0=t1(f33), in1=t2(f33), op=ALU.subtract), r=[t1B, t2B], w=[yB_])
            for part in range(2):
                ysrc = yr if part == 0 else yn
                K.op("act", lambda e, ysrc=ysrc: e.activation(out=WBst([(1024, 4), (1, 1024)]), in_=ysrc([(33 * 32, 4), (1, 1024)], 32), func=AF.Copy),
                     reads=[yB_], writes=[wBB])
                wb = self.wscr["s5B_%d" % sl]
                K.dma("sp", dap(wb["t"], (fc * 2 + part) * 128 * 4096, [(4096, 128), (1, 4096)]), WBst(full), reads=[wBB], writes=[self.wB("s5B_%d" % sl)])
                G(lambda e, ysrc=ysrc, part=part: e.tensor_copy(out=sap(self.hB, FC * L, 0, 128, part * 4096, [(1024, 4), (1, 1024)]),
                                                                in_=ysrc([(33 * 32, 4), (1, 1024)], 0)), r=[yB_, xB_], w=[ycB, xB_])
            pA, pAb = self.bank()
            pBk, pBb = self.bank()
            for gpl in range(4):
                gp = 4 * fc + gpl
                for half, (pt, pb) in enumerate(((pA, pAb), (pBk, pBb))):
                    for part in range(2):
                        K.op("pe", lambda e, pt=pt, part=part, gpl=gpl, gp=gp, half=half: e.matmul(
                            sap(pt, 512, 32 * gpl, 32, 0, [(1, 512)]), lhsT=bdb(part, gp),
                            rhs=sap(self.hB, FC * L, 0, 128, part * 4096 + gpl * 1024 + half * 512, [(1, 512)]),
                            start=(part == 0), stop=(part == 1), tile_position=(0, 32 * gpl)),
                            reads=[ycB, B], writes=[pb], inc=(part == 1))
            for gpl in range(4):
                for half, (pt, pb) in enumerate(((pA, pAb), (pBk, pBb))):
                    eng = "act" if half == 0 else "dve"
                    dst = WCst([(128, 16), (1, 32)], half * 16 * 128 + 32 * gpl)
                    dst = sap(self.arena, self.AR, 32 * gpl, 32, WCO + half * 16 * 128 + 32 * gpl, [(128, 16), (1, 32)])
                    src = sap(pt, 512, 32 * gpl, 32, 0, [(32, 16), (1, 32)])
                    if eng == "act":
                        K.op("act", lambda e, dst=dst, src=src: e.activation(out=dst, in_=src, func=AF.Copy), reads=[pb], writes=[wCB])
                    else:
                        K.op("dve", lambda e, dst=dst, src=src: e.tensor_copy(out=dst, in_=src), reads=[pb], writes=[wCB])
            wc = self.wscr["s5C_%d" % sl]
            K.dma("sp", dap(wc["t"], fc * 128 * 4096, [(4096, 128), (1, 4096)]), sap(self.arena, self.AR, 0, 128, WCO, full),
                  reads=[wCB], writes=[self.wB("s5C_%d" % sl)])

    def s5(self, sl, glu=True):
        self._s5glu = glu
        K = self.K
        K.fence()
        A32 = self.AR // 2
        AR = self.AR
        RE, IM, T1, T2, T3 = 0, 2048, 4096, 6144, 8192
        SBF = 20480
        APW = 2 * KS_STEPS * 3 * 32
        sB, t1B, t2B, t3B, sbfB, taB = (Buf() for _ in range(6))
        f32 = lambda off, dims: sap(self.arena32, A32, 0, 128, off, dims)
        def consA(i, slot):
            fc, part = i // 2, i % 2
            for gpl in range(4):
                pt, pb = self.bank()
                for j in range(32):
                    K.op("pe", lambda e, pt=pt, gpl=gpl, j=j, fc=fc: e.matmul(
                        sap(pt, 512, 0, 128, 0, [(1, 64)]),
                        lhsT=self.rslot(slot, (31 - j) * 128, [(1, 128)], p0=32 * gpl, pn=32),
                        rhs=sap(self.hB, FC * L, 32 * gpl, 32, fc * L + j, [(32, 64)]),
                        start=(j == 0), stop=(j == 31), tile_position=(32 * gpl, 0)),
                        reads=[self.ringB[slot], self.hBuf[fc][0], self.hBuf[fc][1], self.hBuf[fc][2], self.hBuf[fc][3]],
                        writes=[pb], inc=(j == 31))
                dst = f32(part * 2048 + (4 * fc + gpl) * 64, [(1, 64)])
                if gpl % 2 == 0:
                    K.op("act", lambda e, dst=dst, pt=pt: e.activation(out=dst, in_=sap(pt, 512, 0, 128, 0, [(1, 64)]), func=AF.Copy), reads=[pb], writes=[sB])
                else:
                    K.op("dve", lambda e, dst=dst, pt=pt: e.tensor_copy(out=dst, in_=sap(pt, 512, 0, 128, 0, [(1, 64)])), reads=[pb], writes=[sB])
        self.stream([("s5A_%d" % sl, pc) for pc in range(16)], consA)
        import os
        PH = "123"
        if "2" not in PH:
            return
        for k in range(6):
            d = 1 << k
            n = 64 - d
            ar = sap(self.Apow, APW, 0, 128, ((sl * KS_STEPS + k + 5) * 3 + 0) * 32, [(1, 32), (0, n)])
            ai = sap(self.Apow, APW, 0, 128, ((sl * KS_STEPS + k + 5) * 3 + 1) * 32, [(1, 32), (0, n)])
            v = lambda off, sh=0, n=n: f32(off + sh, [(64, 32), (1, n)])
            K.op("dve", lambda e, v=v, ar=ar: e.tensor_tensor(out=v(T1), in0=v(RE), in1=ar, op=ALU.mult), reads=[sB, self.cB], writes=[t1B])
            K.op("pool", lambda e, v=v, ai=ai: e.tensor_tensor(out=v(T2), in0=v(IM), in1=ai, op=ALU.mult), reads=[sB, self.cB], writes=[t2B])
            K.op("dve", lambda e, v=v: e.tensor_tensor(out=v(T1), in0=v(T1), in1=v(T2), op=ALU.subtract), reads=[t1B, t2B], writes=[t1B])
            K.op("pool", lambda e, v=v, ai=ai: e.tensor_tensor(out=v(T2), in0=v(RE), in1=ai, op=ALU.mult), reads=[sB, t1B, self.cB], writes=[t2B])
            K.op("dve", lambda e, v=v, ar=ar: e.tensor_tensor(out=v(T3), in0=v(IM), in1=ar, op=ALU.mult), reads=[sB, self.cB], writes=[t3B])
            K.op("pool", lambda e, v=v: e.tensor_tensor(out=v(T2), in0=v(T2), in1=v(T3), op=ALU.add), reads=[t2B, t3B], writes=[t2B])
            K.op("dve", lambda e, v=v, d=d: e.tensor_tensor(out=v(RE, d), in0=v(RE, d), in1=v(T1), op=ALU.add), reads=[sB, t1B, t2B], writes=[sB])
            K.op("pool", lambda e, v=v, d=d: e.tensor_tensor(out=v(IM, d), in0=v(IM, d), in1=v(T2), op=ALU.add), reads=[sB, t2B], writes=[sB])
        for part in range(2):
            K.op("pool", lambda e, part=part: e.memset(sap(self.arena, AR, 0, 128, SBF + part * 2048, [(64, 32), (1, 1)]), 0.0), reads=[sbfB], writes=[sbfB])
            K.op("act", lambda e, part=part: e.activation(out=sap(self.arena, AR, 0, 128, SBF + part * 2048 + 1, [(64, 32), (1, 63)]),
                                                          in_=f32(part * 2048, [(64, 32), (1, 63)]), func=AF.Copy), reads=[sB, sbfB], writes=[sbfB])
        if "3" not in PH:
            return
        ybanks = [(self.ps[4 + b], self.psB[4 + b]) for b in range(4)]
        ybufs = [yb for _, yb in ybanks]

        def consBC(i, slot):
            fc, kind = i // 3, i % 3
            hall = [self.hBuf[fc][b] for b in range(NBLK)]
            if kind == 0:
                import os
                if False:
                    ops = [(t, j) for t in range(32) for j in range(t, 32)]
                    for n_, (t, j) in enumerate(ops):
                        yt, yb = ybanks[j // 8]
                        K.op("pe", lambda e, yt=yt, t=t, j=j, fc=fc: e.matmul(
                            sap(yt, 512, 0, 128, (j % 8) * 64, [(1, 64)]),
                            lhsT=self.rslot(slot, t * 128, [(1, 128)]),
                            rhs=sap(self.hB, FC * L, 0, 128, fc * L + (j - t), [(32, 64)]),
                            start=(t == 0), stop=False), reads=[self.ringB[slot]] + hall, writes=[yb], inc=(n_ == len(ops) - 1))
                else:
                    ops = []
                    for t in range(32):
                        for b in range(4):
                            jlo, jhi = max(8 * b, t), 8 * b + 8
                            if jlo < jhi:
                                ops.append((t, b, jlo, jhi - jlo))
                    for n_, (t, b, jlo, nj) in enumerate(ops):
                        yt, yb = ybanks[b]
                        K.op("pe", lambda e, yt=yt, t=t, b=b, jlo=jlo, nj=nj, fc=fc: e.matmul(
                            sap(yt, 512, 0, 128, (jlo - 8 * b) * 64, [(1, nj * 64)]),
                            lhsT=self.rslot(slot, t * 128, [(1, 128)]),
                            rhs=sap(self.hB, FC * L, 0, 128, fc * L + (jlo - t), [(1, nj), (32, 64)]),
                            start=(t == 0), stop=False), reads=[self.ringB[slot]] + hall, writes=[yb], inc=(n_ == len(ops) - 1))
            else:
                part = kind - 1
                for j in range(32):
                    yt, yb = ybanks[j // 8]
                    for gpl in range(4):
                        gp = 4 * fc + gpl
                        last = (j == 31 and gpl == 3)
                        K.op("pe", lambda e, yt=yt, j=j, gpl=gpl, gp=gp, part=part, last=last: e.matmul(
                            sap(yt, 512, 32 * gpl, 32, (j % 8) * 64, [(1, 64)]),
                            lhsT=self.rslot(slot, gpl * 1024 + j * 32, [(1, 32)]),
                            rhs=sap(self.arena, AR, 0, 128, SBF + part * 2048 + gp * 64, [(1, 64)]),
                            start=False, stop=(last and part == 1), tile_position=(0, 32 * gpl)),
                            reads=[self.ringB[slot], sbfB], writes=[yb], inc=last)
                if part == 1:
                    for b in range(4):
                        yt, yb = ybanks[b]
                        K.op("dve", lambda e, yt=yt, b=b, fc=fc: e.scalar_tensor_tensor(
                            out=f32(T1 + 8 * b, [(32, 64), (1, 8)]),
                            in0=sap(self.hB, FC * L, 0, 128, fc * L + 8 * b, [(32, 64), (1, 8)]),
                            scalar=sap(self.dT, 2 * FC, 0, 128, sl * FC + fc, [(1, 1)]),
                            in1=sap(yt, 512, 0, 128, 0, [(1, 64), (64, 8)]), op0=ALU.mult, op1=ALU.add),
                            reads=[yb, self.cB] + hall, writes=[taB])
                    for blk in range(NBLK):
                        K.op("act", lambda e, blk=blk, fc=fc: e.activation(out=self.hap(fc, blk * TB, TB), in_=f32(T1 + blk * TB, [(1, TB)]),
                                                                          func=AF.Gelu_apprx_tanh), reads=[taB], writes=[self.hBuf[fc][blk]])
        self.stream([("s5%s_%d" % (("C", "B", "B")[i % 3], sl), (i // 3) if i % 3 == 0 else (2 * (i // 3) + (i % 3) - 1)) for i in range(24)], consBC)
        if self._s5glu:
            self.s5_glu(sl)

    def s5_glu(self, sl):
        K = self.K
        A32 = self.AR // 2
        K.fence()
        sgB = [Buf() for _ in range(FC)]
        tB_ = [Buf(), Buf()]
        for blk in range(NBLK):
            def cons(i, slot, blk=blk):
                for q in range(4):
                    fo = (i % 2) * 4 + q
                    pt, pb = self.bank()
                    for kc in range(FC):
                        K.op("pe", lambda e, pt=pt, kc=kc, q=q: e.matmul(
                            pt[:, :], lhsT=self.rslot(slot, kc * 512 + q * 128, [(1, 128)]),
                            rhs=self.hap(kc, blk * TB, TB), start=(kc == 0), stop=(kc == FC - 1)),
                            reads=[self.ringB[slot], self.hBuf[kc][blk]], writes=[pb], inc=(kc == FC - 1))
                    sg = sap(self.arena32, A32, 0, 128, fo * TB, [(1, TB)])
                    if i < 2:
                        K.op("act", lambda e, sg=sg, pt=pt: e.activation(out=sg, in_=pt[:, :], func=AF.Sigmoid), reads=[pb], writes=[sgB[fo]])
                    else:
                        j = fo % 2
                        tt = sap(self.arena32, A32, 0, 128, 4096 + j * TB, [(1, TB)])
                        K.op("dve", lambda e, tt=tt, pt=pt, sg=sg: e.tensor_tensor(out=tt, in0=pt[:, :], in1=sg, op=ALU.mult),
                             reads=[pb, sgB[fo]], writes=[tB_[j]])
                        K.op("pool", lambda e, tt=tt, fo=fo: e.tensor_tensor(out=self.xap(fo, blk), in0=self.xap(fo, blk), in1=tt, op=ALU.add),
                             reads=[tB_[j], self.xBuf[fo][blk]], writes=[self.xBuf[fo][blk]])
            self.stream([("glu_%d" % sl, 2), ("glu_%d" % sl, 3), ("glu_%d" % sl, 0), ("glu_%d" % sl, 1)], cons)

    def layer(self, s, li):
        kind, sl = li % 3, li // 3
        self.norm(0, li)
        if kind == 0:
            self.s5(sl)
        elif kind == 1:
            self.conv()
        else:
            self.fox()
        self.norm(1, li)
        self.mlp(li)
        self.norm(2, li)
        self.ple(s, li)


_CACHE = {}


def _get_prog():
    if "p" not in _CACHE:
        _CACHE["p"] = Prog()
    return _CACHE["p"]


def kernel(**inputs):
    prog = _get_prog()
    names = ["norm_mix", "norm_ffn", "norm_ple", "norm_final", "ssm_lam_re", "ssm_lam_im", "ssm_log_dt", "ssm_b_re",
             "ssm_b_im", "ssm_c_re", "ssm_c_im", "ssm_d", "ssm_w_glu", "conv_w_in", "conv_w", "conv_w_out", "fox_w_in",
             "fox_b_f", "fox_w_out", "mlp_w1", "mlp_w2", "ple_w", "ple_gate_w"]
    shared = {n: np.ascontiguousarray(np.asarray(inputs[n], dtype=np.float32)) for n in names}
    x = np.asarray(inputs["x"], dtype=np.float32)
    p = np.asarray(inputs["p"], dtype=np.float32)
    in_maps = []
    for c in range(NCORES):
        m = dict(shared)
        m["x"] = np.ascontiguousarray(x[c * SEQ_PER_CORE:(c + 1) * SEQ_PER_CORE])
        m["p"] = np.ascontiguousarray(p[:, c * SEQ_PER_CORE:(c + 1) * SEQ_PER_CORE])
        in_maps.append(m)
    res = run_bass_kernel_spmd(prog.nc, in_maps, core_ids=list(range(NCORES)))
    return np.concatenate([np.asarray(r["out"]) for r in res.results], axis=0).astype(np.float32)
```

```python
import math
import numpy as np
import concourse.bass as bass
import concourse.mybir as mybir
from concourse.bass_utils import run_bass_kernel_spmd

F32 = mybir.dt.float32
BF16 = mybir.dt.bfloat16
AF = mybir.ActivationFunctionType
ALU = mybir.AluOpType

D = 1024
L = 2048
FC = 8
NBLK = 4
TB = 512
DFF = 4096
PLE = 256
DEPTH = 4
NCORES = 8
SEQ_PER_CORE = 4
EPS = 1e-6
NDS = 16
KS_STEPS = 11


class Buf:
    __slots__ = ("name", "w", "r")

    def __init__(self, name=""):
        self.name = name
        self.w = None
        self.r = {}


class Trk:
    def __init__(self, nc):
        self.nc = nc
        self.eng = {"pe": nc.tensor, "act": nc.scalar, "dve": nc.vector, "pool": nc.gpsimd, "sp": nc.sync}
        self.sem = {e: nc.alloc_semaphore("sem_" + e) for e in self.eng}
        self.cnt = {e: 0 for e in self.eng}
        self.seen = {e: {} for e in self.eng}
        self.dsem = [nc.alloc_semaphore("dsem%d" % i) for i in range(NDS)]
        self.dval = [0] * NDS
        self.dnext = 0
        self.fdeps = {}
        self.nbank = 0

    def fence(self):
        d = {}
        for e, c in self.cnt.items():
            if c > 0:
                d[e] = c
        for i, v in enumerate(self.dval):
            if v > 0:
                d[i] = v
        self.fdeps = d

    def _wait(self, e, deps):
        for key, val in deps.items():
            if self.seen[e].get(key, 0) >= val:
                continue
            semh = self.sem[key] if isinstance(key, str) else self.dsem[key]
            self.eng[e].wait_ge(semh, val)
            self.seen[e][key] = val

    def _deps(self, e, reads, writes, same_ok, nofence):
        deps = {}

        def add(k, v):
            if k == e and same_ok:
                return
            if deps.get(k, 0) < v:
                deps[k] = v

        for b in reads:
            if b.w:
                add(*b.w)
        for b in writes:
            if b.w:
                add(*b.w)
            for k, v in b.r.items():
                add(k, v)
        if not nofence:
            for k, v in self.fdeps.items():
                add(k, v)
        return deps

    def op(self, e, fn, reads=(), writes=(), inc=True):
        self._wait(e, self._deps(e, reads, writes, e == "pe", False))
        inst = fn(self.eng[e])
        if inc:
            self.cnt[e] += 1
            inst.then_inc(self.sem[e], 1)
            n = self.cnt[e]
        else:
            n = self.cnt[e] + 1
        for b in reads:
            b.r[e] = n
        for b in writes:
            b.w = (e, n)
            b.r = {}

    def dma(self, e, out, in_, reads=(), writes=(), nofence=False, slow=False):
        i = self.dnext
        self.dnext = (self.dnext + 1) % NDS
        deps = self._deps(e, reads, writes, False, nofence)
        if self.dval[i] > 0 and deps.get(i, 0) < self.dval[i]:
            deps[i] = self.dval[i]
        self._wait(e, deps)
        self.dval[i] += 16
        v = self.dval[i]
        if slow:
            self.eng[e].dma_start(out=out, in_=in_, allow_slow_non_contiguous=True).then_inc(self.dsem[i], 16)
        else:
            self.eng[e].dma_start(out=out, in_=in_).then_inc(self.dsem[i], 16)
        for b in reads:
            b.r[i] = v
        for b in writes:
            b.w = (i, v)
            b.r = {}


def sap(t, rs, p0, pn, off, dims):
    return bass.AP(t, p0 * rs + off, [[rs, pn]] + [[a, b] for a, b in dims])


def dap(t, off, dims):
    return bass.AP(t, off, [[a, b] for a, b in dims])


class Prog:
    def __init__(self, nseq=SEQ_PER_CORE, layers=(0, 1, 2, 3), do_final=True, mid_after=None, s5dbg=False, s5main=False):
        self.s5dbg = s5dbg
        self.s5main = s5main
        self.mid_after = mid_after
        self.nseq = nseq
        self.layers = tuple(layers)
        self.do_final = do_final
        nc = bass.Bass("TRN2", target_bir_lowering=False)
        self.nc = nc
        self.K = Trk(nc)
        if getattr(self, "s5dbg", False):
            self.s5debug()
            return
        self.decl()
        self.alloc()
        self.consts()
        self.prologue_weights()
        for sl in range(2):
            if (3 * sl) in self.layers:
                self.s5_prologue(sl)
                self.s5_prologue2(sl)
        self.K.fence()
        self.K._wait("sp", dict(self.K.fdeps))
        for s in range(nseq):
            self.load_x(s)
            for li in self.layers:
                self.layer(s, li)
                if li == self.mid_after:
                    self.store_out(s, self.out_mid, False)
            self.store_out(s, self.out, self.do_final)
        self.finish()

    def s5debug(self):
        nc = self.nc
        K = self.K
        inp = lambda name, shape: nc.dram_tensor(name, list(shape), F32, kind="ExternalInput")
        self.ssm_lam_re = inp("ssm_lam_re", [2, 64, 64])
        self.ssm_lam_im = inp("ssm_lam_im", [2, 64, 64])
        self.ssm_log_dt = inp("ssm_log_dt", [2, 64])
        self.ssm_b_re = inp("ssm_b_re", [2, 64, 64, 16])
        self.ssm_b_im = inp("ssm_b_im", [2, 64, 64, 16])
        self.ssm_c_re = inp("ssm_c_re", [2, 64, 16, 64])
        self.ssm_c_im = inp("ssm_c_im", [2, 64, 16, 64])
        self.wscr = {}
        self.alloc()
        cB = self.cB
        K.op("pool", lambda e: e.memset(self.onesf[:, :], 1.0), writes=[cB])
        K.op("pool", lambda e: e.affine_select(out=self.ident[:, :], in_=self.onesf[:, :], pattern=[[1, 128]],
                                                compare_op=ALU.is_equal, fill=0.0, base=0, channel_multiplier=-1),
             reads=[cB], writes=[cB])
        self.s5_prologue(0)
        self.s5_prologue2(0)
        K.fence()
        K._wait("sp", dict(K.fdeps))
        if getattr(self, "s5main", False):
            u_in = inp("u", [128, FC * L])
            self.ssm_d = inp("ssm_d", [2, D])
            for sl_ in range(2):
                K.dma("sp", sap(self.dT, 2 * FC, 0, 128, sl_ * FC, [(1, FC)]), dap(self.ssm_d, sl_ * D, [(1, 128), (128, FC)]), writes=[cB], slow=True)
            ub = Buf()
            for fc in range(FC):
                K.dma("sp", sap(self.xT, FC * L, 0, 128, fc * L, [(1, L)]), dap(u_in, fc * L, [(FC * L, 128), (1, L)]), writes=[ub])
                K.op("act", lambda e, fc=fc: e.activation(out=self.hap(fc, 0, L), in_=sap(self.xT, FC * L, 0, 128, fc * L, [(1, L)]), func=AF.Copy),
                     reads=[ub], writes=[self.hBuf[fc][b] for b in range(NBLK)])
            self.s5(0, glu=False)
            K.fence()
            d_y = nc.dram_tensor("d_y", [128, FC * L], BF16, kind="ExternalOutput")
            K.dma("sp", d_y.ap(), self.hB[:, :], reads=[self.hBuf[f][b] for f in range(FC) for b in range(NBLK)], writes=[Buf()])
        for nm in ("s5A_0", "s5B_0", "s5C_0"):
            w = self.wscr[nm]
            dd = nc.dram_tensor("d_" + nm, [w["npieces"] * 128, 4096], BF16, kind="ExternalOutput")
            K.dma("sp", dd.ap(), w["t"].ap(), reads=[self.wB(nm)], writes=[Buf()])
        d_ap = nc.dram_tensor("d_apow", [128, 2 * KS_STEPS * 3 * 32], F32, kind="ExternalOutput")
        d_bt = nc.dram_tensor("d_bt", [128, 2 * 2 * FC * 128], BF16, kind="ExternalOutput")
        d_cd = nc.dram_tensor("d_cd", [128, 2 * 2 * 1024], BF16, kind="ExternalOutput")
        d_x = nc.dram_tensor("d_x", [128, 8192], F32, kind="ExternalOutput")
        d_id = nc.dram_tensor("d_id", [128, 128], F32, kind="ExternalOutput")
        ob = Buf()
        K.dma("sp", d_ap.ap(), self.Apow[:, :], writes=[ob])
        K.dma("sp", d_bt.ap(), self.BT[:, :], writes=[ob])
        K.dma("sp", d_cd.ap(), self.Cd[:, :], writes=[ob])
        K.dma("sp", d_x.ap(), sap(self.xT, FC * L, 0, 128, 0, [(1, 8192)]), writes=[ob])
        K.dma("sp", d_id.ap(), self.ident[:, :], writes=[ob])
        self.finish()

    def decl(self):
        nc = self.nc
        S = self.nseq

        def inp(name, shape):
            return nc.dram_tensor(name, list(shape), F32, kind="ExternalInput")

        self.x = inp("x", [S, L, D])
        self.p = inp("p", [DEPTH, S, L, PLE])
        self.norm_mix = inp("norm_mix", [DEPTH, D])
        self.norm_ffn = inp("norm_ffn", [DEPTH, D])
        self.norm_ple = inp("norm_ple", [DEPTH, D])
        self.norm_final = inp("norm_final", [D])
        self.ssm_lam_re = inp("ssm_lam_re", [2, 64, 64])
        self.ssm_lam_im = inp("ssm_lam_im", [2, 64, 64])
        self.ssm_log_dt = inp("ssm_log_dt", [2, 64])
        self.ssm_b_re = inp("ssm_b_re", [2, 64, 64, 16])
        self.ssm_b_im = inp("ssm_b_im", [2, 64, 64, 16])
        self.ssm_c_re = inp("ssm_c_re", [2, 64, 16, 64])
        self.ssm_c_im = inp("ssm_c_im", [2, 64, 16, 64])
        self.ssm_d = inp("ssm_d", [2, D])
        self.ssm_w_glu = inp("ssm_w_glu", [2, D, 2 * D])
        self.conv_w_in = inp("conv_w_in", [1, D, 3 * D])
        self.conv_w = inp("conv_w", [1, 3, D])
        self.conv_w_out = inp("conv_w_out", [1, D, D])
        self.fox_w_in = inp("fox_w_in", [1, D, 3 * D + 16])
        self.fox_b_f = inp("fox_b_f", [1, 16])
        self.fox_w_out = inp("fox_w_out", [1, D, D])
        self.mlp_w1 = inp("mlp_w1", [DEPTH, D, DFF])
        self.mlp_w2 = inp("mlp_w2", [DEPTH, DFF, D])
        self.ple_w = inp("ple_w", [DEPTH, PLE, D])
        self.ple_gate_w = inp("ple_gate_w", [DEPTH, D, D])
        self.out = nc.dram_tensor("out", [S, L, D], F32, kind="ExternalOutput")
        if self.mid_after is not None:
            self.out_mid = nc.dram_tensor("out_mid", [S, L, D], F32, kind="ExternalOutput")
        self.wscr = {}

    def add_w(self, key, src, src_off, ncols_total, nk, cw, col0, ncols):
        npieces = ncols // cw
        pe = nk * cw
        t = self.nc.dram_tensor("ws_" + key, [npieces * 128, pe], BF16, kind="Internal")
        self.wscr[key] = dict(t=t, npieces=npieces, pe=pe, nk=nk, cw=cw, src=src, src_off=src_off,
                              rs=ncols_total, col0=col0, ksplit=False)

    def add_w_custom(self, key, src, rs, nk, cw, offsets):
        pe = nk * cw
        t = self.nc.dram_tensor("ws_" + key, [len(offsets) * 128, pe], BF16, kind="Internal")
        self.wscr[key] = dict(t=t, npieces=len(offsets), pe=pe, nk=nk, cw=cw, src=src, src_off=0, rs=rs, col0=0, ksplit=False, offsets=list(offsets))

    def add_w_ksplit(self, key, src, src_off, ncols_total, nkp, npieces):
        pe = nkp * ncols_total
        t = self.nc.dram_tensor("ws_" + key, [npieces * 128, pe], BF16, kind="Internal")
        self.wscr[key] = dict(t=t, npieces=npieces, pe=pe, nk=nkp, cw=ncols_total, src=src, src_off=src_off,
                              rs=ncols_total, col0=0, ksplit=True)

    def alloc(self):
        nc = self.nc
        self.xT = nc.alloc_sbuf_tensor("xT", [128, FC * L], F32)
        self.hB = nc.alloc_sbuf_tensor("hB", [128, FC * L], BF16)
        self.AR = 24576
        self.arena = nc.alloc_sbuf_tensor("arena", [128, self.AR], BF16)
        self.arena32 = self.arena.bitcast(F32)
        self.ring = nc.alloc_sbuf_tensor("ring", [128, 4 * 4096], BF16)
        self.ring32 = self.ring.bitcast(F32)
        self.hB32 = self.hB.bitcast(F32)
        self.ringB = [Buf("ring%d" % i) for i in range(4)]
        self.ringn = 0
        self.xBuf = [[Buf("x%d_%d" % (f, b)) for b in range(NBLK)] for f in range(FC)]
        self.hBuf = [[Buf("h%d_%d" % (f, b)) for b in range(NBLK)] for f in range(FC)]
        self.ps = [nc.alloc_psum_tensor("ps%d" % i, [128, 512], F32) for i in range(8)]
        self.psB = [Buf("ps%d" % i) for i in range(8)]
        self.gT = nc.alloc_sbuf_tensor("gT", [128, 3 * DEPTH * FC], F32)
        self.gfin = nc.alloc_sbuf_tensor("gfin", [128, D], F32)
        self.cwT = nc.alloc_sbuf_tensor("cwT", [128, 3 * FC], F32)
        self.dT = nc.alloc_sbuf_tensor("dT", [128, 2 * FC], F32)
        self.nbf = nc.alloc_sbuf_tensor("nbf", [128, 1], F32)
        self.ident = nc.alloc_sbuf_tensor("ident", [128, 128], F32)
        self.identb = nc.alloc_sbuf_tensor("identb", [128, 128], BF16)
        self.onesb = nc.alloc_sbuf_tensor("onesb", [128, 128], BF16)
        self.onesf = nc.alloc_sbuf_tensor("onesf", [128, 128], F32)
        self.trib = nc.alloc_sbuf_tensor("trib", [128, 128], BF16)
        self.wfT = nc.alloc_sbuf_tensor("wfT", [128, 8 * 16], BF16)
        self.halo = nc.alloc_sbuf_tensor("halo", [128, FC * 2], F32)
        self.BT = nc.alloc_sbuf_tensor("BT", [128, 2 * 2 * FC * 128], BF16)
        self.Cd = nc.alloc_sbuf_tensor("Cd", [128, 2 * 2 * 32 * 32], BF16)
        self.Apow = nc.alloc_sbuf_tensor("Apow", [128, 2 * KS_STEPS * 3 * 32], F32)
        self.cB = Buf("consts")

    def bank(self):
        bs = getattr(self, "bank_set", None) or list(range(8))
        self.K.nbank = (self.K.nbank + 1) % len(bs)
        i = bs[self.K.nbank]
        return self.ps[i], self.psB[i]

    def fbank(self, i):
        return self.ps[i], self.psB[i]

    def consts(self):
        K = self.K
        cB = self.cB
        for k, t in enumerate((self.norm_mix, self.norm_ffn, self.norm_ple)):
            for li in range(DEPTH):
                K.dma("sp", sap(self.gT, 3 * DEPTH * FC, 0, 128, (k * DEPTH + li) * FC, [(1, FC)]),
                      dap(t, li * D, [(1, 128), (128, FC)]), writes=[cB], slow=True)
        K.dma("sp", sap(self.gfin, D, 0, 128, 0, [(1, D)]), dap(self.norm_final, 0, [(0, 128), (1, D)]), writes=[cB])
        for tap in range(3):
            K.dma("sp", sap(self.cwT, 3 * FC, 0, 128, tap * FC, [(1, FC)]),
                  dap(self.conv_w, tap * D, [(1, 128), (128, FC)]), writes=[cB], slow=True)
        for sl in range(2):
            K.dma("sp", sap(self.dT, 2 * FC, 0, 128, sl * FC, [(1, FC)]),
                  dap(self.ssm_d, sl * D, [(1, 128), (128, FC)]), writes=[cB], slow=True)
        K.dma("sp", sap(self.nbf, 1, 0, 16, 0, [(1, 1)]), dap(self.fox_b_f, 0, [(1, 16), (1, 1)]), writes=[cB], slow=True)
        K.op("dve", lambda e: e.tensor_scalar(out=sap(self.nbf, 1, 0, 16, 0, [(1, 1)]), in0=sap(self.nbf, 1, 0, 16, 0, [(1, 1)]),
                                               scalar1=-1.0, scalar2=None, op0=ALU.mult), reads=[cB], writes=[cB])
        K.op("pool", lambda e: e.memset(self.onesf[:, :], 1.0), writes=[cB])
        K.op("pool", lambda e: e.memset(self.onesb[:, :], 1.0), writes=[cB])
        K.op("pool", lambda e: e.affine_select(out=self.ident[:, :], in_=self.onesf[:, :], pattern=[[1, 128]],
                                                compare_op=ALU.is_equal, fill=0.0, base=0, channel_multiplier=-1),
             reads=[cB], writes=[cB])
        K.op("pool", lambda e: e.affine_select(out=self.identb[:, :], in_=self.onesb[:, :], pattern=[[1, 128]],
                                                compare_op=ALU.is_equal, fill=0.0, base=0, channel_multiplier=-1),
             reads=[cB], writes=[cB])
        K.op("pool", lambda e: e.affine_select(out=self.trib[:, :], in_=self.onesb[:, :], pattern=[[1, 128]],
                                                compare_op=ALU.is_ge, fill=0.0, base=0, channel_multiplier=-1),
             reads=[cB], writes=[cB])
        K.op("pool", lambda e: e.memset(self.halo[:, :], 0.0), writes=[cB])

    def prologue_weights(self):
        K = self.K
        for li in self.layers:
            self.add_w("w1_%d" % li, self.mlp_w1, li * D * DFF, DFF, 8, 512, 0, DFF)
            self.add_w_custom("w2_%d" % li, self.mlp_w2, D, 8, 512,
                              [li * DFF * D + hq * 8 * 128 * D + cp * 512 for hq in range(4) for cp in range(2)])
            self.add_w("gate_%d" % li, self.ple_gate_w, li * D * D, D, 8, 512, 0, D)
            self.add_w("ple_%d" % li, self.ple_w, li * PLE * D, D, 2, 1024, 0, D)
            kind, sl = li % 3, li // 3
            if kind == 0:
                self.add_w("glu_%d" % sl, self.ssm_w_glu, sl * D * 2 * D, 2 * D, 8, 512, 0, 2 * D)
            elif kind == 1:
                self.add_w("cin", self.conv_w_in, 0, 3 * D, 8, 512, 0, 3 * D)
                self.add_w("cout", self.conv_w_out, 0, D, 8, 512, 0, D)
            else:
                self.add_w("fin", self.fox_w_in, 0, 3 * D + 16, 8, 256, 0, 3 * D)
                self.add_w_ksplit("fout", self.fox_w_out, 0, D, 2, 4)
        stB = [Buf("stg0"), Buf("stg1")]
        cvB = [Buf("cv0"), Buf("cv1")]
        n = 0
        for key, w in self.wscr.items():
            for pc in range(w["npieces"]):
                j = n % 2
                nk, cw, rs = w["nk"], w["cw"], w["rs"]
                if "offsets" in w:
                    src = dap(w["src"], w["offsets"][pc], [(rs, 128), (128 * rs, nk), (1, cw)])
                elif w["ksplit"]:
                    src = dap(w["src"], w["src_off"] + pc * nk * 128 * rs, [(rs, 128), (128 * rs, nk), (1, cw)])
                else:
                    src = dap(w["src"], w["src_off"] + w["col0"] + pc * cw, [(rs, 128), (128 * rs, nk), (1, cw)])
                stg = sap(self.xT, FC * L, 0, 128, j * 4096, [(cw, nk), (1, cw)])
                K.dma("sp", stg, src, writes=[stB[j]])
                cv = sap(self.hB, FC * L, 0, 128, j * 4096, [(1, nk * cw)])
                stg_flat = sap(self.xT, FC * L, 0, 128, j * 4096, [(1, nk * cw)])
                eng = ("act", "dve")[n % 2]
                if eng == "act":
                    K.op("act", lambda e, o=cv, i=stg_flat: e.activation(out=o, in_=i, func=AF.Copy), reads=[stB[j]], writes=[cvB[j]])
                else:
                    K.op(eng, lambda e, o=cv, i=stg_flat: e.tensor_copy(out=o, in_=i), reads=[stB[j]], writes=[cvB[j]])
                dst = dap(w["t"], pc * 128 * w["pe"], [(w["pe"], 128), (1, w["pe"])])
                K.dma("sp", dst, cv, reads=[cvB[j]], writes=[self.wB(key)])
                n += 1
        if 2 in self.layers:
            stg = sap(self.xT, FC * L, 0, 128, 2 * 4096, [(16, 8), (1, 16)])
            K.dma("sp", stg, dap(self.fox_w_in, 3 * D, [(3 * D + 16, 128), (128 * (3 * D + 16), 8), (1, 16)]), writes=[stB[0]], slow=True)
            K.op("dve", lambda e: e.tensor_copy(out=self.wfT[:, :], in_=sap(self.xT, FC * L, 0, 128, 2 * 4096, [(1, 128)])),
                 reads=[stB[0]], writes=[self.cB])

    def wB(self, key):
        w = self.wscr[key]
        if "B" not in w:
            w["B"] = Buf("w_" + key)
        return w["B"]

    def ring_load(self, key, pc):
        K = self.K
        w = self.wscr[key]
        i = self.ringn
        self.ringn = (i + 1) % 4
        pe = w["pe"]
        K.dma("sp", sap(self.ring, 4 * 4096, 0, 128, i * 4096, [(1, pe)]),
              dap(w["t"], pc * 128 * pe, [(pe, 128), (1, pe)]), reads=[self.wB(key)], writes=[self.ringB[i]], nofence=True)
        return i

    def stream(self, plist, consume, depth=2):
        n = len(plist)
        slots = {}
        for i in range(min(depth, n)):
            slots[i] = self.ring_load(*plist[i])
        for i in range(n):
            consume(i, slots.pop(i))
            if i + depth < n:
                slots[i + depth] = self.ring_load(*plist[i + depth])

    def rslot(self, i, off, dims, p0=0, pn=128):
        return sap(self.ring, 4 * 4096, p0, pn, i * 4096 + off, dims)

    def xap(self, fc, blk, p0=0, pn=128, n=TB, off=0):
        return sap(self.xT, FC * L, p0, pn, fc * L + blk * TB + off, [(1, n)])

    def hap(self, fc, tok0, n, p0=0, pn=128):
        return sap(self.hB, FC * L, p0, pn, fc * L + tok0, [(1, n)])

    def load_x(self, s):
        K = self.K
        K.fence()
        stB = [Buf(), Buf()]
        for tt in range(16):
            j = tt % 2
            stg = sap(self.arena32, self.AR // 2, 0, 128, j * 1024, [(1, 1024)])
            K.dma("sp", stg, dap(self.x, (s * L + tt * 128) * D, [(D, 128), (1, D)]), writes=[stB[j]])
            for half in range(2):
                pt, pb = self.bank()
                for q in range(4):
                    fc = half * 4 + q
                    K.op("pe", lambda e, pt=pt, q=q, fc=fc, j=j: e.transpose(
                        out=sap(pt, 512, 0, 128, q * 128, [(1, 128)]),
                        in_=sap(self.arena32, self.AR // 2, 0, 128, j * 1024 + fc * 128, [(1, 128)]),
                        identity=self.ident[:, :]), reads=[stB[j], self.cB], writes=[pb], inc=(q == 3))
                dst = sap(self.xT, FC * L, 0, 128, half * 4 * L + tt * 128, [(L, 4), (1, 128)])
                src = sap(pt, 512, 0, 128, 0, [(128, 4), (1, 128)])
                wr = [self.xBuf[half * 4 + q][tt // 4] for q in range(4)]
                if (tt + half) % 2 == 0:
                    K.op("act", lambda e, dst=dst, src=src: e.activation(out=dst, in_=src, func=AF.Copy), reads=[pb], writes=wr)
                else:
                    K.op("dve", lambda e, dst=dst, src=src: e.tensor_copy(out=dst, in_=src), reads=[pb], writes=wr)

    def store_out(self, s, target, final):
        K = self.K
        K.fence()
        oB = [Buf(), Buf()]
        ssB = [Buf(), Buf()]
        for tt in range(16):
            j = tt % 2
            blk = tt // 4
            ot = sap(self.arena32, self.AR // 2, 0, 128, j * 1024, [(1, 1024)])
            for half in range(2):
                pt, pb = self.bank()
                for q in range(4):
                    fc = half * 4 + q
                    K.op("pe", lambda e, pt=pt, q=q, fc=fc: e.transpose(
                        out=sap(pt, 512, 0, 128, q * 128, [(1, 128)]),
                        in_=sap(self.xT, FC * L, 0, 128, fc * L + tt * 128, [(1, 128)]),
                        identity=self.ident[:, :]), reads=[self.xBuf[fc][blk], self.cB], writes=[pb], inc=(q == 3))
                dst = sap(self.arena32, self.AR // 2, 0, 128, j * 1024 + half * 512, [(1, 512)])
                if final:
                    K.op("dve", lambda e, dst=dst, pt=pt: e.tensor_copy(out=dst, in_=pt[:, :]), reads=[pb], writes=[oB[j]])
                else:
                    K.op("act", lambda e, dst=dst, pt=pt: e.activation(out=dst, in_=pt[:, :], func=AF.Copy), reads=[pb], writes=[oB[j]])
            if final:
                sq = sap(self.arena32, self.AR // 2, 0, 128, 2048 + j * 1024, [(1, 1024)])
                ss = sap(self.arena32, self.AR // 2, 0, 128, 4096 + j, [(1, 1)])
                K.op("act", lambda e, sq=sq, ot=ot, ss=ss: e.activation(out=sq, in_=ot, func=AF.Square, accum_out=ss),
                     reads=[oB[j]], writes=[ssB[j]])
                K.op("act", lambda e, ss=ss: e.activation(out=ss, in_=ss, func=AF.Sqrt, scale=1.0 / D, bias=EPS), reads=[ssB[j]], writes=[ssB[j]])
                K.op("dve", lambda e, ss=ss: e.reciprocal(out=ss, in_=ss), reads=[ssB[j]], writes=[ssB[j]])
                K.op("dve", lambda e, ot=ot, ss=ss: e.scalar_tensor_tensor(out=ot, in0=ot, scalar=ss, in1=self.gfin[:, :],
                                                                            op0=ALU.mult, op1=ALU.mult),
                     reads=[oB[j], ssB[j], self.cB], writes=[oB[j]])
            K.dma("sp", dap(target, (s * L + tt * 128) * D, [(D, 128), (1, D)]), ot, reads=[oB[j]], writes=[self.outB()])

    def outB(self):
        if not hasattr(self, "_outB"):
            self._outB = Buf("out")
        return self._outB

    def finish(self):
        K = self.K
        deps = {i: v for i, v in enumerate(K.dval) if v > 0}
        K._wait("sp", deps)

    def norm(self, kind, li, perm=False):
        K = self.K
        K.fence()
        self.bank_set = None
        sqB = Buf()
        rB = [Buf(), Buf()]
        goff = (kind * DEPTH + li) * FC
        for blk in range(NBLK):
            sq = sap(self.arena, self.AR, 0, 128, 0, [(TB, FC), (1, TB)])
            xin = sap(self.xT, FC * L, 0, 128, blk * TB, [(L, FC), (1, TB)])
            K.op("act", lambda e, sq=sq, xin=xin: e.activation(out=sq, in_=xin, func=AF.Square),
                 reads=[self.xBuf[f][blk] for f in range(FC)], writes=[sqB])
            pt, pb = self.bank()
            for fc in range(FC):
                K.op("pe", lambda e, pt=pt, fc=fc: e.matmul(pt[:, :], lhsT=self.onesb[:, :],
                                                            rhs=sap(self.arena, self.AR, 0, 128, fc * TB, [(1, TB)]),
                                                            start=(fc == 0), stop=(fc == FC - 1)),
                     reads=[sqB, self.cB], writes=[pb], inc=(fc == FC - 1))
            j = blk % 2
            rs = sap(self.arena32, self.AR // 2, 0, 128, 2048 + j * TB, [(1, TB)])
            K.op("act", lambda e, rs=rs, pt=pt: e.activation(out=rs, in_=pt[:, :], func=AF.Sqrt, scale=1.0 / D, bias=EPS),
                 reads=[pb], writes=[rB[j]])
            K.op("dve", lambda e, rs=rs: e.reciprocal(out=rs, in_=rs), reads=[rB[j]], writes=[rB[j]])
            for fc in range(FC):
                if perm:
                    o_ap = sap(self.hB, FC * L, 0, 128, fc * L + 16 * blk, [(1, 16), (64, 32)])
                    i_ap = sap(self.xT, FC * L, 0, 128, fc * L + blk * TB, [(32, 16), (1, 32)])
                    r_ap = sap(self.arena32, self.AR // 2, 0, 128, 2048 + j * TB, [(32, 16), (1, 32)])
                else:
                    o_ap, i_ap, r_ap = self.hap(fc, blk * TB, TB), self.xap(fc, blk), rs
                K.op("dve", lambda e, fc=fc, o_ap=o_ap, i_ap=i_ap, r_ap=r_ap: e.scalar_tensor_tensor(
                    out=o_ap, in0=i_ap,
                    scalar=sap(self.gT, 3 * DEPTH * FC, 0, 128, goff + fc, [(1, 1)]), in1=r_ap, op0=ALU.mult, op1=ALU.mult),
                     reads=[self.xBuf[fc][blk], rB[j], self.cB], writes=[self.hBuf[fc][blk]])

    def mlp(self, li):
        K = self.K
        K.fence()
        aB = [[Buf() for _ in range(NBLK)] for _ in range(8)]
        plist = []
        for hq in range(4):
            plist += [("w1_%d" % li, 2 * hq), ("w1_%d" % li, 2 * hq + 1), ("w2_%d" % li, 2 * hq), ("w2_%d" % li, 2 * hq + 1)]

        def cons(i, slot):
            hq, kind = i // 4, i % 4
            if kind < 2:
                for blk in range(NBLK):
                    for mm in range(4):
                        ml = kind * 4 + mm
                        pt, pb = self.bank()
                        for kc in range(FC):
                            K.op("pe", lambda e, pt=pt, kc=kc, mm=mm, blk=blk: e.matmul(
                                pt[:, :], lhsT=self.rslot(slot, kc * 512 + mm * 128, [(1, 128)]),
                                rhs=self.hap(kc, blk * TB, TB), start=(kc == 0), stop=(kc == FC - 1)),
                                reads=[self.ringB[slot], self.hBuf[kc][blk]], writes=[pb], inc=(kc == FC - 1))
                        a = sap(self.arena, self.AR, 0, 128, ml * L + blk * TB, [(1, TB)])
                        K.op("act", lambda e, a=a, pt=pt: e.activation(out=a, in_=pt[:, :], func=AF.Relu), reads=[pb], writes=[aB[ml][blk]])
                        eng = "pool" if (ml + blk) % 2 == 0 else "dve"
                        K.op(eng, lambda e, a=a: e.tensor_tensor(out=a, in0=a, in1=a, op=ALU.mult), reads=[aB[ml][blk]], writes=[aB[ml][blk]])
            else:
                cp = kind - 2
                for blk in range(NBLK):
                    for q in range(4):
                        fo = cp * 4 + q
                        pt, pb = self.bank()
                        for mc in range(8):
                            K.op("pe", lambda e, pt=pt, mc=mc, q=q, blk=blk: e.matmul(
                                pt[:, :], lhsT=self.rslot(slot, mc * 512 + q * 128, [(1, 128)]),
                                rhs=sap(self.arena, self.AR, 0, 128, mc * L + blk * TB, [(1, TB)]), start=(mc == 0), stop=(mc == 7)),
                                reads=[self.ringB[slot], aB[mc][blk]], writes=[pb], inc=(mc == 7))
                        K.op("dve", lambda e, pt=pt, fo=fo, blk=blk: e.tensor_tensor(out=self.xap(fo, blk), in0=pt[:, :], in1=self.xap(fo, blk), op=ALU.add),
                             reads=[pb, self.xBuf[fo][blk]], writes=[self.xBuf[fo][blk]])
        self.stream(plist, cons)

    def ple(self, s, li):
        K = self.K
        K.fence()
        A32 = self.AR // 2
        pstB = [Buf(), Buf()]
        pTB = Buf()
        gB = [Buf(), Buf()]
        tB = [Buf(), Buf()]
        for blk in range(NBLK):
            pts = [self.bank(), self.bank()]
            for t4 in range(4):
                j = t4 % 2
                stg = sap(self.arena32, A32, 0, 128, 1024 + j * 256, [(1, 256)])
                K.dma("sp", stg, dap(self.p, ((li * self.nseq + s) * L + blk * TB + t4 * 128) * PLE, [(PLE, 128), (1, PLE)]),
                      writes=[pstB[j]])
                for kc in range(2):
                    pt, pb = pts[kc]
                    K.op("pe", lambda e, pt=pt, kc=kc, j=j, t4=t4: e.transpose(
                        out=sap(pt, 512, 0, 128, t4 * 128, [(1, 128)]),
                        in_=sap(self.arena32, A32, 0, 128, 1024 + j * 256 + kc * 128, [(1, 128)]),
                        identity=self.ident[:, :]), reads=[pstB[j], self.cB], writes=[pb])
            for kc in range(2):
                pt, pb = pts[kc]
                K.op("act", lambda e, pt=pt, kc=kc: e.activation(out=sap(self.arena, self.AR, 0, 128, kc * TB, [(1, TB)]),
                                                                in_=pt[:, :], func=AF.Copy), reads=[pb], writes=[pTB])

            def cons(i, slot, blk=blk):
                if i < 2:
                    for q in range(4):
                        fo = i * 4 + q
                        pt, pb = self.bank()
                        for kc in range(FC):
                            K.op("pe", lambda e, pt=pt, kc=kc, q=q: e.matmul(
                                pt[:, :], lhsT=self.rslot(slot, kc * 512 + q * 128, [(1, 128)]),
                                rhs=self.hap(kc, blk * TB, TB), start=(kc == 0), stop=(kc == FC - 1)),
                                reads=[self.ringB[slot], self.hBuf[kc][blk]], writes=[pb], inc=(kc == FC - 1))
                        g = sap(self.arena32, A32, 0, 128, 2048 + fo * TB, [(1, TB)])
                        K.op("act", lambda e, g=g, pt=pt: e.activation(out=g, in_=pt[:, :], func=AF.Sigmoid), reads=[pb], writes=[self._gB[fo]])
                else:
                    for fo in range(FC):
                        pt, pb = self.bank()
                        for kc in range(2):
                            K.op("pe", lambda e, pt=pt, kc=kc, fo=fo: e.matmul(
                                pt[:, :], lhsT=self.rslot(slot, kc * 1024 + fo * 128, [(1, 128)]),
                                rhs=sap(self.arena, self.AR, 0, 128, kc * TB, [(1, TB)]), start=(kc == 0), stop=(kc == 1)),
                                reads=[self.ringB[slot], pTB], writes=[pb], inc=(kc == 1))
                        g = sap(self.arena32, A32, 0, 128, 2048 + fo * TB, [(1, TB)])
                        j = fo % 2
                        t = sap(self.arena32, A32, 0, 128, 6144 + j * TB, [(1, TB)])
                        K.op("dve", lambda e, t=t, pt=pt, g=g: e.tensor_tensor(out=t, in0=pt[:, :], in1=g, op=ALU.mult),
                             reads=[pb, self._gB[fo]], writes=[tB[j]])
                        K.op("pool", lambda e, t=t, fo=fo: e.tensor_tensor(out=self.xap(fo, blk), in0=self.xap(fo, blk), in1=t, op=ALU.add),
                             reads=[tB[j], self.xBuf[fo][blk]], writes=[self.xBuf[fo][blk]])
            self._gB = [Buf() for _ in range(FC)]
            self.stream([("gate_%d" % li, 0), ("gate_%d" % li, 1), ("ple_%d" % li, 0)], cons, depth=2)

    def conv(self):
        K = self.K
        K.fence()
        A32 = self.AR // 2
        ZS, CS, CV = 0, 4112, 6160
        GB = 14368
        zB = [Buf() for _ in range(FC)]
        cSB = [Buf() for _ in range(4)]
        cvB = [Buf(), Buf()]
        gB = [Buf() for _ in range(FC)]
        hlB = Buf()
        for blk in range(NBLK):
            if blk == 0:
                K.op("pool", lambda e: e.memset(sap(self.arena32, A32, 0, 128, ZS, [(514, FC), (1, 2)]), 0.0), writes=zB)
            else:
                K.op("pool", lambda e: e.tensor_copy(out=sap(self.arena32, A32, 0, 128, ZS, [(514, FC), (1, 2)]),
                                                      in_=sap(self.halo, FC * 2, 0, 128, 0, [(2, FC), (1, 2)])), reads=[hlB], writes=zB)

            def cons(i, slot, blk=blk):
                half = i // 3
                kind = i % 3
                for q in range(4):
                    fc = half * 4 + q
                    pt, pb = self.bank()
                    for kc in range(FC):
                        K.op("pe", lambda e, pt=pt, kc=kc, q=q: e.matmul(
                            pt[:, :], lhsT=self.rslot(slot, kc * 512 + q * 128, [(1, 128)]),
                            rhs=self.hap(kc, blk * TB, TB), start=(kc == 0), stop=(kc == FC - 1)),
                            reads=[self.ringB[slot], self.hBuf[kc][blk]], writes=[pb], inc=(kc == FC - 1))
                    z = sap(self.arena32, A32, 0, 128, ZS + fc * 514 + 2, [(1, TB)])
                    if kind == 0:
                        c = sap(self.arena32, A32, 0, 128, CS + q * TB, [(1, TB)])
                        K.op("act", lambda e, c=c, pt=pt: e.activation(out=c, in_=pt[:, :], func=AF.Copy), reads=[pb], writes=[cSB[q]])
                    elif kind == 1:
                        c = sap(self.arena32, A32, 0, 128, CS + q * TB, [(1, TB)])
                        K.op("dve", lambda e, z=z, pt=pt, c=c: e.tensor_tensor(out=z, in0=pt[:, :], in1=c, op=ALU.mult),
                             reads=[pb, cSB[q]], writes=[zB[fc]])
                    else:
                        j = q % 2
                        cv = sap(self.arena32, A32, 0, 128, CV + j * TB, [(1, TB)])
                        w = lambda tap, fc=fc: sap(self.cwT, 3 * FC, 0, 128, tap * FC + fc, [(1, 1)])
                        K.op("act", lambda e, cv=cv, z=z, w=w: e.activation(out=cv, in_=z, func=AF.Copy, scale=w(2)),
                             reads=[zB[fc], self.cB], writes=[cvB[j]])
                        z1 = sap(self.arena32, A32, 0, 128, ZS + fc * 514 + 1, [(1, TB)])
                        z0 = sap(self.arena32, A32, 0, 128, ZS + fc * 514 + 0, [(1, TB)])
                        K.op("dve", lambda e, cv=cv, z1=z1, w=w: e.scalar_tensor_tensor(out=cv, in0=z1, scalar=w(1), in1=cv, op0=ALU.mult, op1=ALU.add),
                             reads=[zB[fc], cvB[j], self.cB], writes=[cvB[j]])
                        K.op("dve", lambda e, cv=cv, z0=z0, w=w: e.scalar_tensor_tensor(out=cv, in0=z0, scalar=w(0), in1=cv, op0=ALU.mult, op1=ALU.add),
                             reads=[zB[fc], cvB[j], self.cB], writes=[cvB[j]])
                        g = sap(self.arena, self.AR, 0, 128, GB + fc * TB, [(1, TB)])
                        K.op("dve", lambda e, g=g, pt=pt, cv=cv: e.tensor_tensor(out=g, in0=pt[:, :], in1=cv, op=ALU.mult),
                             reads=[pb, cvB[j]], writes=[gB[fc]])
            self.stream([("cin", 2), ("cin", 4), ("cin", 0), ("cin", 3), ("cin", 5), ("cin", 1)], cons)
            K.op("pool", lambda e: e.tensor_copy(out=sap(self.halo, FC * 2, 0, 128, 0, [(2, FC), (1, 2)]),
                                                  in_=sap(self.arena32, A32, 0, 128, ZS + 512, [(514, FC), (1, 2)])), reads=zB, writes=[hlB])

            def cons2(i, slot, blk=blk):
                for q in range(4):
                    fo = i * 4 + q
                    pt, pb = self.bank()
                    for kc in range(FC):
                        K.op("pe", lambda e, pt=pt, kc=kc, q=q: e.matmul(
                            pt[:, :], lhsT=self.rslot(slot, kc * 512 + q * 128, [(1, 128)]),
                            rhs=sap(self.arena, self.AR, 0, 128, GB + kc * TB, [(1, TB)]), start=(kc == 0), stop=(kc == FC - 1)),
                            reads=[self.ringB[slot], gB[kc]], writes=[pb], inc=(kc == FC - 1))
                    K.op("dve", lambda e, pt=pt, fo=fo: e.tensor_tensor(out=self.xap(fo, blk), in0=pt[:, :], in1=self.xap(fo, blk), op=ALU.add),
                         reads=[pb, self.xBuf[fo][blk]], writes=[self.xBuf[fo][blk]])
            self.stream([("cout", 0), ("cout", 1)], cons2)

    def fox(self):
        K = self.K
        K.fence()
        A32 = self.AR // 2
        AR = self.AR
        QT, KT, VV, OG, PT = 0, 4096, 8192, 12352, 14400
        CUM, CTM, BIA, OSB, RRW = 8192, 10240, 10496, 11520, 11776
        cumB, ctmB, biaB = Buf(), Buf(), Buf()
        allh = lambda blk: [self.hBuf[f][blk] for f in range(FC)]
        cum = lambda c0, n: sap(self.arena32, A32, 0, 16, CUM + c0, [(1, n)])
        for blk in range(NBLK):
            pt, pb = self.bank()
            for kc in range(FC):
                K.op("pe", lambda e, pt=pt, kc=kc, blk=blk: e.matmul(
                    sap(pt, 512, 0, 16, 0, [(1, TB)]), lhsT=sap(self.wfT, 128, 0, 128, kc * 16, [(1, 16)]),
                    rhs=self.hap(kc, blk * TB, TB), start=(kc == 0), stop=(kc == FC - 1)),
                    reads=[self.cB, self.hBuf[kc][blk]], writes=[pb], inc=(kc == FC - 1))
            K.op("act", lambda e, pt=pt, blk=blk: e.activation(out=cum(blk * TB, TB), in_=sap(pt, 512, 0, 16, 0, [(1, TB)]),
                                                              func=AF.Exp, scale=-1.0, bias=sap(self.nbf, 1, 0, 16, 0, [(1, 1)])),
                 reads=[pb, self.cB], writes=[cumB])
        K.op("act", lambda e: e.activation(out=cum(0, L), in_=cum(0, L), func=AF.Ln, scale=1.0, bias=1.0), reads=[cumB], writes=[cumB])
        K.op("dve", lambda e: e.tensor_scalar(out=cum(0, L), in0=cum(0, L), scalar1=-1.0, scalar2=None, op0=ALU.mult), reads=[cumB], writes=[cumB])
        K.op("dve", lambda e: e.tensor_tensor_scan(out=cum(0, L), data0=sap(self.onesf, 128, 0, 16, 0, [(0, L)]), data1=cum(0, L),
                                                   initial=0.0, op0=ALU.mult, op1=ALU.add), reads=[cumB, self.cB], writes=[cumB])
        pt, pb = self.bank()
        for tt in range(16):
            K.op("pe", lambda e, pt=pt, tt=tt: e.transpose(out=sap(pt, 512, 0, 128, tt * 16, [(1, 16)]), in_=cum(tt * 128, 128),
                                                          identity=sap(self.ident, 128, 0, 16, 0, [(1, 16)])),
                 reads=[cumB, self.cB], writes=[pb], inc=(tt == 15))
        K.op("dve", lambda e, pt=pt: e.tensor_copy(out=sap(self.arena32, A32, 0, 128, CTM, [(1, 256)]), in_=sap(pt, 512, 0, 128, 0, [(1, 256)])),
             reads=[pb], writes=[ctmB])
        rhsD = sap(self.arena32, A32, 0, 16, OSB, [(1, 256)])
        K.op("dve", lambda e: e.tensor_tensor(out=sap(self.arena32, A32, 0, 16, OSB, [(16, 16), (1, 16)]),
                                              in0=sap(self.ident, 128, 0, 16, 0, [(0, 16), (1, 16)]),
                                              in1=sap(self.arena32, A32, 0, 16, CUM, [(128, 16), (0, 16)]), op=ALU.mult),
             reads=[cumB, self.cB], writes=[biaB])
        ptc, pbc = self.bank()
        K.op("pe", lambda e: e.matmul(sap(ptc, 512, 0, 128, 0, [(1, 256)]), lhsT=sap(self.onesf, 128, 0, 16, 0, [(1, 128)]), rhs=rhsD,
                                      start=True, stop=True), reads=[biaB, self.cB], writes=[pbc])
        BIH = CUM
        for hf in range(8):
            K.op("dve", lambda e, hf=hf: e.tensor_tensor(out=sap(self.arena32, A32, 0, 128, BIH + hf * 256, [(16, 16), (1, 16)]),
                                                         in0=sap(ptc, 512, 0, 128, (2 * hf + 1) * 16, [(0, 16), (1, 16)]),
                                                         in1=sap(self.arena32, A32, 0, 128, CTM, [(16, 16), (1, 16)]), op=ALU.subtract),
                 reads=[pbc, ctmB, cumB], writes=[biaB, cumB])
        qB, kB, vB = Buf(), Buf(), Buf()
        pTB = [Buf(), Buf(), Buf()]
        oGB = [Buf(), Buf()]
        osB, rrB = Buf(), Buf()
        npt = 0
        for hg in range(4):
            K.op("pool", lambda e: e.memset(sap(self.arena, AR, 0, 128, VV + 64, [(65, 64), (1, 1)]), 1.0), reads=[vB], writes=[vB])

            def consqkv(i, slot, hg=hg):
                if i < 2:
                    base, bb = (QT, qB) if i == 0 else (KT, kB)
                    for blk in range(NBLK):
                        for c2 in range(2):
                            pt, pb = self.bank()
                            for kc in range(FC):
                                K.op("pe", lambda e, pt=pt, kc=kc, c2=c2, blk=blk: e.matmul(
                                    pt[:, :], lhsT=self.rslot(slot, kc * 256 + c2 * 128, [(1, 128)]),
                                    rhs=self.hap(kc, blk * TB, TB), start=(kc == 0), stop=(kc == FC - 1)),
                                    reads=[self.ringB[slot], self.hBuf[kc][blk]], writes=[pb], inc=(kc == FC - 1))
                            dst = sap(self.arena, AR, 0, 128, base + c2 * L + blk * TB, [(1, TB)])
                            if (blk + c2) % 2 == 0:
                                K.op("act", lambda e, dst=dst, pt=pt: e.activation(out=dst, in_=pt[:, :], func=AF.Copy), reads=[pb], writes=[bb])
                            else:
                                K.op("dve", lambda e, dst=dst, pt=pt: e.tensor_copy(out=dst, in_=pt[:, :]), reads=[pb], writes=[bb])
                else:
                    for tt in range(16):
                        pt, pb = self.bank()
                        for kc in range(FC):
                            K.op("pe", lambda e, pt=pt, kc=kc, tt=tt: e.matmul(
                                sap(pt, 512, 0, 128, 0, [(1, 256)]), lhsT=self.hap(kc, tt * 128, 128),
                                rhs=self.rslot(slot, kc * 256, [(1, 256)]), start=(kc == 0), stop=(kc == FC - 1)),
                                reads=[self.ringB[slot], self.hBuf[kc][tt // 4]], writes=[pb], inc=(kc == FC - 1))
                        dst = sap(self.arena, AR, 0, 128, VV + tt * 260, [(65, 4), (1, 64)])
                        src = sap(pt, 512, 0, 128, 0, [(64, 4), (1, 64)])
                        if tt % 2 == 0:
                            K.op("act", lambda e, dst=dst, src=src: e.activation(out=dst, in_=src, func=AF.Copy), reads=[pb], writes=[vB])
                        else:
                            K.op("dve", lambda e, dst=dst, src=src: e.tensor_copy(out=dst, in_=src), reads=[pb], writes=[vB])
            self.stream([("fin", hg), ("fin", 4 + hg), ("fin", 8 + hg)], consqkv)
            wslot = self.ring_load("fout", hg)
            self.bank_set = [0, 1, 2, 3, 4]
            nhead = 0
            tiles = [(qb, h4, j) for qb in range(4) for h4 in range(4) for j in range(4 * qb + 4)]

            def geom(qb, h4, j):
                r = j - 4 * qb
                c0 = 128 * r if r > 0 else 0
                return r, c0, TB - c0, h4 // 2, 64 * (h4 % 2)

            def emit_qk(qb, h4, j):
                r, c0, n, c2, ph = geom(qb, h4, j)
                self._sb = (getattr(self, "_sb", 0) + 1) % 3
                pst, psb = self.fbank(self._sb)
                K.op("pe", lambda e: e.matmul(
                    sap(pst, 512, 0, 128, c0, [(1, n)]),
                    lhsT=sap(self.arena, AR, ph, 64, KT + c2 * L + j * 128, [(1, 128)]),
                    rhs=sap(self.arena, AR, ph, 64, QT + c2 * L + qb * TB + c0, [(1, n)]), start=True, stop=True),
                    reads=[qB, kB], writes=[psb])
                return pst, psb

            LA = 1
            nxt = emit_qk(*tiles[0]) if LA else None
            for ti, (qb, h4, j) in enumerate(tiles):
                if LA:
                    pst, psb = nxt
                    if ti + 1 < len(tiles):
                        nxt = emit_qk(*tiles[ti + 1])
                else:
                    pst, psb = emit_qk(qb, h4, j)
                r, c0, n, c2, ph = geom(qb, h4, j)
                og = OG + (qb % 2) * 1024
                h = hg * 4 + h4
                nj = 4 * qb + 4
                if j == 0:
                    po, pob = self.fbank(6 + nhead % 2)
                    nhead += 1
                pj = npt % 3
                npt += 1
                pT = sap(self.arena, AR, 0, 128, PT + pj * TB + c0, [(1, n)])
                for hq in range(2):
                    lo = max(c0, 256 * hq)
                    hi = 256 * hq + 256
                    if lo >= hi:
                        continue
                    sub = sap(self.arena, AR, 0, 128, PT + pj * TB + lo, [(1, hi - lo)])
                    K.op("act", lambda e, sub=sub, pst=pst, lo=lo, hi=hi, hq=hq, qb=qb, j=j, h=h: e.activation(
                        out=sub, in_=sap(pst, 512, 0, 128, lo, [(1, hi - lo)]), func=AF.Exp, scale=0.125,
                        bias=sap(self.arena32, A32, 0, 128, BIH + (2 * qb + hq) * 256 + j * 16 + h, [(1, 1)])),
                        reads=[psb, biaB], writes=[pTB[pj]])
                if r >= 0:
                    pd = sap(self.arena, AR, 0, 128, PT + pj * TB + 128 * r, [(1, 128)])
                    K.op("pool", lambda e, pd=pd: e.tensor_tensor(out=pd, in0=pd, in1=self.trib[:, :], op=ALU.mult),
                         reads=[pTB[pj], self.cB], writes=[pTB[pj]])
                K.op("pe", lambda e, po=po, pT=pT, j=j, c0=c0, n=n, h4=h4, nj=nj: e.matmul(
                    sap(po, 512, 0, 65, c0, [(1, n)]),
                    lhsT=sap(self.arena, AR, 0, 128, VV + j * 260 + h4 * 65, [(1, 65)]), rhs=pT,
                    start=(j == 0), stop=(j == nj - 1)), reads=[vB, pTB[pj]], writes=[pob], inc=True)
                if j < nj - 1:
                    continue
                rr = sap(self.arena32, A32, 64, 1, RRW, [(1, TB)])
                K.op("dve", lambda e, rr=rr, po=po: e.reciprocal(out=rr, in_=sap(po, 512, 64, 1, 0, [(1, TB)])), reads=[pob], writes=[rrB])
                osb = sap(self.arena32, A32, 0, 64, OSB, [(1, TB)])
                K.op("act", lambda e, osb=osb, po=po: e.activation(out=osb, in_=sap(po, 512, 0, 64, 0, [(1, TB)]), func=AF.Copy),
                     reads=[pob, biaB], writes=[osB])
                pr, prb = self.fbank(5)
                K.op("pe", lambda e, pr=pr, rr=rr: e.matmul(sap(pr, 512, 0, 64, 0, [(1, TB)]),
                                                           lhsT=sap(self.onesf, 128, 64, 1, 0, [(1, 64)]), rhs=rr, start=True, stop=True),
                     reads=[rrB, self.cB], writes=[prb])
                K.op("dve", lambda e, osb=osb, pr=pr, og=og, c2=c2, ph=ph: e.tensor_tensor(
                    out=sap(self.arena, AR, ph, 64, og + c2 * TB, [(1, TB)]), in0=osb, in1=sap(pr, 512, 0, 64, 0, [(1, TB)]), op=ALU.mult),
                    reads=[osB, prb], writes=[oGB[qb % 2]])
                if h4 < 3:
                    continue
                for fo in range(FC):
                    pt, pb = self.fbank(3 + fo % 2)
                    for kc in range(2):
                        K.op("pe", lambda e, pt=pt, kc=kc, fo=fo, og=og: e.matmul(
                            pt[:, :], lhsT=self.rslot(wslot, kc * 1024 + fo * 128, [(1, 128)]),
                            rhs=sap(self.arena, AR, 0, 128, og + kc * TB, [(1, TB)]), start=(kc == 0), stop=(kc == 1)),
                            reads=[self.ringB[wslot], oGB[qb % 2]], writes=[pb], inc=(kc == 1))
                    K.op("dve", lambda e, pt=pt, fo=fo, qb=qb: e.tensor_tensor(out=self.xap(fo, qb), in0=pt[:, :], in1=self.xap(fo, qb), op=ALU.add),
                         reads=[pb, self.xBuf[fo][qb]], writes=[self.xBuf[fo][qb]])

    def s5_prologue(self, sl):
        K = self.K
        K.fence()
        X = self.xT
        RS = FC * L
        B = Buf("s5pro")
        o = [0]

        def T(n):
            a = o[0]
            o[0] += n
            return a
        t = lambda off, n=32: sap(X, RS, 0, 128, off, [(1, n)])
        lr, li_, ldt, dt, th, mg, s16, c16 = (T(32) for _ in range(8))
        er, ei, r2, m2, nm, den, cr, ci, nr, t1, t2 = (T(32) for _ in range(11))
        for (src, dst) in ((self.ssm_lam_re, lr), (self.ssm_lam_im, li_)):
            for g2 in range(2):
                K.dma("sp", sap(X, RS, 64 * g2, 64, dst, [(1, 32)]), dap(src, sl * 4096 + g2 * 64, [(1, 64), (128, 32)]), writes=[B], slow=True)
        for g2 in range(2):
            K.dma("sp", sap(X, RS, 64 * g2, 64, ldt, [(1, 32)]), dap(self.ssm_log_dt, sl * 64 + g2, [(0, 64), (2, 32)]), writes=[B], slow=True)
        A = lambda fn: K.op("act", fn, reads=[B], writes=[B])
        V = lambda fn: K.op("dve", fn, reads=[B], writes=[B])
        tt_ = lambda o_, a, b, op, n=32: V(lambda e: e.tensor_tensor(out=t(o_, n), in0=t(a, n), in1=t(b, n), op=op))
        A(lambda e: e.activation(out=t(dt), in_=t(ldt), func=AF.Exp))
        tt_(th, li_, dt, ALU.mult)
        tt_(mg, lr, dt, ALU.mult)
        V(lambda e: e.tensor_scalar(out=t(th), in0=t(th), scalar1=1.0 / 16, scalar2=None, op0=ALU.mult))
        V(lambda e: e.tensor_scalar(out=t(mg), in0=t(mg), scalar1=1.0 / 16, scalar2=None, op0=ALU.mult))
        uu = T(32)
        acc = T(32)
        tt_(uu, th, th, ALU.mult)

        def horner(dst, coefs, var):
            V(lambda e: e.tensor_scalar(out=t(acc), in0=t(var), scalar1=float(coefs[-1]), scalar2=None, op0=ALU.mult))
            for c in coefs[-2:0:-1]:
                V(lambda e, c=c: e.scalar_tensor_tensor(out=t(acc), in0=t(acc), scalar=float(c), in1=t(var), op0=ALU.add, op1=ALU.mult))
            V(lambda e: e.tensor_scalar(out=t(dst), in0=t(acc), scalar1=float(coefs[0]), scalar2=None, op0=ALU.add))
        f = math.factorial
        horner(s16, [(-1.0) ** k / f(2 * k + 1) for k in range(7)], uu)
        tt_(s16, s16, th, ALU.mult)
        horner(c16, [(-1.0) ** k / f(2 * k) for k in range(7)], uu)
        y_ = T(32)
        V(lambda e: e.tensor_copy(out=t(y_), in_=t(mg)))
        horner(mg, [1.0 / f(k) for k in range(5)], y_)
        tt_(er, mg, c16, ALU.mult)
        tt_(ei, mg, s16, ALU.mult)

        def square():
            tt_(r2, er, er, ALU.mult)
            tt_(m2, ei, ei, ALU.mult)
            tt_(nm, er, ei, ALU.mult)
            tt_(er, r2, m2, ALU.subtract)
            V(lambda e: e.tensor_scalar(out=t(ei), in0=t(nm), scalar1=2.0, scalar2=None, op0=ALU.mult))
        for _ in range(4):
            square()
        V(lambda e: e.tensor_scalar(out=t(nr), in0=t(er), scalar1=-1.0, scalar2=None, op0=ALU.add))
        tt_(t1, lr, lr, ALU.mult)
        tt_(t2, li_, li_, ALU.mult)
        tt_(den, t1, t2, ALU.add)
        V(lambda e: e.reciprocal(out=t(den), in_=t(den)))
        tt_(t1, nr, lr, ALU.mult)
        tt_(t2, ei, li_, ALU.mult)
        tt_(cr, t1, t2, ALU.add)
        tt_(cr, cr, den, ALU.mult)
        tt_(t1, ei, lr, ALU.mult)
        tt_(t2, nr, li_, ALU.mult)
        tt_(ci, t1, t2, ALU.subtract)
        tt_(ci, ci, den, ALU.mult)
        for k in range(KS_STEPS):
            ap_ = lambda c, k=k: sap(self.Apow, 2 * KS_STEPS * 3 * 32, 0, 128, ((sl * KS_STEPS + k) * 3 + c) * 32, [(1, 32)])
            V(lambda e, ap_=ap_: e.tensor_copy(out=ap_(0), in_=t(er)))
            V(lambda e, ap_=ap_: e.tensor_copy(out=ap_(1), in_=t(ei)))
            V(lambda e, ap_=ap_: e.tensor_scalar(out=ap_(2), in0=t(ei), scalar1=-1.0, scalar2=None, op0=ALU.mult))
            if k < KS_STEPS - 1:
                square()
        Bn = [T(512), T(512)]
        for part, src in enumerate((self.ssm_b_re, self.ssm_b_im)):
            K.dma("sp", sap(X, RS, 0, 128, Bn[part], [(16, 32), (1, 16)]), dap(src, sl * 65536, [(16, 128), (2048, 32), (1, 16)]), writes=[B])
        Bb = [T(512), T(512)]
        u1, u2 = T(512), T(512)
        b3 = lambda off: sap(X, RS, 0, 128, off, [(16, 32), (1, 16)])
        cb = lambda off: sap(X, RS, 0, 128, off, [(1, 32), (0, 16)])
        V(lambda e: e.tensor_tensor(out=b3(u1), in0=b3(Bn[0]), in1=cb(cr), op=ALU.mult))
        V(lambda e: e.tensor_tensor(out=b3(u2), in0=b3(Bn[1]), in1=cb(ci), op=ALU.mult))
        V(lambda e: e.tensor_tensor(out=b3(Bb[0]), in0=b3(u1), in1=b3(u2), op=ALU.subtract))
        V(lambda e: e.tensor_tensor(out=b3(u1), in0=b3(Bn[1]), in1=cb(cr), op=ALU.mult))
        V(lambda e: e.tensor_tensor(out=b3(u2), in0=b3(Bn[0]), in1=cb(ci), op=ALU.mult))
        V(lambda e: e.tensor_tensor(out=b3(Bb[1]), in0=b3(u1), in1=b3(u2), op=ALU.add))
        Bd = [T(1024), T(1024)]
        self._s5off = dict(Bd=Bd, T=T, B=B)
        for part in range(2):
            V(lambda e, part=part: e.memset(t(Bd[part], 1024), 0.0))
            for g2 in range(2):
                V(lambda e, part=part, g2=g2: e.tensor_copy(out=sap(X, RS, 64 * g2, 64, Bd[part] + 16 * g2, [(32, 32), (1, 16)]),
                                                            in_=sap(X, RS, 64 * g2, 64, Bb[part], [(16, 32), (1, 16)])))
            for fc in range(FC):
                pt, pb = self.bank()
                K.op("pe", lambda e, pt=pt, part=part, fc=fc: e.transpose(out=sap(pt, 512, 0, 128, 0, [(1, 128)]),
                                                                         in_=t(Bd[part] + fc * 128, 128), identity=self.ident[:, :]),
                     reads=[B, self.cB], writes=[pb])
                K.op("act", lambda e, pt=pt, part=part, fc=fc: e.activation(
                    out=sap(self.BT, 2 * 2 * FC * 128, 0, 128, ((sl * 2 + part) * FC + fc) * 128, [(1, 128)]),
                    in_=sap(pt, 512, 0, 128, 0, [(1, 128)]), func=AF.Copy), reads=[pb], writes=[self.cB])
        CT = [T(1024), T(1024)]
        self._s5off["CT"] = CT
        for part, src in enumerate((self.ssm_c_re, self.ssm_c_im)):
            V(lambda e, part=part: e.memset(t(CT[part], 1024), 0.0))
            for gpl in range(4):
                for g2 in range(2):
                    p0 = 32 * gpl + 16 * g2
                    K.dma("sp", sap(X, RS, p0, 16, CT[part] + 64 * g2, [(128, FC), (1, 64)]),
                          dap(src, sl * 65536 + (2 * gpl + g2) * 1024, [(64, 16), (8192, FC), (1, 64)]), reads=[B], writes=[B])
            for fc in range(FC):
                pt, pb = self.bank()
                K.op("pe", lambda e, pt=pt, part=part, fc=fc: e.transpose(out=sap(pt, 512, 0, 128, 0, [(1, 128)]),
                                                                         in_=t(CT[part] + fc * 128, 128), identity=self.ident[:, :]),
                     reads=[B, self.cB], writes=[pb])
                K.op("act", lambda e, pt=pt, part=part, fc=fc: e.activation(
                    out=sap(self.Cd, 2 * 2 * 1024, 0, 128, (sl * 2 + part) * 1024 + fc * 128, [(1, 128)]),
                    in_=sap(pt, 512, 0, 128, 0, [(1, 128)]), func=AF.Copy, scale=(1.0 if part == 0 else -1.0)), reads=[pb], writes=[self.cB])

    def s5_prologue2(self, sl):
        K = self.K
        nc = self.nc
        X = self.xT
        RS = FC * L
        T = self._s5off["T"]
        Bd = self._s5off["Bd"]
        B = self._s5off["B"]
        A32 = self.AR // 2
        for nm, npc in (("s5A_%d" % sl, 16), ("s5B_%d" % sl, 16), ("s5C_%d" % sl, 8)):
            t_ = nc.dram_tensor("ws_" + nm, [npc * 128, 4096], BF16, kind="Internal")
            self.wscr[nm] = dict(t=t_, npieces=npc, pe=4096)
        V = lambda fn, r=(), w=(): K.op("dve", fn, reads=[B] + list(r), writes=[B] + list(w))
        G = lambda fn, r=(), w=(): K.op("pool", fn, reads=[B] + list(r), writes=[B] + list(w))
        APW = 2 * KS_STEPS * 3 * 32
        apw = lambda k, c, dims: sap(self.Apow, APW, 0, 128, ((sl * KS_STEPS + k) * 3 + c) * 32, dims)
        PW = T(33 * 64)
        pw = lambda t0, c, dims: sap(X, RS, 0, 128, PW + t0 * 64 + c * 32, dims)
        tq1, tq2 = T(16 * 32), T(16 * 32)
        tq = lambda off, d: sap(X, RS, 0, 128, off, [(32, d), (1, 32)])
        V(lambda e: e.memset(pw(0, 0, [(1, 32)]), 1.0))
        V(lambda e: e.memset(pw(0, 1, [(1, 32)]), 0.0))
        V(lambda e: e.tensor_copy(out=pw(1, 0, [(1, 32)]), in_=apw(0, 0, [(1, 32)])))
        V(lambda e: e.tensor_copy(out=pw(1, 1, [(1, 32)]), in_=apw(0, 1, [(1, 32)])))
        for k in range(1, 5):
            d = 1 << k
            pr = pw(0, 0, [(64, d), (1, 32)])
            pi = pw(0, 1, [(64, d), (1, 32)])
            ar = apw(k, 0, [(0, d), (1, 32)])
            ai = apw(k, 1, [(0, d), (1, 32)])
            V(lambda e, pr=pr, ar=ar, d=d: e.tensor_tensor(out=tq(tq1, d), in0=pr, in1=ar, op=ALU.mult))
            V(lambda e, pi=pi, ai=ai, d=d: e.tensor_tensor(out=tq(tq2, d), in0=pi, in1=ai, op=ALU.mult))
            V(lambda e, d=d: e.tensor_tensor(out=pw(d, 0, [(64, d), (1, 32)]), in0=tq(tq1, d), in1=tq(tq2, d), op=ALU.subtract))
            V(lambda e, pr=pr, ai=ai, d=d: e.tensor_tensor(out=tq(tq1, d), in0=pr, in1=ai, op=ALU.mult))
            V(lambda e, pi=pi, ar=ar, d=d: e.tensor_tensor(out=tq(tq2, d), in0=pi, in1=ar, op=ALU.mult))
            V(lambda e, d=d: e.tensor_tensor(out=pw(d, 1, [(64, d), (1, 32)]), in0=tq(tq1, d), in1=tq(tq2, d), op=ALU.add))
        V(lambda e: e.tensor_copy(out=pw(32, 0, [(1, 32)]), in_=apw(5, 0, [(1, 32)])))
        V(lambda e: e.tensor_copy(out=pw(32, 1, [(1, 32)]), in_=apw(5, 1, [(1, 32)])))
        BdB = T(1024)
        Xb = X.bitcast(BF16)
        bdb = lambda part, gp: sap(Xb, 2 * RS, 0, 128, 2 * BdB + part * 1024 + gp * 32, [(1, 32)])
        for part in range(2):
            V(lambda e, part=part: e.tensor_copy(out=sap(Xb, 2 * RS, 0, 128, 2 * BdB + part * 1024, [(1, 1024)]),
                                                 in_=sap(X, RS, 0, 128, Bd[part], [(1, 1024)])))
        YN = T(4224)
        xr = lambda dims, off=0: sap(self.hB32, FC * L // 2, 0, 128, off, dims)
        t1 = lambda dims: sap(self.arena32, A32, 0, 128, 0, dims)
        t2 = lambda dims: sap(self.arena32, A32, 0, 128, 4224, dims)
        yr = lambda dims, off=0: sap(self.ring32, 2 * 4096, 0, 128, off, dims)
        yn = lambda dims, off=0: sap(X, RS, 0, 128, YN + off, dims)
        WAst = lambda dims, off=0: sap(self.ring, 4 * 4096, 0, 128, 8448 + off, dims)
        WBst = lambda dims, off=0: sap(Xb, 2 * RS, 0, 128, 2 * self._s5off["CT"][0] + off, dims)
        WCO = 16896
        WCst = lambda dims, off=0: sap(self.arena, self.AR, 0, 128, WCO + off, dims)
        xB_, t1B, t2B, yB_, wAB, wBB, wCB, ycB = (Buf() for _ in range(8))
        cdap = lambda part, fc: sap(self.Cd, 2 * 2 * 1024, 0, 128, (sl * 2 + part) * 1024 + fc * 128, [(32, 4), (0, 33), (1, 32)])
        G(lambda e: e.memset(WCst([(1, 4096)]), 0.0), w=[wCB])
        full = [(1, 4096)]
        f33 = [(1, 4224)]
        for fc in range(FC):
            bdd = lambda part: sap(X, RS, 0, 128, Bd[part] + fc * 128, [(32, 4), (0, 32), (1, 32)])
            pwd = lambda c, nt: sap(X, RS, 0, 128, PW + c * 32 + 4 * fc, [(1, 4), (64, nt), (0, 32)])
            bdx = lambda part: sap(X, RS, 0, 128, Bd[part] + fc * 128, [(0, 32), (32, 4), (1, 32)])
            pwx = lambda c: sap(X, RS, 0, 128, PW + c * 32 + 4 * fc, [(64, 32), (1, 4), (0, 32)])
            V(lambda e: e.tensor_tensor(out=t1(full), in0=bdx(0), in1=pwx(0), op=ALU.mult), w=[t1B])
            G(lambda e: e.tensor_tensor(out=t2(full), in0=bdx(1), in1=pwx(1), op=ALU.mult), w=[t2B])
            V(lambda e: e.tensor_tensor(out=xr(full), in0=t1(full), in1=t2(full), op=ALU.subtract), r=[t1B, t2B], w=[xB_, ycB])
            V(lambda e: e.tensor_tensor(out=t1(full), in0=bdx(1), in1=pwx(0), op=ALU.mult), w=[t1B])
            G(lambda e: e.tensor_tensor(out=t2(full), in0=bdx(0), in1=pwx(1), op=ALU.mult), w=[t2B])
            V(lambda e: e.tensor_tensor(out=xr(full, 4096), in0=t1(full), in1=t2(full), op=ALU.add), r=[t1B, t2B], w=[xB_, ycB])
            for part in range(2):
                for tb in range(8):
                    pt, pb = self.bank()
                    for q in range(4):
                        t = tb * 4 + q
                        K.op("pe", lambda e, pt=pt, q=q, t=t, part=part: e.transpose(
                            out=sap(pt, 512, 0, 128, q * 128, [(1, 128)]),
                            in_=xr([(1, 128)], part * 4096 + t * 128), identity=self.ident[:, :]),
                            reads=[xB_, self.cB], writes=[pb], inc=(q == 3))
                    if tb % 2 == 0:
                        K.op("act", lambda e, pt=pt, tb=tb: e.activation(out=WAst([(1, 512)], tb * 512), in_=pt[:, :], func=AF.Copy),
                             reads=[pb], writes=[wAB])
                    else:
                        K.op("dve", lambda e, pt=pt, tb=tb: e.tensor_copy(out=WAst([(1, 512)], tb * 512), in_=pt[:, :]), reads=[pb], writes=[wAB])
                wa = self.wscr["s5A_%d" % sl]
                K.dma("sp", dap(wa["t"], (fc * 2 + part) * 128 * 4096, [(4096, 128), (1, 4096)]), WAst(full), reads=[wAB], writes=[self.wB("s5A_%d" % sl)])
            V(lambda e: e.tensor_tensor(out=t1(f33), in0=cdap(0, fc), in1=pwd(0, 33), op=ALU.mult), w=[t1B])
            G(lambda e: e.tensor_tensor(out=t2(f33), in0=cdap(1, fc), in1=pwd(1, 33), op=ALU.mult), w=[t2B])
            V(lambda e: e.tensor_tensor(out=yr(f33), in0=t1(f33), in1=t2(f33), op=ALU.add), r=[t1B, t2B], w=[yB_])
            V(lambda e: e.tensor_tensor(out=t1(f33), in0=cdap(1, fc), in1=pwd(0, 33), op=ALU.mult), w=[t1B])
            G(lambda e: e.tensor_tensor(out=t2(f33), in0=cdap(0, fc), in1=pwd(1, 33), op=ALU.mult), w=[t2B])
            V(lambda e: e.tensor_tensor(out=yn(f33), in0=t1(f33), in1=t2(f33), op=ALU.subtract), r=[t1B, t2B], w=[yB_])
            for part in range(2):
                ysrc = yr if part == 0 else yn
                K.op("act", lambda e, ysrc=ysrc: e.activation(out=WBst([(1024, 4), (1, 1024)]), in_=ysrc([(33 * 32, 4), (1, 1024)], 32), func=AF.Copy),
                     reads=[yB_], writes=[wBB])
                wb = self.wscr["s5B_%d" % sl]
                K.dma("sp", dap(wb["t"], (fc * 2 + part) * 128 * 4096, [(4096, 128), (1, 4096)]), WBst(full), reads=[wBB], writes=[self.wB("s5B_%d" % sl)])
                G(lambda e, ysrc=ysrc, part=part: e.tensor_copy(out=sap(self.hB, FC * L, 0, 128, part * 4096, [(1024, 4), (1, 1024)]),
                                                                in_=ysrc([(33 * 32, 4), (1, 1024)], 0)), r=[yB_, xB_], w=[ycB, xB_])
            pA, pAb = self.bank()
            pBk, pBb = self.bank()
            for gpl in range(4):
                gp = 4 * fc + gpl
                for half, (pt, pb) in enumerate(((pA, pAb), (pBk, pBb))):
                    for part in range(2):
                        K.op("pe", lambda e, pt=pt, part=part, gpl=gpl, gp=gp, half=half: e.matmul(
                            sap(pt, 512, 32 * gpl, 32, 0, [(1, 512)]), lhsT=bdb(part, gp),
                            rhs=sap(self.hB, FC * L, 0, 128, part * 4096 + gpl * 1024 + half * 512, [(1, 512)]),
                            start=(part == 0), stop=(part == 1), tile_position=(0, 32 * gpl)),
                            reads=[ycB, B], writes=[pb], inc=(part == 1))
            for gpl in range(4):
                for half, (pt, pb) in enumerate(((pA, pAb), (pBk, pBb))):
                    eng = "act" if half == 0 else "dve"
                    dst = WCst([(128, 16), (1, 32)], half * 16 * 128 + 32 * gpl)
                    dst = sap(self.arena, self.AR, 32 * gpl, 32, WCO + half * 16 * 128 + 32 * gpl, [(128, 16), (1, 32)])
                    src = sap(pt, 512, 32 * gpl, 32, 0, [(32, 16), (1, 32)])
                    if eng == "act":
                        K.op("act", lambda e, dst=dst, src=src: e.activation(out=dst, in_=src, func=AF.Copy), reads=[pb], writes=[wCB])
                    else:
                        K.op("dve", lambda e, dst=dst, src=src: e.tensor_copy(out=dst, in_=src), reads=[pb], writes=[wCB])
            wc = self.wscr["s5C_%d" % sl]
            K.dma("sp", dap(wc["t"], fc * 128 * 4096, [(4096, 128), (1, 4096)]), sap(self.arena, self.AR, 0, 128, WCO, full),
                  reads=[wCB], writes=[self.wB("s5C_%d" % sl)])

    def s5(self, sl, glu=True):
        self._s5glu = glu
        K = self.K
        K.fence()
        A32 = self.AR // 2
        AR = self.AR
        RE, IM, T1, T2, T3 = 0, 2048, 4096, 6144, 8192
        SBF = 20480
        APW = 2 * KS_STEPS * 3 * 32
        sB, t1B, t2B, t3B, sbfB, taB = (Buf() for _ in range(6))
        f32 = lambda off, dims: sap(self.arena32, A32, 0, 128, off, dims)
        def consA(i, slot):
            fc, part = i // 2, i % 2
            for gpl in range(4):
                pt, pb = self.bank()
                for j in range(32):
                    K.op("pe", lambda e, pt=pt, gpl=gpl, j=j, fc=fc: e.matmul(
                        sap(pt, 512, 0, 128, 0, [(1, 64)]),
                        lhsT=self.rslot(slot, (31 - j) * 128, [(1, 128)], p0=32 * gpl, pn=32),
                        rhs=sap(self.hB, FC * L, 32 * gpl, 32, fc * L + j * 64, [(1, 64)]),
                        start=(j == 0), stop=(j == 31), tile_position=(32 * gpl, 0)),
                        reads=[self.ringB[slot], self.hBuf[fc][0], self.hBuf[fc][1], self.hBuf[fc][2], self.hBuf[fc][3]],
                        writes=[pb], inc=(j == 31))
                dst = f32(part * 2048 + (4 * fc + gpl) * 64, [(1, 64)])
                if gpl % 2 == 0:
                    K.op("act", lambda e, dst=dst, pt=pt: e.activation(out=dst, in_=sap(pt, 512, 0, 128, 0, [(1, 64)]), func=AF.Copy), reads=[pb], writes=[sB])
                else:
                    K.op("dve", lambda e, dst=dst, pt=pt: e.tensor_copy(out=dst, in_=sap(pt, 512, 0, 128, 0, [(1, 64)])), reads=[pb], writes=[sB])
        self.stream([("s5A_%d" % sl, pc) for pc in range(16)], consA)
        import os
        PH = "123"
        if "2" not in PH:
            return
        for k in range(6):
            d = 1 << k
            n = 64 - d
            ar = sap(self.Apow, APW, 0, 128, ((sl * KS_STEPS + k + 5) * 3 + 0) * 32, [(1, 32), (0, n)])
            ai = sap(self.Apow, APW, 0, 128, ((sl * KS_STEPS + k + 5) * 3 + 1) * 32, [(1, 32), (0, n)])
            v = lambda off, sh=0, n=n: f32(off + sh, [(64, 32), (1, n)])
            K.op("dve", lambda e, v=v, ar=ar: e.tensor_tensor(out=v(T1), in0=v(RE), in1=ar, op=ALU.mult), reads=[sB, self.cB], writes=[t1B])
            K.op("pool", lambda e, v=v, ai=ai: e.tensor_tensor(out=v(T2), in0=v(IM), in1=ai, op=ALU.mult), reads=[sB, self.cB], writes=[t2B])
            K.op("dve", lambda e, v=v: e.tensor_tensor(out=v(T1), in0=v(T1), in1=v(T2), op=ALU.subtract), reads=[t1B, t2B], writes=[t1B])
            K.op("pool", lambda e, v=v, ai=ai: e.tensor_tensor(out=v(T2), in0=v(RE), in1=ai, op=ALU.mult), reads=[sB, t1B, self.cB], writes=[t2B])
            K.op("dve", lambda e, v=v, ar=ar: e.tensor_tensor(out=v(T3), in0=v(IM), in1=ar, op=ALU.mult), reads=[sB, self.cB], writes=[t3B])
            K.op("pool", lambda e, v=v: e.tensor_tensor(out=v(T2), in0=v(T2), in1=v(T3), op=ALU.add), reads=[t2B, t3B], writes=[t2B])
            K.op("dve", lambda e, v=v, d=d: e.tensor_tensor(out=v(RE, d), in0=v(RE, d), in1=v(T1), op=ALU.add), reads=[sB, t1B, t2B], writes=[sB])
            K.op("pool", lambda e, v=v, d=d: e.tensor_tensor(out=v(IM, d), in0=v(IM, d), in1=v(T2), op=ALU.add), reads=[sB, t2B], writes=[sB])
        for part in range(2):
            K.op("pool", lambda e, part=part: e.memset(sap(self.arena, AR, 0, 128, SBF + part * 2048, [(64, 32), (1, 1)]), 0.0), reads=[sbfB], writes=[sbfB])
            K.op("act", lambda e, part=part: e.activation(out=sap(self.arena, AR, 0, 128, SBF + part * 2048 + 1, [(64, 32), (1, 63)]),
                                                          in_=f32(part * 2048, [(64, 32), (1, 63)]), func=AF.Copy), reads=[sB, sbfB], writes=[sbfB])
        if "3" not in PH:
            return
        ybanks = [(self.ps[4 + b], self.psB[4 + b]) for b in range(4)]
        ybufs = [yb for _, yb in ybanks]

        def consBC(i, slot):
            fc, kind = i // 3, i % 3
            hall = [self.hBuf[fc][b] for b in range(NBLK)]
            if kind == 0:
                import os
                if False:
                    ops = [(t, j) for t in range(32) for j in range(t, 32)]
                    for n_, (t, j) in enumerate(ops):
                        yt, yb = ybanks[j // 8]
                        K.op("pe", lambda e, yt=yt, t=t, j=j, fc=fc: e.matmul(
                            sap(yt, 512, 0, 128, (j % 8) * 64, [(1, 64)]),
                            lhsT=self.rslot(slot, t * 128, [(1, 128)]),
                            rhs=sap(self.hB, FC * L, 0, 128, fc * L + (j - t), [(32, 64)]),
                            start=(t == 0), stop=False), reads=[self.ringB[slot]] + hall, writes=[yb], inc=(n_ == len(ops) - 1))
                else:
                    ops = []
                    for t in range(32):
                        for b in range(4):
                            jlo, jhi = max(8 * b, t), 8 * b + 8
                            if jlo < jhi:
                                ops.append((t, b, jlo, jhi - jlo))
                    for n_, (t, b, jlo, nj) in enumerate(ops):
                        yt, yb = ybanks[b]
                        K.op("pe", lambda e, yt=yt, t=t, b=b, jlo=jlo, nj=nj, fc=fc: e.matmul(
                            sap(yt, 512, 0, 128, (jlo - 8 * b) * 64, [(1, nj * 64)]),
                            lhsT=self.rslot(slot, t * 128, [(1, 128)]),
                            rhs=sap(self.hB, FC * L, 0, 128, fc * L + (jlo - t) * 64, [(1, nj * 64)]),
                            start=(t == 0), stop=False), reads=[self.ringB[slot]] + hall, writes=[yb], inc=(n_ == len(ops) - 1))
            else:
                part = kind - 1
                for j in range(32):
                    yt, yb = ybanks[j // 8]
                    for gpl in range(4):
                        gp = 4 * fc + gpl
                        last = (j == 31 and gpl == 3)
                        K.op("pe", lambda e, yt=yt, j=j, gpl=gpl, gp=gp, part=part, last=last: e.matmul(
                            sap(yt, 512, 32 * gpl, 32, (j % 8) * 64, [(1, 64)]),
                            lhsT=self.rslot(slot, gpl * 1024 + j * 32, [(1, 32)]),
                            rhs=sap(self.arena, AR, 0, 128, SBF + part * 2048 + gp * 64, [(1, 64)]),
                            start=False, stop=(last and part == 1), tile_position=(0, 32 * gpl)),
                            reads=[self.ringB[slot], sbfB], writes=[yb], inc=last)
                if part == 1:
                    for b in range(4):
                        yt, yb = ybanks[b]
                        K.op("dve", lambda e, yt=yt, b=b, fc=fc: e.scalar_tensor_tensor(
                            out=f32(T1 + b * TB, [(1, TB)]),
                            in0=sap(self.hB, FC * L, 0, 128, fc * L + b * TB, [(1, TB)]),
                            scalar=sap(self.dT, 2 * FC, 0, 128, sl * FC + fc, [(1, 1)]),
                            in1=yt[:, :], op0=ALU.mult, op1=ALU.add),
                            reads=[yb, self.cB] + hall, writes=[taB])
                    for b in range(4):
                        K.op("act", lambda e, b=b, fc=fc: e.activation(
                            out=sap(self.hB, FC * L, 0, 128, fc * L + 8 * b, [(1, 8), (32, 64)]),
                            in_=f32(T1 + b * TB, [(64, 8), (1, 64)]), func=AF.Gelu_apprx_tanh), reads=[taB], writes=hall)
        self.stream([("s5%s_%d" % (("C", "B", "B")[i % 3], sl), (i // 3) if i % 3 == 0 else (2 * (i // 3) + (i % 3) - 1)) for i in range(24)], consBC)
        if self._s5glu:
            self.s5_glu(sl)

    def s5_glu(self, sl):
        K = self.K
        A32 = self.AR // 2
        K.fence()
        sgB = [Buf() for _ in range(FC)]
        tB_ = [Buf(), Buf()]
        for blk in range(NBLK):
            def cons(i, slot, blk=blk):
                for q in range(4):
                    fo = (i % 2) * 4 + q
                    pt, pb = self.bank()
                    for kc in range(FC):
                        K.op("pe", lambda e, pt=pt, kc=kc, q=q: e.matmul(
                            pt[:, :], lhsT=self.rslot(slot, kc * 512 + q * 128, [(1, 128)]),
                            rhs=self.hap(kc, blk * TB, TB), start=(kc == 0), stop=(kc == FC - 1)),
                            reads=[self.ringB[slot], self.hBuf[kc][blk]], writes=[pb], inc=(kc == FC - 1))
                    sg = sap(self.arena32, A32, 0, 128, fo * TB, [(1, TB)])
                    if i < 2:
                        K.op("act", lambda e, sg=sg, pt=pt: e.activation(out=sg, in_=pt[:, :], func=AF.Sigmoid), reads=[pb], writes=[sgB[fo]])
                    else:
                        j = fo % 2
                        tt = sap(self.arena32, A32, 0, 128, 4096 + j * TB, [(1, TB)])
                        K.op("dve", lambda e, tt=tt, pt=pt, sg=sg: e.tensor_tensor(out=tt, in0=pt[:, :], in1=sg, op=ALU.mult),
                             reads=[pb, sgB[fo]], writes=[tB_[j]])
                        K.op("pool", lambda e, tt=tt, fo=fo: e.tensor_tensor(out=self.xap(fo, blk), in0=self.xap(fo, blk), in1=tt, op=ALU.add),
                             reads=[tB_[j], self.xBuf[fo][blk]], writes=[self.xBuf[fo][blk]])
            self.stream([("glu_%d" % sl, 2), ("glu_%d" % sl, 3), ("glu_%d" % sl, 0), ("glu_%d" % sl, 1)], cons)

    def layer(self, s, li):
        kind, sl = li % 3, li // 3
        self.norm(0, li, perm=(kind == 0))
        if kind == 0:
            self.s5(sl)
        elif kind == 1:
            self.conv()
        else:
            self.fox()
        self.norm(1, li)
        self.mlp(li)
        self.norm(2, li)
        self.ple(s, li)


_CACHE = {}


def _get_prog():
    if "p" not in _CACHE:
        _CACHE["p"] = Prog()
    return _CACHE["p"]


def kernel(**inputs):
    prog = _get_prog()
    names = ["norm_mix", "norm_ffn", "norm_ple", "norm_final", "ssm_lam_re", "ssm_lam_im", "ssm_log_dt", "ssm_b_re",
             "ssm_b_im", "ssm_c_re", "ssm_c_im", "ssm_d", "ssm_w_glu", "conv_w_in", "conv_w", "conv_w_out", "fox_w_in",
             "fox_b_f", "fox_w_out", "mlp_w1", "mlp_w2", "ple_w", "ple_gate_w"]
    shared = {n: np.ascontiguousarray(np.asarray(inputs[n], dtype=np.float32)) for n in names}
    x = np.asarray(inputs["x"], dtype=np.float32)
    p = np.asarray(inputs["p"], dtype=np.float32)
    in_maps = []
    for c in range(NCORES):
        m = dict(shared)
        m["x"] = np.ascontiguousarray(x[c * SEQ_PER_CORE:(c + 1) * SEQ_PER_CORE])
        m["p"] = np.ascontiguousarray(p[:, c * SEQ_PER_CORE:(c + 1) * SEQ_PER_CORE])
        in_maps.append(m)
    res = run_bass_kernel_spmd(prog.nc, in_maps, core_ids=list(range(NCORES)))
    return np.concatenate([np.asarray(r["out"]) for r in res.results], axis=0).astype(np.float32)
```
